# Optimizing a Trainium2 kernel written in Bass

```python
import math
import jax
import jax.numpy as jnp
from jax import lax
import numpy as np

D_MODEL = 1024
BATCH = 16
SEQ = 256
DEPTH = 4
DEC_BATCH = 2
DEC_SEQ = 4096
PAST_LEN = 256

GRID_W = 64
MIX_W = D_MODEL
GROUP_W = MIX_W // 4
H_A = 4
DV_A = GROUP_W // H_A
DQK_A = DV_A // 2
H_B = 4
DK_B = GROUP_W // H_B
DV_B = GROUP_W // H_B
H_C = 4
NOPE_C = 64
ROPE_C = 32
V_C = GROUP_W // H_C
Q_RANK = 256
KV_RANK = 128
H_D = 4
P_D = GROUP_W // H_D
N_D = 128
G_D = 2
CONV_W = 3
CHUNK = 64
Q_BLOCK = 128
D_FF = 4 * D_MODEL
ROPE_BASE = 10000.0
EPS = 1e-6
ALPHA = (2 * DEPTH) ** 0.25
BETA_INIT = (8 * DEPTH) ** -0.25
SPLIT_SIZES = (
    H_A * 2 * DQK_A, H_A * 2 * DQK_A, H_A * DV_A,
    H_B * (2 * DK_B + DV_B), 2 * H_B, 2 * H_B, H_B * DV_B,
    Q_RANK, KV_RANK, ROPE_C,
    H_D * P_D, H_D * P_D + 2 * G_D * N_D, 2 * H_D,
)
IN_COLS = sum(SPLIT_SIZES)

kernel_name = 'hybrid_diffusion_parallel_heads_step'


def rms_norm(x, g):
    xf = x.astype(jnp.float32)
    y = xf * lax.rsqrt(jnp.mean(xf * xf, axis=-1, keepdims=True) + EPS)
    return y * g.astype(jnp.float32)


def layer_norm(x, g, b):
    xf = x.astype(jnp.float32)
    mu = jnp.mean(xf, axis=-1, keepdims=True)
    var = jnp.mean(jnp.square(xf - mu), axis=-1, keepdims=True)
    y = (xf - mu) * lax.rsqrt(var + EPS) * g.astype(jnp.float32) + b.astype(jnp.float32)
    return y.astype(x.dtype)


def l2_normalize(x):
    xf = x.astype(jnp.float32)
    return xf * lax.rsqrt(jnp.sum(xf * xf, axis=-1, keepdims=True) + EPS)


def depthwise_conv(x, w, b=None):
    pad = CONV_W // 2
    y = lax.conv_general_dilated(x, w[:, None, :].astype(x.dtype), window_strides=(1,),
                                 padding=[(pad, pad)], dimension_numbers=('NWC', 'WIO', 'NWC'),
                                 feature_group_count=x.shape[-1])
    return y if b is None else y + b.astype(x.dtype)


def rope2d_tables(length, dim):
    rows = length // GRID_W
    row_pos = jnp.repeat(jnp.arange(rows, dtype=jnp.float32), GRID_W)
    col_pos = jnp.tile(jnp.arange(GRID_W, dtype=jnp.float32), rows)
    half = dim // 2
    inv_freq = ROPE_BASE ** (-jnp.arange(0, half, 2, dtype=jnp.float32) / half)
    ang_r = row_pos[:, None] * inv_freq
    ang_c = col_pos[:, None] * inv_freq
    ang = jnp.concatenate([ang_r, ang_r, ang_c, ang_c], axis=-1)
    return jnp.cos(ang), jnp.sin(ang)


def apply_rope2d(x, cos, sin):
    qd = x.shape[-1] // 4
    x1, x2, x3, x4 = (x[..., i * qd:(i + 1) * qd] for i in range(4))
    rot = jnp.concatenate([-x2, x1, -x4, x3], axis=-1)
    shape = (1, x.shape[1]) + (1,) * (x.ndim - 3) + (x.shape[-1],)
    return x * cos.reshape(shape).astype(x.dtype) + rot * sin.reshape(shape).astype(x.dtype)


def over_query_blocks(fn, q):
    b, lq = q.shape[:2]
    nb = lq // Q_BLOCK
    blocks = jnp.moveaxis(q.reshape(b, nb, Q_BLOCK, *q.shape[2:]), 1, 0)
    out = jnp.moveaxis(lax.map(fn, blocks), 0, 1)
    return out.reshape(b, lq, *out.shape[3:])


def diff_attention(q, k, v, lam):
    scale = DQK_A ** -0.5

    def block(qb):
        s = jnp.einsum('bqhtd,bkhtd->bhtqk', qb, k).astype(jnp.float32) * scale
        p = jax.nn.softmax(s, axis=-1)
        w = p[:, :, 0] - lam * p[:, :, 1]
        return jnp.einsum('bhqk,bkhv->bqhv', w.astype(v.dtype), v)

    return over_query_blocks(block, q)


def softmax_attention(q, k, v):
    scale = q.shape[-1] ** -0.5

    def block(qb):
        s = jnp.einsum('bqhd,bkhd->bhqk', qb, k).astype(jnp.float32) * scale
        p = jax.nn.softmax(s, axis=-1)
        return jnp.einsum('bhqk,bkhv->bqhv', p.astype(v.dtype), v)

    return over_query_blocks(block, q)


def to_chunks(t):
    b, l, h = t.shape[:3]
    t = t.reshape(b, l // CHUNK, CHUNK, h, *t.shape[3:])
    return jnp.moveaxis(t, 3, 2)


def from_chunks(t):
    t = jnp.moveaxis(t, 2, 3)
    return t.reshape(t.shape[0], t.shape[1] * t.shape[2], t.shape[3], t.shape[4])


def gated_delta_chunked(q, k, v, beta, g, s0):
    qc, kc, vc, bc = to_chunks(q), to_chunks(k), to_chunks(v), to_chunks(beta)
    gc = jnp.cumsum(to_chunks(g), axis=-1)
    idx = jnp.arange(CHUNK)
    causal = idx[:, None] >= idx[None, :]
    strict = idx[:, None] > idx[None, :]
    decay = jnp.exp(jnp.where(causal, gc[..., :, None] - gc[..., None, :], -jnp.inf))
    kb = kc * bc[..., None]
    a_mat = jnp.where(strict, jnp.einsum('bchid,bchjd->bchij', kb, kc) * decay, 0.0)
    eye = jnp.eye(CHUNK, dtype=jnp.float32)
    t_mat = lax.linalg.triangular_solve(eye + a_mat, jnp.broadcast_to(eye, a_mat.shape),
                                        left_side=True, lower=True, unit_diagonal=True)
    u = t_mat @ (vc * bc[..., None])
    w = t_mat @ (kb * jnp.exp(gc)[..., None])
    qk = jnp.where(causal, jnp.einsum('bchid,bchjd->bchij', qc, kc) * decay, 0.0)
    q_dec = qc * jnp.exp(gc)[..., None]
    k_dec = kc * jnp.exp(gc[..., -1:] - gc)[..., None]
    g_last = gc[..., -1]

    def step(s, inp):
        q_i, k_i, u_i, w_i, qk_i, gl_i = inp
        v_new = u_i - w_i @ s
        o = q_i @ s + qk_i @ v_new
        s = s * jnp.exp(gl_i)[..., None, None] + jnp.einsum('bhck,bhcv->bhkv', k_i, v_new)
        return s, o

    xs = tuple(jnp.moveaxis(t, 1, 0) for t in (q_dec, k_dec, u, w, qk, g_last))
    s_fin, o = lax.scan(step, s0, xs)
    return from_chunks(jnp.moveaxis(o, 0, 1)), s_fin


def ssd_chunked(x, dt, a, bm, cm, s0):
    rep = x.shape[2] // bm.shape[2]
    bc = to_chunks(jnp.repeat(bm, rep, axis=2))
    cc = to_chunks(jnp.repeat(cm, rep, axis=2))
    xdt = to_chunks(x * dt[..., None])
    ac = jnp.cumsum(to_chunks(dt * a), axis=-1)
    idx = jnp.arange(CHUNK)
    causal = idx[:, None] >= idx[None, :]
    seg = jnp.exp(jnp.where(causal, ac[..., :, None] - ac[..., None, :], -jnp.inf))
    y_intra = jnp.einsum('bchij,bchjp->bchip', jnp.einsum('bchis,bchjs->bchij', cc, bc) * seg, xdt)
    last = ac[..., -1]
    states = jnp.einsum('bchjs,bchjp->bchps', bc * jnp.exp(last[..., None] - ac)[..., None], xdt)
    c_dec = cc * jnp.exp(ac)[..., None]

    def step(s, inp):
        c_i, st_i, last_i = inp
        y = jnp.einsum('bhis,bhps->bhip', c_i, s)
        s = s * jnp.exp(last_i)[..., None, None] + st_i
        return s, y

    xs = tuple(jnp.moveaxis(t, 1, 0) for t in (c_dec, states, last))
    s_fin, y_inter = lax.scan(step, s0, xs)
    return from_chunks(y_intra + jnp.moveaxis(y_inter, 0, 1)), s_fin


def mixer_diff(q, k, v, lam_p, norm_g, layer, rope, past_k, past_v):
    b, l = q.shape[:2]
    q = q.reshape(b, l, H_A, 2, DQK_A)
    k = k.reshape(b, l, H_A, 2, DQK_A)
    v = v.reshape(b, l, H_A, DV_A)
    own_k, own_v = k, v
    if rope is not None:
        q = apply_rope2d(q, *rope)
        k = apply_rope2d(k, *rope)
    if past_k is not None:
        k = jnp.concatenate([k, past_k.astype(k.dtype)], axis=1)
        v = jnp.concatenate([v, past_v.astype(v.dtype)], axis=1)
    lam_init = 0.8 - 0.6 * math.exp(-0.3 * layer)
    lp = lam_p.astype(jnp.float32)
    lam = jnp.exp(jnp.sum(lp[0] * lp[1])) - jnp.exp(jnp.sum(lp[2] * lp[3])) + lam_init
    o = diff_attention(q, k, v, lam)
    o = rms_norm(o, norm_g) * (1.0 - lam_init)
    return o.reshape(b, l, GROUP_W).astype(q.dtype), own_k, own_v


def mixer_deltanet(qkv, beta_raw, decay_raw, gate, conv_w, a_log, dt_bias, norm_g, past_state):
    b, l = qkv.shape[:2]
    qkv = jax.nn.silu(depthwise_conv(qkv, conv_w))
    q, k, v = jnp.split(qkv, [H_B * DK_B, 2 * H_B * DK_B], axis=-1)
    q = l2_normalize(q.reshape(b, l, H_B, DK_B)) * DK_B ** -0.5
    k = l2_normalize(k.reshape(b, l, H_B, DK_B))
    v = v.reshape(b, l, H_B, DV_B).astype(jnp.float32)
    beta = jax.nn.sigmoid(beta_raw.astype(jnp.float32)).reshape(b, l, 2, H_B)
    g = -jnp.exp(a_log.astype(jnp.float32)) * jax.nn.softplus(
        decay_raw.astype(jnp.float32).reshape(b, l, 2, H_B) + dt_bias.astype(jnp.float32))
    if past_state is None:
        s0 = jnp.zeros((b, 2, H_B, DK_B, DV_B), jnp.float32)
    else:
        s0 = past_state.astype(jnp.float32)
    fl = lambda t: jnp.flip(t, axis=1)
    o_f, s_f = gated_delta_chunked(q, k, v, beta[:, :, 0], g[:, :, 0], s0[:, 0])
    o_b, s_b = gated_delta_chunked(fl(q), fl(k), fl(v), fl(beta[:, :, 1]), fl(g[:, :, 1]), s0[:, 1])
    o = rms_norm(o_f + fl(o_b), norm_g) * jax.nn.silu(gate.reshape(b, l, H_B, DV_B).astype(jnp.float32))
    return o.reshape(b, l, GROUP_W).astype(qkv.dtype), jnp.stack([s_f, s_b], axis=1)


def mixer_mla(c_q, c_kv, k_rope, q_norm, kv_norm, w_uq, w_uk, w_uv, rope, past_ckv, past_kr):
    b, l = c_q.shape[:2]
    q = (rms_norm(c_q, q_norm).astype(c_q.dtype) @ w_uq).reshape(b, l, H_C, NOPE_C + ROPE_C)
    q_nope, q_rope = q[..., :NOPE_C], q[..., NOPE_C:]
    ckv = rms_norm(c_kv, kv_norm).astype(c_kv.dtype)
    own_ckv, own_kr = ckv, k_rope
    if rope is not None:
        q_rope = apply_rope2d(q_rope, *rope)
        k_rope = apply_rope2d(k_rope, *rope)
    if past_ckv is not None:
        ckv = jnp.concatenate([ckv, past_ckv.astype(ckv.dtype)], axis=1)
        k_rope = jnp.concatenate([k_rope, past_kr.astype(k_rope.dtype)], axis=1)
    lk = ckv.shape[1]
    k_nope = (ckv @ w_uk).reshape(b, lk, H_C, NOPE_C)
    v = (ckv @ w_uv).reshape(b, lk, H_C, V_C)
    k = jnp.concatenate([k_nope, jnp.broadcast_to(k_rope[:, :, None, :], (b, lk, H_C, ROPE_C))], axis=-1)
    q = jnp.concatenate([q_nope, q_rope], axis=-1)
    o = softmax_attention(q, k, v)
    return o.reshape(b, l, GROUP_W), own_ckv, own_kr


def mixer_ssd(z, xbc, dt_raw, conv_w, conv_b, a_log, dt_bias, d_skip, norm_g, past_state):
    b, l = z.shape[:2]
    xbc = jax.nn.silu(depthwise_conv(xbc, conv_w, conv_b))
    x, bm, cm = jnp.split(xbc, [H_D * P_D, H_D * P_D + G_D * N_D], axis=-1)
    x = x.reshape(b, l, H_D, P_D).astype(jnp.float32)
    bm = bm.reshape(b, l, G_D, N_D).astype(jnp.float32)
    cm = cm.reshape(b, l, G_D, N_D).astype(jnp.float32)
    dt = jax.nn.softplus(dt_raw.astype(jnp.float32).reshape(b, l, 2, H_D) + dt_bias.astype(jnp.float32))
    a = -jnp.exp(a_log.astype(jnp.float32))
    if past_state is None:
        s0 = jnp.zeros((b, 2, H_D, P_D, N_D), jnp.float32)
    else:
        s0 = past_state.astype(jnp.float32)
    fl = lambda t: jnp.flip(t, axis=1)
    y_f, s_f = ssd_chunked(x, dt[:, :, 0], a[0], bm, cm, s0[:, 0])
    y_b, s_b = ssd_chunked(fl(x), fl(dt[:, :, 1]), a[1], fl(bm), fl(cm), s0[:, 1])
    y = y_f + fl(y_b) + d_skip.astype(jnp.float32)[:, None] * x
    y = y * jax.nn.silu(z.reshape(b, l, H_D, P_D).astype(jnp.float32))
    y = rms_norm(y.reshape(b, l, G_D, GROUP_W // G_D), norm_g.reshape(G_D, GROUP_W // G_D))
    return y.reshape(b, l, GROUP_W).astype(z.dtype), jnp.stack([s_f, s_b], axis=1)


def trunk_layer(x, mod, lw, layer, rope, past):
    shift1, scale1, gate1, shift2, scale2, gate2 = jnp.split(mod[:, None, :].astype(x.dtype), 6, axis=-1)
    h = x * (1 + scale1) + shift1
    parts = jnp.split(h @ lw['w_in'], np.cumsum(SPLIT_SIZES)[:-1].tolist(), axis=-1)
    (a_q, a_k, a_v, b_qkv, b_beta, b_decay, b_gate, c_q, c_kv, c_kr, d_z, d_xbc, d_dt) = parts
    rope_a, rope_c = (None, None) if rope is None else rope
    p = (None,) * 6 if past is None else past
    o_a, ctx_k, ctx_v = mixer_diff(a_q, a_k, a_v, lw['diff_lam'], lw['diff_norm'], layer, rope_a, p[0], p[1])
    o_b, st_b = mixer_deltanet(b_qkv, b_beta, b_decay, b_gate, lw['dn_conv'], lw['dn_a_log'],
                               lw['dn_dt_bias'], lw['dn_norm'], p[2])
    o_c, ctx_ckv, ctx_kr = mixer_mla(c_q, c_kv, c_kr, lw['mla_q_norm'], lw['mla_kv_norm'], lw['mla_w_uq'],
                                     lw['mla_w_uk'], lw['mla_w_uv'], rope_c, p[3], p[4])
    o_d, st_d = mixer_ssd(d_z, d_xbc, d_dt, lw['ssm_conv_w'], lw['ssm_conv_b'], lw['ssm_a_log'],
                          lw['ssm_dt_bias'], lw['ssm_d'], lw['ssm_norm'], p[5])
    mixed = jnp.concatenate([o_a, o_b, o_c, o_d], axis=-1).astype(x.dtype) @ lw['w_out']
    x = layer_norm(ALPHA * x + gate1 * mixed, lw['ln1_g'], lw['ln1_b'])
    h = x * (1 + scale2) + shift2
    ff = jnp.square(jax.nn.relu(h @ lw['w_ff1'])) @ lw['w_ff2']
    x = layer_norm(ALPHA * x + gate2 * ff, lw['ln2_g'], lw['ln2_b'])
    return x, (ctx_k, ctx_v, st_b, ctx_ckv, ctx_kr, st_d)


def _dt_bias(key, shape):
    dt = jnp.exp(jax.random.uniform(key, shape, jnp.float32, math.log(1e-3), math.log(1e-1)))
    return dt + jnp.log(-jnp.expm1(-dt))


def setup_inputs(seed: int = 0) -> dict:
    key = jax.random.key(seed)
    ks = iter(jax.random.split(key, 40))
    f32 = jnp.float32

    def nrm(shape, s):
        return jax.random.normal(next(ks), shape, f32) * s

    return {
        'x_prompt': nrm((BATCH, SEQ, D_MODEL), 1.0),
        'x_sample': nrm((DEC_BATCH, DEC_SEQ, D_MODEL), 1.0),
        'cache_diff_k': nrm((DEC_BATCH, DEPTH, PAST_LEN, H_A, 2, DQK_A), 1.0),
        'cache_diff_v': nrm((DEC_BATCH, DEPTH, PAST_LEN, H_A, DV_A), 1.0),
        'state_delta': nrm((DEC_BATCH, DEPTH, 2, H_B, DK_B, DV_B), 0.1),
        'cache_mla_ckv': nrm((DEC_BATCH, DEPTH, PAST_LEN, KV_RANK), 1.0),
        'cache_mla_krope': nrm((DEC_BATCH, DEPTH, PAST_LEN, ROPE_C), 1.0),
        'state_ssm': nrm((DEC_BATCH, DEPTH, 2, H_D, P_D, N_D), 0.1),
        'c': nrm((DEC_BATCH, D_MODEL), 1.0),
        'c_ctx': nrm((D_MODEL,), 1.0),
        'w_mod': nrm((DEPTH, D_MODEL, 6 * D_MODEL), D_MODEL ** -0.5),
        'b_mod': nrm((DEPTH, 6 * D_MODEL), 0.02),
        'w_in': nrm((DEPTH, D_MODEL, IN_COLS), D_MODEL ** -0.5),
        'diff_lam': nrm((DEPTH, 4, DQK_A), 0.1),
        'diff_norm': 1.0 + nrm((DEPTH, DV_A), 0.02),
        'dn_conv': nrm((DEPTH, CONV_W, H_B * (2 * DK_B + DV_B)), CONV_W ** -0.5),
        'dn_a_log': jnp.log(jax.random.uniform(next(ks), (DEPTH, 2, H_B), f32, 1.0, 16.0)),
        'dn_dt_bias': _dt_bias(next(ks), (DEPTH, 2, H_B)),
        'dn_norm': 1.0 + nrm((DEPTH, DV_B), 0.02),
        'mla_q_norm': 1.0 + nrm((DEPTH, Q_RANK), 0.02),
        'mla_kv_norm': 1.0 + nrm((DEPTH, KV_RANK), 0.02),
        'mla_w_uq': nrm((DEPTH, Q_RANK, H_C * (NOPE_C + ROPE_C)), Q_RANK ** -0.5),
        'mla_w_uk': nrm((DEPTH, KV_RANK, H_C * NOPE_C), KV_RANK ** -0.5),
        'mla_w_uv': nrm((DEPTH, KV_RANK, H_C * V_C), KV_RANK ** -0.5),
        'ssm_conv_w': nrm((DEPTH, CONV_W, H_D * P_D + 2 * G_D * N_D), CONV_W ** -0.5),
        'ssm_conv_b': nrm((DEPTH, H_D * P_D + 2 * G_D * N_D), 0.02),
        'ssm_a_log': jnp.log(jax.random.uniform(next(ks), (DEPTH, 2, H_D), f32, 1.0, 16.0)),
        'ssm_dt_bias': _dt_bias(next(ks), (DEPTH, 2, H_D)),
        'ssm_d': 1.0 + nrm((DEPTH, H_D), 0.1),
        'ssm_norm': 1.0 + nrm((DEPTH, GROUP_W), 0.02),
        'w_out': nrm((DEPTH, MIX_W, D_MODEL), MIX_W ** -0.5 * BETA_INIT),
        'ln1_g': 1.0 + nrm((DEPTH, D_MODEL), 0.02),
        'ln1_b': nrm((DEPTH, D_MODEL), 0.02),
        'ln2_g': 1.0 + nrm((DEPTH, D_MODEL), 0.02),
        'ln2_b': nrm((DEPTH, D_MODEL), 0.02),
        'w_ff1': nrm((DEPTH, D_MODEL, D_FF), D_MODEL ** -0.5),
        'w_ff2': nrm((DEPTH, D_FF, D_MODEL), D_FF ** -0.5 * BETA_INIT),
    }


def reference(x_prompt, x_sample, cache_diff_k, cache_diff_v, state_delta, cache_mla_ckv, cache_mla_krope,
              state_ssm, c, c_ctx, w_mod, b_mod, w_in, diff_lam, diff_norm, dn_conv, dn_a_log, dn_dt_bias,
              dn_norm, mla_q_norm, mla_kv_norm, mla_w_uq, mla_w_uk, mla_w_uv, ssm_conv_w, ssm_conv_b,
              ssm_a_log, ssm_dt_bias, ssm_d, ssm_norm, w_out, ln1_g, ln1_b, ln2_g, ln2_b, w_ff1, w_ff2):
    lat_len = x_sample.shape[1]
    rope = (rope2d_tables(lat_len, DQK_A), rope2d_tables(lat_len, ROPE_C))
    y_prompt, y_sample = x_prompt, x_sample
    ctx_out = ([], [], [], [], [], [])
    for l in range(DEPTH):
        lw = {
            'w_in': w_in[l], 'diff_lam': diff_lam[l], 'diff_norm': diff_norm[l],
            'dn_conv': dn_conv[l], 'dn_a_log': dn_a_log[l], 'dn_dt_bias': dn_dt_bias[l], 'dn_norm': dn_norm[l],
            'mla_q_norm': mla_q_norm[l], 'mla_kv_norm': mla_kv_norm[l], 'mla_w_uq': mla_w_uq[l],
            'mla_w_uk': mla_w_uk[l], 'mla_w_uv': mla_w_uv[l],
            'ssm_conv_w': ssm_conv_w[l], 'ssm_conv_b': ssm_conv_b[l], 'ssm_a_log': ssm_a_log[l],
            'ssm_dt_bias': ssm_dt_bias[l], 'ssm_d': ssm_d[l], 'ssm_norm': ssm_norm[l],
            'w_out': w_out[l], 'ln1_g': ln1_g[l], 'ln1_b': ln1_b[l], 'ln2_g': ln2_g[l], 'ln2_b': ln2_b[l],
            'w_ff1': w_ff1[l], 'w_ff2': w_ff2[l],
        }
        mod_ctx = jax.nn.silu(c_ctx)[None, :] @ w_mod[l] + b_mod[l]
        mod_lat = jax.nn.silu(c) @ w_mod[l] + b_mod[l]
        y_prompt, ctx_t = trunk_layer(y_prompt, mod_ctx, lw, l, None, None)
        past = (cache_diff_k[:, l], cache_diff_v[:, l], state_delta[:, l],
                cache_mla_ckv[:, l], cache_mla_krope[:, l], state_ssm[:, l])
        y_sample, _ = trunk_layer(y_sample, mod_lat, lw, l, rope, past)
        for store, t in zip(ctx_out, ctx_t):
            store.append(t)
    new_diff_k = jnp.stack(ctx_out[0], axis=1)
    new_diff_v = jnp.stack(ctx_out[1], axis=1)
    new_state_delta = jnp.stack(ctx_out[2], axis=1)
    new_mla_ckv = jnp.stack(ctx_out[3], axis=1)
    new_mla_krope = jnp.stack(ctx_out[4], axis=1)
    new_state_ssm = jnp.stack(ctx_out[5], axis=1)
    return (y_prompt, y_sample, new_diff_k, new_diff_v, new_state_delta, new_mla_ckv, new_mla_krope, new_state_ssm)
```

```python
import math
from contextlib import ExitStack
import numpy as np
import ml_dtypes
import concourse.bass as bass
import concourse.mybir as mybir
from concourse.bass_utils import run_bass_kernel_spmd

F32 = mybir.dt.float32
BF16 = mybir.dt.bfloat16
AF = mybir.ActivationFunctionType
ALU = mybir.AluOpType
AX = mybir.AxisListType

ENGS = ("pe", "act", "dve", "pool", "sp")
NDMASEM = 8
EPS = 1e-6
PE_FILL = True
D = 1024
DFF = 4096
BIG = 1.0e4


class Sched:
    def __init__(self, nc, es):
        self.nc = nc
        self.ops = []
        self.last_w = {}
        self.readers = {}
        self.cnt_sem = {e: es.enter_context(nc.semaphore("c_" + e)) for e in ENGS}
        self.dma_sems = {e: [es.enter_context(nc.semaphore("d_%s%d" % (e, i))) for i in range(NDMASEM)]
                         for e in ("sp", "pool")}
        self.dma_n = {e: 0 for e in self.dma_sems}
        self.dma_semcnt = {e: [0] * NDMASEM for e in self.dma_sems}
        self.dma_last = {e: [None] * NDMASEM for e in self.dma_sems}
        self.sig_cnt = {e: 0 for e in ENGS}
        self.last_op = {e: None for e in ENGS}
        self.out_dma_events = []

    def add(self, eng, emit, reads=(), writes=(), dma=False, is_out=False, extra_deps=()):
        idx = len(self.ops)
        deps = set(extra_deps)
        for r in reads:
            if r in self.last_w:
                deps.add(self.last_w[r])
        for w in writes:
            if w in self.last_w:
                deps.add(self.last_w[w])
            for rd in self.readers.get(w, ()):
                deps.add(rd)
        op = dict(eng=eng, emit=emit, deps=deps, dma=dma, pre=None)
        if dma:
            n = self.dma_n[eng]
            slot = n % NDMASEM
            self.dma_n[eng] = n + 1
            prev = self.dma_semcnt[eng][slot]
            self.dma_semcnt[eng][slot] = prev + 16
            op["event"] = (self.dma_sems[eng][slot], prev + 16)
            op["pre"] = (self.dma_sems[eng][slot], prev) if prev > 0 else None
            self.dma_last[eng][slot] = idx
            if is_out:
                self.out_dma_events.append(op["event"])
        else:
            self.sig_cnt[eng] += 1
            op["event"] = (self.cnt_sem[eng], self.sig_cnt[eng])
            self.last_op[eng] = idx
        self.ops.append(op)
        for w in writes:
            self.last_w[w] = idx
            self.readers[w] = []
        for r in reads:
            if r not in writes:
                self.readers.setdefault(r, []).append(idx)
        return idx

    def barrier(self):
        deps = [v for v in self.last_op.values() if v is not None]
        for e in self.dma_last:
            deps += [v for v in self.dma_last[e] if v is not None]
        for e in ENGS:
            self.add(e, lambda eng: eng.nop(), extra_deps=deps)
        self.last_w = {}
        self.readers = {}

    def emit_all(self):
        nc = self.nc
        per_eng = {e: [] for e in ENGS}
        waited = {e: {} for e in ENGS}
        for op in self.ops:
            e = op["eng"]
            need = {}
            if op["pre"] is not None:
                need[op["pre"][0]] = op["pre"][1]
            for d in op["deps"]:
                p = self.ops[d]
                if p["eng"] == "pe" and e == "pe" and not p["dma"]:
                    continue
                s, v = p["event"]
                if need.get(s, 0) < v:
                    need[s] = v
            ws = []
            for s, v in need.items():
                if waited[e].get(s, 0) >= v:
                    continue
                waited[e][s] = v
                ws.append((s, v))
            op["waits"] = ws
            per_eng[e].append(op)
        finals = {}
        for s, v in self.out_dma_events:
            finals[s] = max(finals.get(s, 0), v)

        with nc.Block() as block:
            def run(engname):
                def body(eng):
                    for op in per_eng[engname]:
                        for s, v in op["waits"]:
                            eng.wait_ge(s, v)
                        ins = op["emit"](eng)
                        s, v = op["event"]
                        ins.then_inc(s, 16 if op["dma"] else 1)
                    if engname == "sp":
                        for s, v in finals.items():
                            eng.wait_ge(s, v)
                return body
            block.tensor(run("pe"))
            block.scalar(run("act"))
            block.vector(run("dve"))
            block.gpsimd(run("pool"))
            block.sync(run("sp"))


SUBKEY = {}


def _keys(ap):
    if isinstance(ap, (str, tuple)):
        return (ap,)
    nm = ap.name
    sk = SUBKEY.get(nm)
    if sk is None:
        return (nm,)
    row, slot = sk
    first = ap.offset % row
    ext = 1
    for (st, cnt) in ap.ap[1:]:
        ext += (cnt - 1) * st
    last = min(first + ext - 1, row - 1)
    return tuple((nm, i) for i in range(first // slot, last // slot + 1))


def _key(ap):
    return _keys(ap)[0]


class K:
    def __init__(self, nc):
        self.nc = nc
        self.es = ExitStack()
        self.s = Sched(nc, self.es)
        self.ntile = 0
        self.stack = [self.es]
        self.rr = 0
        self.dram_names = set()
        self.psum_names = set()
        self.ndw = 0

    def sb(self, shape, dt=F32, name=None):
        self.ntile += 1
        return self.stack[-1].enter_context(self.nc.sbuf_tensor("%s_%d" % (name or "t", self.ntile), list(shape), dt))

    def ps(self, shape=(128, 512), dt=F32, name=None):
        self.ntile += 1
        nm = "%s_%d" % (name or "p", self.ntile)
        self.psum_names.add(nm)
        return self.stack[-1].enter_context(self.nc.psum_tensor(nm, list(shape), dt))

    def fix(self, eng, aps):
        if eng == "pool" and any((not isinstance(a, (float, int))) and a.name in self.psum_names for a in aps):
            return "dve"
        return eng

    def stage_begin(self):
        self.stack.append(ExitStack())

    def subkey(self, tile, row_stride, slot):
        SUBKEY[tile.name] = (row_stride, slot)
        return tile

    def sbk(self, shape, dt=F32, name=None):
        t = self.sb(shape, dt, name)
        row = 1
        for d_ in shape[1:]:
            row *= d_
        return self.subkey(t, row, row // shape[1])

    def stage_end(self):
        self.s.barrier()
        self.stack.pop().close()

    def op(self, eng, fn, r=(), w=()):
        w = list(w) + [x for x in r if (not isinstance(x, (str, float, int))) and x.name in self.psum_names]
        rk = tuple(kk for x in r for kk in _keys(x))
        wk = tuple(kk for x in w for kk in _keys(x))
        return self.s.add(eng, fn, reads=rk, writes=wk)

    def dma(self, out, in_, q="sp", is_out=False, rk=None, wk=None, **kw):
        r = (rk,) if rk is not None else _keys(in_)
        w = (wk,) if wk is not None else _keys(out)
        if w[0] in self.dram_names:
            self.ndw += 1
            w = ("__dw%d" % self.ndw,)
        return self.s.add(q, lambda e: e.dma_start(out=out, in_=in_, **kw), reads=r, writes=w, dma=True, is_out=is_out)

    def mm(self, out, lhsT, rhs, start=True, stop=True):
        w = (out,)
        r = (lhsT, rhs) if start else (lhsT, rhs, out)
        return self.op("pe", lambda e: e.matmul(out, lhsT=lhsT, rhs=rhs, start=start, stop=stop), r, w)

    def act(self, out, in_, func, bias=None, scale=None, accum_out=None):
        kw = {}
        r = [in_]
        w = [out]
        if bias is not None:
            kw["bias"] = bias
            if not isinstance(bias, float):
                r.append(bias)
        if scale is not None:
            kw["scale"] = scale
            if not isinstance(scale, float):
                r.append(scale)
        if accum_out is not None:
            kw["accum_out"] = accum_out
            w.append(accum_out)
        return self.op("act", lambda e: e.activation(out=out, in_=in_, func=func, **kw), r, w)

    def tt(self, out, in0, in1, op, eng="dve"):
        eng = self.fix(eng, (out, in0, in1))
        return self.op(eng, lambda e: e.tensor_tensor(out=out, in0=in0, in1=in1, op=op), (in0, in1), (out,))

    def ts(self, out, in0, s1, s2=None, op0=ALU.mult, op1=None, eng="dve"):
        eng = self.fix(eng, (out, in0))
        r = [in0] + [s for s in (s1, s2) if s is not None and not isinstance(s, (float, int))]
        if op1 is None:
            return self.op(eng, lambda e: e.tensor_scalar(out=out, in0=in0, scalar1=s1, scalar2=None, op0=op0), r, (out,))
        return self.op(eng, lambda e: e.tensor_scalar(out=out, in0=in0, scalar1=s1, scalar2=s2, op0=op0, op1=op1), r, (out,))

    def stt(self, out, in0, scalar, in1, op0, op1, eng="dve"):
        eng = "dve"
        r = [in0, in1] + ([] if isinstance(scalar, (float, int)) else [scalar])
        return self.op(eng, lambda e: e.scalar_tensor_tensor(out=out, in0=in0, scalar=scalar, in1=in1, op0=op0, op1=op1), r, (out,))

    def copy(self, out, in_, eng="dve"):
        if eng == "act":
            return self.act(out, in_, AF.Copy)
        eng = self.fix(eng, (out, in_))
        return self.op(eng, lambda e: e.tensor_copy(out=out, in_=in_), (in_,), (out,))

    def anycopy(self, out, in_):
        self.rr += 1
        return self.copy(out, in_, eng=("dve", "pool")[self.rr % 2])

    def memset(self, ap, val, eng="pool"):
        return self.op(eng, lambda e: e.memset(ap, val), (), (ap,))

    def recip(self, out, in_):
        return self.op("dve", lambda e: e.reciprocal(out=out, in_=in_), (in_,), (out,))

    def rsqrt(self, out, in_, scale, eps):
        self.act(out, in_, AF.Ln, bias=eps, scale=scale)
        self.act(out, out, AF.Exp, scale=-0.5)

    def arecip(self, out, in_):
        self.act(out, in_, AF.Ln)
        self.act(out, out, AF.Exp, scale=-1.0)

    def finish(self):
        self.s.emit_all()
        self.es.close()


O_AQ, O_AK, O_AV, O_BQKV, O_BBETA, O_BDEC, O_BGATE, O_CQ, O_CKV, O_CKR, O_DZ, O_DXBC, O_DDT = (
    0, 256, 512, 768, 1536, 1544, 1552, 1808, 2064, 2192, 2224, 2480, 3248)
FM_TILES = [("aq0", O_AQ, 128), ("aq1", O_AQ + 128, 128), ("ak0", O_AK, 128), ("ak1", O_AK + 128, 128),
            ("bq0", O_BQKV, 128), ("bq1", O_BQKV + 128, 128), ("bk0", O_BQKV + 256, 128), ("bk1", O_BQKV + 384, 128),
            ("bv0", O_BQKV + 512, 128), ("bv1", O_BQKV + 640, 128), ("bg0", O_BGATE, 128), ("bg1", O_BGATE + 128, 128),
            ("cq0", O_CQ, 128), ("cq1", O_CQ + 128, 128), ("ckv", O_CKV, 128),
            ("dx0", O_DXBC, 128), ("dx1", O_DXBC + 128, 128), ("dB0", O_DXBC + 256, 128), ("dB1", O_DXBC + 384, 128),
            ("dC0", O_DXBC + 512, 128), ("dC1", O_DXBC + 640, 128)]
NFM = len(FM_TILES)
C_KR = NFM * 128
C_TM1 = C_KR + 96
C_TM2 = C_TM1 + 280
WIN_COLS = C_TM2 + 256


def perm_w_in(w):
    out = np.zeros((w.shape[0], WIN_COLS), np.float32)
    for i, (_, o, n) in enumerate(FM_TILES):
        out[:, i * 128:i * 128 + n] = w[:, o:o + n]
    out[:, C_KR + 64:C_KR + 96] = w[:, O_CKR:O_CKR + 32]
    out[:, C_TM1:C_TM1 + 256] = w[:, O_AV:O_AV + 256]
    out[:, C_TM1 + 256:C_TM1 + 264] = w[:, O_BBETA:O_BBETA + 8]
    out[:, C_TM1 + 264:C_TM1 + 272] = w[:, O_BDEC:O_BDEC + 8]
    out[:, C_TM1 + 272:C_TM1 + 280] = w[:, O_DDT:O_DDT + 8]
    out[:, C_TM2:C_TM2 + 256] = w[:, O_DZ:O_DZ + 256]
    return out


PV_BMOD, PV_LN1G, PV_LN1B, PV_LN2G, PV_LN2B, PV_DIFFN, PV_DNCONV, PV_DNN, PV_QN, PV_KVN, PV_SCW, PV_SCB = (
    0, 48, 56, 64, 72, 80, 81, 99, 100, 102, 103, 121)
PV_L = 127
PR_LAM, PR_DNA, PR_DNB, PR_SA, PR_SB, PR_SD, PR_SN = 0, 128, 136, 144, 152, 160, 164
PR_L = 164 + 256
(CI, CONESBD, CONES, CR128, CR96, CE2, CTRI0, CTRI1, CMB0, CMB1, CNM0, CNM1, CST0, CST1) = range(14)
CMS = 14
CMS2 = 20
CI2 = 32
NCST = 34
NCST_BASE = 20


def make_consts():
    c = np.zeros((NCST, 128, 128), np.float32)
    c[CI] = np.eye(128)
    bd = np.zeros((128, 128), np.float32)
    bd[:64, :64] = 1
    bd[64:, 64:] = 1
    c[CONESBD] = bd
    c[CONES] = 1
    r32 = np.zeros((32, 32), np.float32)
    for dp in range(32):
        q = dp // 8
        if q == 0:
            r32[dp + 8, dp] = -1
        elif q == 1:
            r32[dp - 8, dp] = 1
        elif q == 2:
            r32[dp + 8, dp] = -1
        else:
            r32[dp - 8, dp] = 1
    for b in range(4):
        c[CR128, b * 32:(b + 1) * 32, b * 32:(b + 1) * 32] = r32
    c[CR96, 64:96, 64:96] = r32
    c[CE2, :64, :64] = np.eye(64)
    c[CE2, 64:, :64] = np.eye(64)
    idx = np.arange(64)
    for d in range(2):
        tri = (idx[:, None] <= idx[None, :]) if d == 0 else (idx[:, None] >= idx[None, :])
        t = np.zeros((128, 128), np.float32)
        t[:64, :64] = tri
        t[64:, 64:] = tri
        c[CTRI0 + d] = t
        valid = (idx[:, None] >= idx[None, :]) if d == 0 else (idx[:, None] <= idx[None, :])
        strict_v = (idx[:, None] > idx[None, :]) if d == 0 else (idx[:, None] < idx[None, :])
        mb = np.full((128, 128), BIG, np.float32)
        mb[:64, :64] = np.where(strict_v, 0, BIG)
        mb[64:, 64:] = np.where(strict_v, 0, BIG)
        c[CMB0 + d] = mb
        nm = np.full((128, 128), -BIG, np.float32)
        nm[:64, :64] = np.where(valid.T, 0, -BIG)
        nm[64:, 64:] = np.where(valid.T, 0, -BIG)
        c[CNM0 + d] = nm
        strict = (idx[:, None] > idx[None, :]) if d == 0 else (idx[:, None] < idx[None, :])
        st = np.zeros((128, 128), np.float32)
        st[:64, :64] = strict
        st[64:, 64:] = strict
        c[CST0 + d] = st
    for li in range(6):
        sz = 1 << li
        m64 = ((idx[:, None] // (2 * sz) == idx[None, :] // (2 * sz)) & (idx[:, None] // sz != idx[None, :] // sz))
        c[CMS + li, :64, :64] = m64
        c[CMS + li, 64:, 64:] = m64
    for li in range(6):
        c[CMS2 + 2 * li] = c[CMS + li]
        c[CMS2 + 2 * li + 1] = c[CMS + li]
    c[CI2] = c[CI]
    c[CI2 + 1] = c[CI]
    return np.ascontiguousarray(c.transpose(1, 0, 2).reshape(128, NCST * 128))


def rope_tables(length, grid_w=64, base=10000.0):
    rows = length // grid_w
    row_pos = np.repeat(np.arange(rows, dtype=np.float32), grid_w)
    col_pos = np.tile(np.arange(grid_w, dtype=np.float32), rows)
    half = 16
    inv_freq = (base ** (-np.arange(0, half, 2, dtype=np.float32) / half)).astype(np.float32)
    ang_r = row_pos[:, None] * inv_freq
    ang_c = col_pos[:, None] * inv_freq
    ang = np.concatenate([ang_r, ang_r, ang_c, ang_c], axis=-1)
    return np.cos(ang).astype(np.float32), np.sin(ang).astype(np.float32)


class Cfg:
    def __init__(self, depth=4, ts=4096, lp=256, npq=2, past=256):
        self.depth, self.ts, self.lp, self.npq, self.past = depth, ts, lp, npq, past
        self.tp = lp * npq
        self.tt = ts + self.tp
        assert ts % 512 == 0 and self.tp % 256 == 0 and lp % 64 == 0 and past % 128 == 0 and lp % 128 == 0
        self.kk = ts + past + self.tp
        self.seqs = [(0, ts, 0, ts + past, True)]
        for i in range(npq):
            self.seqs.append((ts + i * lp, lp, ts + past + i * lp, lp, False))

    def kcol(self, tok):
        return tok if tok < self.ts else tok + self.past


def build(cfg, debug=False):
    nc = bass.Bass("TRN2", target_bir_lowering=False)
    SUBKEY.clear()
    k = K(nc)
    Dp, TS, LP, NPQ, PAST, TT, KKC = cfg.depth, cfg.ts, cfg.lp, cfg.npq, cfg.past, cfg.tt, cfg.kk

    def din(name, shape, dt=F32):
        return nc.dram_tensor(name, list(shape), dt, kind="ExternalInput").ap()

    def dout(name, shape, dt=F32):
        k.dram_names.add(name)
        return nc.dram_tensor(name, list(shape), dt, kind="ExternalOutput").ap()

    def dscr(name, shape, dt=F32):
        k.dram_names.add(name)
        return nc.dram_tensor(name, list(shape), dt, kind="Internal").ap()

    xT_in = din("xT", [D, TT])
    cdk = din("cdk", [Dp, 256, PAST])
    cdv = din("cdv", [Dp, PAST, 256])
    sdl = din("sdl", [Dp, 2, 2, 128, 64])
    cckv = din("cckv", [Dp, 128, PAST])
    ckr = din("ckr", [Dp, 32, PAST])
    sss = din("sss", [Dp, 2, 2, 128, 128])
    cvec = din("cvec", [128, 16])
    pvec = din("pvec", [128, Dp * PV_L])
    prow = din("prow", [1, Dp * PR_L])
    cst_d = din("cst", [128, NCST_BASE * 128])
    cst2_d = din("cst2", [128, (NCST - NCST_BASE) * 128])
    ropec = din("ropec", [128, TS])
    ropes = din("ropes", [128, TS])
    w_mod = din("w_mod", [Dp, D, 6 * D])
    w_in = din("w_in", [Dp, D, WIN_COLS])
    w_uq = din("w_uq", [Dp, 256, 384])
    w_uk = din("w_uk", [Dp, 128, 256])
    w_uv = din("w_uv", [Dp, 128, 256])
    w_out = din("w_out", [Dp, D, D])
    w_ff1 = din("w_ff1", [Dp, D, DFF])
    w_ff2 = din("w_ff2", [Dp, DFF, D])
    yT = dout("yT", [D, TT])
    o_dk = dout("o_dk", [NPQ, Dp, 256, LP])
    o_dv = dout("o_dv", [NPQ, Dp, LP, 256])
    o_sd = dout("o_sd", [NPQ, Dp, 2, 2, 128, 64])
    o_ckv = dout("o_ckv", [NPQ, Dp, 128, LP])
    o_kr = dout("o_kr", [NPQ, Dp, 32, LP])
    o_ss = dout("o_ss", [NPQ, Dp, 2, 2, 128, 128])
    xs = dscr("xs", [D, TT])
    qd = dscr("qd", [256, TT], BF16)
    kd = dscr("kd", [256, KKC], BF16)
    vd = dscr("vd", [KKC, 4, 128], BF16)
    dnpre = dscr("dnpre", [768, TT])
    dngate = dscr("dngate", [256, TT])
    qm = dscr("qm", [4, 96, TT], BF16)
    ckvs = dscr("ckvs", [128, KKC], BF16)
    kr96 = dscr("kr96", [96, KKC], BF16)
    ssmpre = dscr("ssmpre", [768, TT])
    smalls = dscr("smalls", [TT, 24])
    ztm = dscr("ztm", [TT, 256])
    mixT = dscr("mixT", [D, TT], BF16)
    dbg = {}
    if debug:
        dbg["mixT"] = dout("dbg_mixT", [D, TT], BF16)

    cst = k.sb([128, NCST_BASE * 128], name="cst")
    k.dma(cst[:], cst_d[:, :])

    def C(i, rows=128, cols=128, r0=0):
        return cst[r0:r0 + rows, i * 128:i * 128 + cols]

    cstb = k.sb([128, 3 * 128], BF16, name="cstb")
    k.copy(cstb[:, 0:128], C(CI))
    k.copy(cstb[:, 128:256], C(CONES))
    k.copy(cstb[:, 256:384], C(CE2))

    def CB(i, rows=128, cols=128):
        return cstb[0:rows, i * 128:i * 128 + cols]
    pv = k.sb([128, Dp * PV_L], name="pv")
    k.dma(pv[:], pvec[:, :])
    pr = k.sb([128, Dp * PR_L], name="pr")
    k.dma(pr[:], prow[0:1, :].partition_broadcast(128))
    cv = k.sb([128, 16], name="cv")
    k.dma(cv[:], cvec[:, :])
    scv = k.sb([128, 16], name="scv")
    k.act(scv[:], cv[:], AF.Silu)
    modt = k.sb([128, Dp, 48, 2], name="modt")
    onep = k.sb([128, Dp, 48, 2], name="onep")
    neglam = k.sb([128, Dp], name="neglam")

    k.stage_begin()
    wmb = [k.sb([128, 8, 768], name="wmb%d" % i) for i in range(2)]
    pmod = [k.ps([128, 512], name="pmod%d" % i) for i in range(2)]
    it = 0
    for l in range(Dp):
        for g in range(8):
            wb = wmb[it % 2]
            pm = pmod[it % 2]
            it += 1
            k.dma(wb[:], w_mod[l, :, g * 768:(g + 1) * 768].rearrange("(j p) c -> p j c", p=128))
            for cc in range(6):
                for j in range(8):
                    k.mm(pm[:, cc * 2:cc * 2 + 2], wb[:, j, cc * 128:(cc + 1) * 128], scv[:, j:j + 9:8],
                         start=(j == 0), stop=(j == 7))
            for cc in range(6):
                ch = g * 6 + cc
                k.ts(modt[:, l, ch, :], pm[:, cc * 2:cc * 2 + 2], pv[:, l * PV_L + PV_BMOD + ch:l * PV_L + PV_BMOD + ch + 1],
                     None, op0=ALU.add)
        lt = k.sb([128, 64], name="lt%d" % l)
        lsum = k.sb([128, 2], name="lsum%d" % l)
        lb = l * PR_L + PR_LAM
        k.tt(lt[:, 0:32], pr[:, lb:lb + 32], pr[:, lb + 32:lb + 64], ALU.mult)
        k.tt(lt[:, 32:64], pr[:, lb + 64:lb + 96], pr[:, lb + 96:lb + 128], ALU.mult)
        k.op("dve", lambda e, lt=lt, lsum=lsum: e.reduce_sum(out=lsum[:, 0:1], in_=lt[:, 0:32], axis=AX.X), (lt,), (lsum,))
        k.op("dve", lambda e, lt=lt, lsum=lsum: e.reduce_sum(out=lsum[:, 1:2], in_=lt[:, 32:64], axis=AX.X), (lt, lsum), (lsum,))
        k.act(lsum[:], lsum[:], AF.Exp)
        lam_init = 0.8 - 0.6 * math.exp(-0.3 * l)
        k.tt(neglam[:, l:l + 1], lsum[:, 1:2], lsum[:, 0:1], ALU.subtract)
        k.ts(neglam[:, l:l + 1], neglam[:, l:l + 1], -lam_init, None, op0=ALU.add)
    k.ts(onep[:], modt[:], 1.0, None, op0=ALU.add)
    ALPHA0 = (2 * Dp) ** 0.25
    for l in range(Dp):
        for w_ in (2, 5):
            k.ts(modt[:, l, w_ * 8:(w_ + 1) * 8, :], modt[:, l, w_ * 8:(w_ + 1) * 8, :], 1.0 / ALPHA0, None, op0=ALU.mult)
    k.stage_end()

    def modcol(l, which, j, isS):
        t = onep if which in (1, 4) else modt
        return t[:, l, which * 8 + j, (0 if isS else 1):(1 if isS else 2)]

    def pcol(l, off, n=1):
        return pv[:, l * PV_L + off:l * PV_L + off + n]

    nblk = TT // 512 if TT % 512 == 0 else None
    blocks = []
    t0 = 0
    while t0 < TS:
        blocks.append((t0, 512, True))
        t0 += 512
    while t0 < TT:
        n = min(512, TT - t0)
        blocks.append((t0, n, False))
        t0 += n

    def load_w_bf16(dst, src_ap, ncols, stg=None):
        J = dst.shape[1]
        for j in range(J):
            c0 = 0
            while c0 < ncols:
                n = min(2048, ncols - c0)
                k.dma(dst[:, j, c0:c0 + n], src_ap[j * 128:(j + 1) * 128, c0:c0 + n], q="pool")
                c0 += n

    for l in range(Dp):
        xsrc = xT_in if l == 0 else xs
        xdst = yT if l == Dp - 1 else xs
        k.stage_begin()
        winb = k.sbk([128, 8, WIN_COLS], BF16, name="winb")
        load_w_bf16(winb, w_in[l], WIN_COLS)
        wuqb = k.sb([128, 2, 384], BF16, name="wuqb")
        load_w_bf16(wuqb, w_uq[l], 384)
        rcb = [k.sb([128, 512], name="rc%d" % i) for i in range(2)]
        rsb = [k.sb([128, 512], name="rs%d" % i) for i in range(2)]
        xb = [k.sbk([128, 8, 512], name="xb%d" % i) for i in range(2)]
        hb = [k.sbk([128, 8, 512], BF16, name="hb%d" % i) for i in range(2)]
        pA = [k.ps(name="pA%d" % i) for i in range(4)]
        pR = k.ps(name="pR")
        pS = k.ps(name="pS")
        pT = [k.ps(name="pT%d" % i) for i in range(2)]
        ev = [k.sb([128, 512], name="ev%d" % i) for i in range(4)]
        evb = [k.sb([128, 512], BF16, name="evb%d" % i) for i in range(4)]
        t1 = k.sb([128, 512], name="t1")
        sqt = [k.sb([128, 512], name="sq%d" % i) for i in range(2)]
        cqk = [k.sb([128, 512], name="cqk%d" % i) for i in range(2)]
        cqn = k.sb([128, 2, 512], BF16, name="cqn")
        rstd = k.sb([128, 512], name="rstd")
        vtm = [k.sb([128, 280], name="vtm%d" % i) for i in range(2)]
        vaug = [k.sb([128, 4, 128], BF16, name="vaug%d" % i) for i in range(2)]
        for v in vaug:
            k.memset(v[:], 1.0)
        ztt = [k.sb([128, 256], name="ztt%d" % i) for i in range(2)]
        nev = [0]

        def next_ev():
            nev[0] += 1
            return ev[nev[0] % 4], evb[nev[0] % 4]

        def rope(ps_in, rows, Rc, tok0, n, e32):
            lo = 0 if rows == 128 else 64
            k.copy(e32[0:rows, 0:n], ps_in, eng="act")
            k.mm(pR[0:rows, 0:n], C(Rc, rows, rows), e32[0:rows, 0:n])
            k.tt(t1[lo:rows, 0:n], pR[lo:rows, 0:n], rs_[lo:rows, 0:n], ALU.mult)
            k.tt(e32[lo:rows, 0:n], e32[lo:rows, 0:n], rc[lo:rows, 0:n], ALU.mult, eng="pool")
            k.tt(e32[lo:rows, 0:n], e32[lo:rows, 0:n], t1[lo:rows, 0:n], ALU.add)

        for bi, (tok0, n, isS) in enumerate(blocks):
            x_ = xb[bi % 2]
            h_ = hb[bi % 2]
            kc0 = cfg.kcol(tok0)
            k.dma(x_[:, :, 0:n], xsrc[:, tok0:tok0 + n].rearrange("(j p) t -> p j t", p=128))
            rc, rs_ = rcb[bi % 2], rsb[bi % 2]
            if isS:
                k.dma(rc[:, 0:n], ropec[:, tok0:tok0 + n])
                k.dma(rs_[:, 0:n], ropes[:, tok0:tok0 + n])
            for j in range(8):
                k.act(h_[:, j, 0:n], x_[:, j, 0:n], AF.Identity, bias=modcol(l, 0, j, isS), scale=modcol(l, 1, j, isS))
            pseqs = []
            if not isS:
                for si in range(NPQ):
                    s0 = TS + si * LP
                    lo, hi = max(s0, tok0), min(s0 + LP, tok0 + n)
                    if lo < hi:
                        pseqs.append((si, lo - s0, lo - tok0, hi - lo))
            for ti, (nm, _, _) in enumerate(FM_TILES):
                pa = pA[ti % 4]
                for j in range(8):
                    k.mm(pa[:, 0:n], winb[:, j, ti * 128:(ti + 1) * 128], h_[:, j, 0:n], start=(j == 0), stop=(j == 7))
                e32, e16 = next_ev()
                if nm in ("aq0", "aq1", "ak0", "ak1"):
                    hp = int(nm[2])
                    if isS:
                        rope(pa[:, 0:n], 128, CR128, tok0, n, e32)
                    else:
                        k.copy(e32[:, 0:n], pa[:, 0:n], eng="act")
                        if nm[1] == "k":
                            for (si, so, bo, nn) in pseqs:
                                k.dma(o_dk[si, l, hp * 128:(hp + 1) * 128, so:so + nn], e32[:, bo:bo + nn], q="pool", is_out=True)
                    k.anycopy(e16[:, 0:n], e32[:, 0:n])
                    if nm[1] == "q":
                        k.dma(qd[hp * 128:(hp + 1) * 128, tok0:tok0 + n], e16[:, 0:n], q="pool")
                    else:
                        k.dma(kd[hp * 128:(hp + 1) * 128, kc0:kc0 + n], e16[:, 0:n], q="pool")
                elif nm[0] == "b" and nm[1] in "qkv":
                    r0 = {"q": 0, "k": 256, "v": 512}[nm[1]] + int(nm[2]) * 128
                    k.copy(e32[:, 0:n], pa[:, 0:n], eng="act")
                    k.dma(dnpre[r0:r0 + 128, tok0:tok0 + n], e32[:, 0:n], q="pool")
                elif nm[0:2] == "bg":
                    r0 = int(nm[2]) * 128
                    k.act(e32[:, 0:n], pa[:, 0:n], AF.Silu)
                    k.dma(dngate[r0:r0 + 128, tok0:tok0 + n], e32[:, 0:n], q="pool")
                elif nm[0:2] == "cq":
                    i = int(nm[2])
                    k.copy(cqk[i][:, 0:n], pa[:, 0:n], eng="act")
                    k.tt(sqt[i][:, 0:n], cqk[i][:, 0:n], cqk[i][:, 0:n], ALU.mult, eng="pool")
                    k.mm(pS[:, 0:n], C(CONES), sqt[i][:, 0:n], start=(i == 0), stop=(i == 1))
                    if i == 1:
                        k.rsqrt(rstd[:, 0:n], pS[:, 0:n], 1.0 / 256.0, EPS)
                        for ii in range(2):
                            k.stt(cqn[:, ii, 0:n], cqk[ii][:, 0:n], pcol(l, PV_QN + ii), rstd[:, 0:n], ALU.mult, ALU.mult)
                        for h in range(4):
                            pq = pT[h % 2]
                            for ii in range(2):
                                k.mm(pq[0:96, 0:n], wuqb[:, ii, h * 96:(h + 1) * 96], cqn[:, ii, 0:n], start=(ii == 0), stop=(ii == 1))
                            q32, q16 = next_ev()
                            if isS:
                                rope(pq[0:96, 0:n], 96, CR96, tok0, n, q32)
                            else:
                                k.copy(q32[0:96, 0:n], pq[0:96, 0:n], eng="act")
                            k.anycopy(q16[0:96, 0:n], q32[0:96, 0:n])
                            k.dma(qm[h, :, tok0:tok0 + n], q16[0:96, 0:n], q="pool")
                elif nm == "ckv":
                    k.copy(e32[:, 0:n], pa[:, 0:n], eng="act")
                    k.tt(sqt[0][:, 0:n], e32[:, 0:n], e32[:, 0:n], ALU.mult, eng="pool")
                    k.mm(pS[:, 0:n], C(CONES), sqt[0][:, 0:n])
                    k.rsqrt(rstd[:, 0:n], pS[:, 0:n], 1.0 / 128.0, EPS)
                    k.stt(e32[:, 0:n], e32[:, 0:n], pcol(l, PV_KVN), rstd[:, 0:n], ALU.mult, ALU.mult)
                    for (si, so, bo, nn) in pseqs:
                        k.dma(o_ckv[si, l, :, so:so + nn], e32[:, bo:bo + nn], q="pool", is_out=True)
                    k.anycopy(e16[:, 0:n], e32[:, 0:n])
                    k.dma(ckvs[:, kc0:kc0 + n], e16[:, 0:n], q="pool")
                elif nm[0] == "d":
                    r0 = {"x": 0, "B": 256, "C": 512}[nm[1]] + int(nm[2]) * 128
                    k.copy(e32[:, 0:n], pa[:, 0:n], eng="act")
                    k.dma(ssmpre[r0:r0 + 128, tok0:tok0 + n], e32[:, 0:n], q="pool")
            pa = pA[NFM % 4]
            for j in range(8):
                k.mm(pa[0:96, 0:n], winb[:, j, C_KR:C_KR + 96], h_[:, j, 0:n], start=(j == 0), stop=(j == 7))
            e32, e16 = next_ev()
            if isS:
                rope(pa[0:96, 0:n], 96, CR96, tok0, n, e32)
            else:
                k.copy(e32[0:96, 0:n], pa[0:96, 0:n], eng="act")
                for (si, so, bo, nn) in pseqs:
                    k.dma(o_kr[si, l, :, so:so + nn], e32[64:96, bo:bo + nn], q="pool", is_out=True)
            k.anycopy(e16[0:96, 0:n], e32[0:96, 0:n])
            k.dma(kr96[:, kc0:kc0 + n], e16[0:96, 0:n], q="pool")
            for tt_ in range(n // 128):
                tk = tok0 + tt_ * 128
                p1 = pT[0]
                p2 = pT[1]
                for j in range(8):
                    k.mm(p1[:, 0:280], h_[:, j, tt_ * 128:(tt_ + 1) * 128], winb[:, j, C_TM1:C_TM1 + 280], start=(j == 0), stop=(j == 7))
                for j in range(8):
                    k.mm(p2[:, 0:256], h_[:, j, tt_ * 128:(tt_ + 1) * 128], winb[:, j, C_TM2:C_TM2 + 256], start=(j == 0), stop=(j == 7))
                vt = vtm[tt_ % 2]
                va = vaug[tt_ % 2]
                zt = ztt[tt_ % 2]
                k.copy(vt[:, 0:280], p1[:, 0:280], eng="act")
                k.act(zt[:], p2[:, 0:256], AF.Silu)
                for h in range(4):
                    k.anycopy(va[:, h, 0:64], vt[:, h * 64:(h + 1) * 64])
                if not isS:
                    for si in range(NPQ):
                        s0 = TS + si * LP
                        if s0 <= tk < s0 + LP:
                            k.dma(o_dv[si, l, tk - s0:tk - s0 + 128, :], vt[:, 0:256], q="pool", is_out=True)
                kc = cfg.kcol(tk)
                k.dma(vd[kc:kc + 128, :, :], va[:], q="pool")
                k.dma(smalls[tk:tk + 128, :], vt[:, 256:280], q="pool")
                k.dma(ztm[tk:tk + 128, :], zt[:], q="pool")
        pk = k.sb([128, 2, PAST], name="pk")
        pk16 = k.sb([128, 2, PAST], BF16, name="pk16")
        k.dma(pk[:], cdk[l].rearrange("(a p) t -> p a t", p=128))
        k.anycopy(pk16[:], pk[:])
        for a in range(2):
            k.dma(kd[a * 128:(a + 1) * 128, TS:TS + PAST], pk16[:, a, :], q="pool")
        for tt_ in range(PAST // 128):
            vt = vtm[tt_ % 2]
            va = vaug[tt_ % 2]
            k.dma(vt[:, 0:256], cdv[l, tt_ * 128:(tt_ + 1) * 128, :])
            for h in range(4):
                k.anycopy(va[:, h, 0:64], vt[:, h * 64:(h + 1) * 64])
            k.dma(vd[TS + tt_ * 128:TS + (tt_ + 1) * 128, :, :], va[:], q="pool")
        pc = k.sb([128, PAST], name="pc")
        pc16 = k.sb([128, PAST], BF16, name="pc16")
        k.dma(pc[:], cckv[l])
        k.anycopy(pc16[:], pc[:])
        k.dma(ckvs[:, TS:TS + PAST], pc16[:], q="pool")
        pr32 = k.sb([128, PAST], name="pr32")
        pr16 = k.sb([128, PAST], BF16, name="pr16")
        k.dma(pr32[64:96, :], ckr[l])
        k.anycopy(pr16[64:96, :], pr32[64:96, :])
        k.dma(kr96[64:96, TS:TS + PAST], pr16[64:96, :], q="pool")
        k.stage_end()

        def attention(seq, streams, scale, emit_out):
            tokS, L, ks, nk, isS = seq
            nkt = nk // 128
            QC = 512 if L % 512 == 0 else (256 if L % 256 == 0 else 128)
            for qc in range(L // QC):
                qts = []
                for si, (q_ap, kap, vfn) in enumerate(streams):
                    qt = qtl[si][qc % 2]
                    rows = kap.shape[0]
                    r0 = kap.base_partition() if hasattr(kap, "base_partition") else 0
                    k.dma(qt[r0:r0 + rows, 0:QC], q_ap[:, qc * QC:(qc + 1) * QC])
                    qts.append(qt[r0:r0 + rows, 0:QC])
                for si, (q_ap, kap, vfn) in enumerate(streams):
                    k.mm(pSc[si][0][:, 0:QC], kap[:, 0:128], qts[si])
                for kt in range(nkt):
                    if kt + 1 < nkt:
                        for si, (q_ap, kap, vfn) in enumerate(streams):
                            k.mm(pSc[si][(kt + 1) % 2][:, 0:QC], kap[:, (kt + 1) * 128:(kt + 2) * 128], qts[si])
                    if PE_FILL:
                        k.mm(pDm[:, 0:QC], CB(0), dmy[:, 0:QC])
                    for si in range(len(streams)):
                        k.act(ptl[si][kt % 2][:, 0:QC], pSc[si][kt % 2][:, 0:QC], AF.Exp, scale=scale)
                    for si, (q_ap, kap, vfn) in enumerate(streams):
                        k.mm(pO[si][:, 0:QC], vfn(kt), ptl[si][kt % 2][:, 0:QC], start=(kt == 0), stop=(kt == nkt - 1))
                emit_out(qc, QC)

        k.stage_begin()
        qtl = [[k.sb([128, 512], BF16, name="qt%d_%d" % (si, i)) for i in range(2)] for si in range(2)]
        ptl = [[k.sb([128, 512], BF16, name="pt%d_%d" % (si, i)) for i in range(2)] for si in range(2)]
        pSc = [[k.ps(name="pSc%d_%d" % (si, i)) for i in range(2)] for si in range(2)]
        pO = [k.ps(name="pO%d" % i) for i in range(2)]
        pN = k.ps(name="pN")
        pDm = k.ps(name="pDm")
        dmy = k.sb([128, 512], BF16, name="dmy")
        k.memset(dmy[:], 1.0)
        pKV = [pSc[0][0], pSc[0][1]]
        maxk = TS + PAST
        ktl = [k.sb([128, maxk], BF16, name="ktile%d" % i) for i in range(2)]
        vtile = k.sb([128, maxk // 128, 128], BF16, name="vtile")
        o0 = k.sb([64, 512], name="o0")
        o1 = k.sb([64, 512], name="o1")
        rr0 = k.sb([64, 512], name="rr0")
        rr1 = k.sb([64, 512], name="rr1")
        osq = k.sb([64, 512], name="osq")
        ob = [k.sb([64, 512], BF16, name="ob%d" % i) for i in range(2)]
        ob16 = [[k.sb([64, 512], BF16, name="obm%d_%d" % (hh, i)) for i in range(2)] for hh in range(2)]
        osl = [[k.sb([128, 512], name="os%d_%d" % (i, si)) for si in range(2)] for i in range(2)]
        gcol = k.sb([128, 1], name="gcol")
        lam_init = 0.8 - 0.6 * math.exp(-0.3 * l)
        k.ts(gcol[:], pcol(l, PV_DIFFN), 1.0 - lam_init, None, op0=ALU.mult)
        nob = [0]
        for seq in cfg.seqs:
            tokS, L, ks, nk, isS = seq
            for h in range(4):
                ktile = ktl[h % 2]
                k.dma(ktile[0:64, 0:nk], kd[h * 64:(h + 1) * 64, ks:ks + nk])
                k.dma(vtile[:, 0:nk // 128, :], vd[ks:ks + nk, h, :].rearrange("(t p) c -> p t c", p=128))

                def out_diff(qc, QC, h=h, tokS=tokS):
                    nob[0] += 1
                    osb = osl[nob[0] % 2]
                    for si in range(2):
                        k.copy(osb[si][:, 0:QC], pO[si][:, 0:QC], eng="act")
                        k.arecip(osb[si][64:128, 0:QC], osb[si][64:128, 0:QC])
                    k.copy(rr0[:, 0:QC], osb[0][64:128, 0:QC], eng="pool")
                    k.copy(rr1[:, 0:QC], osb[1][64:128, 0:QC], eng="pool")
                    k.tt(o0[:, 0:QC], osb[0][0:64, 0:QC], rr0[:, 0:QC], ALU.mult)
                    k.tt(o1[:, 0:QC], osb[1][0:64, 0:QC], rr1[:, 0:QC], ALU.mult)
                    k.stt(o0[:, 0:QC], o1[:, 0:QC], neglam[0:64, l:l + 1], o0[:, 0:QC], ALU.mult, ALU.add)
                    k.tt(osq[:, 0:QC], o0[:, 0:QC], o0[:, 0:QC], ALU.mult, eng="pool")
                    k.mm(pN[0:64, 0:QC], C(CONESBD, 64, 64), osq[:, 0:QC])
                    k.rsqrt(rr0[:, 0:QC], pN[0:64, 0:QC], 1.0 / 64.0, EPS)
                    o16 = ob[nob[0] % 2]
                    k.stt(o16[:, 0:QC], o0[:, 0:QC], gcol[0:64, :], rr0[:, 0:QC], ALU.mult, ALU.mult)
                    k.dma(mixT[h * 64:(h + 1) * 64, tokS + qc * QC:tokS + (qc + 1) * QC], o16[:, 0:QC], q="pool")

                vfn = lambda kt: vtile[:, kt, :]
                streams = [(qd[h * 64 + s_ * 32:h * 64 + (s_ + 1) * 32, tokS:tokS + L], ktile[s_ * 32:(s_ + 1) * 32, 0:nk], vfn)
                           for s_ in range(2)]
                attention(seq, streams, 32 ** -0.5, out_diff)
        wukb = k.sb([128, 1, 256], BF16, name="wukb")
        wuvb = k.sb([128, 1, 256], BF16, name="wuvb")
        load_w_bf16(wukb, w_uk[l], 256)
        load_w_bf16(wuvb, w_uv[l], 256)
        ckt = k.sb([128, maxk], BF16, name="ckt")
        vall = k.sb([128, maxk // 128, 4, 128], BF16, name="vall")
        k.memset(vall[:], 1.0)
        for seq in cfg.seqs:
            tokS, L, ks, nk, isS = seq
            k.dma(ckt[:, 0:nk], ckvs[:, ks:ks + nk])
            for kt in range(nk // 128):
                pkv = pKV[kt % 2]
                k.mm(pkv[:, 0:256], ckt[:, kt * 128:(kt + 1) * 128], wuvb[:, 0, :])
                for h in range(4):
                    k.copy(vall[:, kt, h, 0:64], pkv[:, h * 64:(h + 1) * 64], eng="dve")
            for hpair in range(2):
                for hh in range(2):
                    h = hpair * 2 + hh
                    ktile = ktl[hh]
                    c0 = 0
                    while c0 < nk:
                        n = min(512, nk - c0)
                        pkv = pKV[(c0 // 512) % 2]
                        k.mm(pkv[0:64, 0:n], wukb[:, 0, h * 64:(h + 1) * 64], ckt[:, c0:c0 + n])
                        k.copy(ktile[0:64, c0:c0 + n], pkv[0:64, 0:n], eng="act")
                        c0 += n
                    k.dma(ktile[64:96, 0:nk], kr96[64:96, ks:ks + nk])

                def out_mla(qc, QC, hpair=hpair, tokS=tokS):
                    nob[0] += 1
                    osb = osl[nob[0] % 2]
                    for hh, rr in ((0, rr0), (1, rr1)):
                        h = hpair * 2 + hh
                        k.copy(osb[hh][:, 0:QC], pO[hh][:, 0:QC], eng="act")
                        k.arecip(osb[hh][64:128, 0:QC], osb[hh][64:128, 0:QC])
                        k.copy(rr[:, 0:QC], osb[hh][64:128, 0:QC], eng="pool")
                        o16 = ob16[hh][nob[0] % 2]
                        k.tt(o16[:, 0:QC], osb[hh][0:64, 0:QC], rr[:, 0:QC], ALU.mult)
                        k.dma(mixT[512 + h * 64:512 + (h + 1) * 64, tokS + qc * QC:tokS + (qc + 1) * QC], o16[:, 0:QC], q="pool")

                streams = [(qm[hpair * 2 + hh, :, tokS:tokS + L], ktl[hh][0:96, 0:nk],
                            (lambda kt, h=hpair * 2 + hh: vall[:, kt, h, :])) for hh in range(2)]
                attention(seq, streams, 96 ** -0.5, out_mla)
        k.stage_end()

        for seq in cfg.seqs:
            dn_seq(k, cfg, l, seq, C, CB, pcol, pr, dnpre, dngate, smalls, sdl, o_sd, mixT, cst2_d)
        for seq in cfg.seqs:
            ssd_seq(k, cfg, l, seq, C, CB, pcol, pr, ssmpre, smalls, ztm, sss, o_ss, mixT)

        if debug and l == 0:
            k.stage_begin()
            dt_ = k.sb([128, 8, 512], BF16, name="dbgt")
            for (tok0, n, isS) in blocks:
                k.dma(dt_[:, :, 0:n], mixT[:, tok0:tok0 + n].rearrange("(j p) t -> p j t", p=128))
                k.dma(dbg["mixT"][:, tok0:tok0 + n].rearrange("(j p) t -> p j t", p=128), dt_[:, :, 0:n], q="pool", is_out=True)
            k.stage_end()

        def layer_norm(xin, n, gq, bq, xout):
            for j in range(8):
                k.mm(pL[0][:, 0:n], C(CONES), xin[:, j, 0:n], start=(j == 0), stop=(j == 7))
            k.ts(mean[:, 0:n], pL[0][:, 0:n], -1.0 / D, None, op0=ALU.mult)
            for j in range(8):
                k.tt(xin[:, j, 0:n], xin[:, j, 0:n], mean[:, 0:n], ALU.add)
                k.act(lsq[:, j, 0:n], xin[:, j, 0:n], AF.Square)
            for j in range(8):
                k.mm(pL[1][:, 0:n], C(CONES), lsq[:, j, 0:n], start=(j == 0), stop=(j == 7))
            k.rsqrt(mean[:, 0:n], pL[1][:, 0:n], 1.0 / D, EPS / (ALPHA * ALPHA))
            for j in range(8):
                k.tt(xin[:, j, 0:n], xin[:, j, 0:n], mean[:, 0:n], ALU.mult)
                k.act(xout[:, j, 0:n], xin[:, j, 0:n], AF.Identity, bias=pcol(l, bq + j), scale=pcol(l, gq + j))

        ALPHA = (2 * Dp) ** 0.25
        k.stage_begin()
        woutb = k.sbk([128, 8, D], BF16, name="woutb")
        load_w_bf16(woutb, w_out[l], D)
        xb = [k.sbk([128, 8, 512], name="xc%d" % i) for i in range(2)]
        mb = [k.sbk([128, 8, 512], BF16, name="mb%d" % i) for i in range(2)]
        xo = [k.sbk([128, 8, 512], name="xo%d" % i) for i in range(2)]
        lsq = k.sbk([128, 8, 512], name="lsq")
        mean = k.sb([128, 512], name="mean")
        pL = [k.ps(name="pL%d" % i) for i in range(2)]
        pC = [k.ps(name="pC%d" % i) for i in range(4)]
        for bi, (tok0, n, isS) in enumerate(blocks):
            x_ = xb[bi % 2]
            m_ = mb[bi % 2]
            xo_ = xo[bi % 2]
            k.dma(x_[:, :, 0:n], xsrc[:, tok0:tok0 + n].rearrange("(j p) t -> p j t", p=128))
            k.dma(m_[:, :, 0:n], mixT[:, tok0:tok0 + n].rearrange("(j p) t -> p j t", p=128))
            for oc in range(8):
                pc_ = pC[oc % 4]
                for j in range(8):
                    k.mm(pc_[:, 0:n], woutb[:, j, oc * 128:(oc + 1) * 128], m_[:, j, 0:n], start=(j == 0), stop=(j == 7))
                k.stt(x_[:, oc, 0:n], pc_[:, 0:n], modcol(l, 2, oc, isS), x_[:, oc, 0:n], ALU.mult, ALU.add)
            layer_norm(x_, n, PV_LN1G, PV_LN1B, xo_)
            k.dma(xs[:, tok0:tok0 + n].rearrange("(j p) t -> p j t", p=128), xo_[:, :, 0:n], q="pool")
        k.stage_end()

        k.stage_begin()
        w1b = k.sbk([128, 8, DFF], BF16, name="w1b")
        w2b = k.sbk([128, 32, D], BF16, name="w2b")
        load_w_bf16(w1b, w_ff1[l], DFF)
        load_w_bf16(w2b, w_ff2[l], D)
        NB = 256
        xb = [k.sbk([128, 8, NB], name="xd%d" % i) for i in range(2)]
        hb = [k.sbk([128, 8, NB], BF16, name="hd%d" % i) for i in range(2)]
        fh = k.sbk([128, 32, NB], BF16, name="fh")
        rl = [k.sb([128, NB], name="rl%d" % i) for i in range(2)]
        lsq = k.sbk([128, 8, NB], name="lsq2")
        mean = k.sb([128, NB], name="mean2")
        pL = [k.ps(name="pL2%d" % i) for i in range(2)]
        pC = [k.ps(name="pC2%d" % i) for i in range(4)]
        bi = 0
        for t0 in range(0, TT, NB):
            n = NB
            isS = t0 < TS
            x_ = xb[bi % 2]
            h_ = hb[bi % 2]
            xo_ = x_
            bi += 1
            k.dma(x_[:, :, 0:n], xs[:, t0:t0 + n].rearrange("(j p) t -> p j t", p=128))
            for j in range(8):
                k.act(h_[:, j, 0:n], x_[:, j, 0:n], AF.Identity, bias=modcol(l, 3, j, isS), scale=modcol(l, 4, j, isS))
            for fc in range(32):
                pc_ = pC[fc % 4]
                for j in range(8):
                    k.mm(pc_[:, 0:n], w1b[:, j, fc * 128:(fc + 1) * 128], h_[:, j, 0:n], start=(j == 0), stop=(j == 7))
                r_ = rl[fc % 2]
                k.act(r_[:, 0:n], pc_[:, 0:n], AF.Relu)
                k.tt(fh[:, fc, 0:n], r_[:, 0:n], r_[:, 0:n], ALU.mult, eng=("dve", "pool")[fc % 2])
            for oc in range(8):
                pc_ = pC[oc % 4]
                for fc in range(32):
                    k.mm(pc_[:, 0:n], w2b[:, fc, oc * 128:(oc + 1) * 128], fh[:, fc, 0:n], start=(fc == 0), stop=(fc == 31))
                k.stt(x_[:, oc, 0:n], pc_[:, 0:n], modcol(l, 5, oc, isS), x_[:, oc, 0:n], ALU.mult, ALU.add)
            layer_norm(x_, n, PV_LN2G, PV_LN2B, xo_)
            k.dma(xdst[:, t0:t0 + n].rearrange("(j p) t -> p j t", p=128), xo_[:, :, 0:n], q="pool", is_out=(l == Dp - 1))
        k.stage_end()

    k.finish()
    return nc


def conv_silu(k, out, tmp, src, w3, n, bias=None):
    k.ts(tmp[:, 0:n], src[:, 1:n + 1], w3[:, 1:2], None, op0=ALU.mult)
    k.stt(tmp[:, 0:n], src[:, 0:n], w3[:, 0:1], tmp[:, 0:n], ALU.mult, ALU.add)
    k.stt(tmp[:, 0:n], src[:, 2:n + 2], w3[:, 2:3], tmp[:, 0:n], ALU.mult, ALU.add)
    if bias is None:
        k.act(out[:, 0:n], tmp[:, 0:n], AF.Silu)
    else:
        k.act(out[:, 0:n], tmp[:, 0:n], AF.Silu, bias=bias, scale=1.0)


class Ring:
    def __init__(self, k, name, shape, dt, n):
        self.t = [k.sb(list(shape), dt, name="%s_%d" % (name, i)) for i in range(n)]

    def __getitem__(self, i):
        return self.t[i % len(self.t)]


def psum_slots(k, name, nbanks, slot=128):
    slots = []
    for i in range(nbanks):
        p = k.ps(name="%s%d" % (name, i))
        slots.append(p[:, 0:slot])
    return slots


def interleave(gens):
    gens = list(gens)
    while gens:
        for g_ in list(gens):
            try:
                next(g_)
            except StopIteration:
                gens.remove(g_)


class Slots:
    def __init__(self, banks, width=128):
        self.s = []
        for j in range(512 // width):
            for b in banks:
                self.s.append(b[:, j * width:(j + 1) * width])
        self.n = 0

    def __call__(self):
        self.n += 1
        return self.s[self.n % len(self.s)]


def dn_seq(k, cfg, l, seq, C, CB, pcol, pr, dnpre, dngate, smalls, sdl, o_sd, mixT, cst2_d):
    tokS, L, ks, nk, isS = seq
    nch = L // 64
    k.stage_begin()
    banks = [k.ps(name="dnP%d" % i) for i in range(8)]
    P0 = Slots(banks)
    HP = []
    for hp in range(2):
        qn = k.sb([128, L], BF16, name="dnq%d" % hp)
        kn = k.sb([128, L], BF16, name="dnk%d" % hp)
        vc = k.sb([128, L], BF16, name="dnv%d" % hp)
        sm = k.sb([128, nch, 24], name="dnsm%d" % hp)
        oall = k.subkey(k.sb([128, nch, 64], name="dnoall%d" % hp), nch * 64, 64)
        HP.append((qn, kn, vc, sm, oall))
    k.stage_begin()
    raw = k.sb([128, L + 2], name="dnraw")
    tmp = k.sb([128, L], name="dntmpc")
    sq = k.sb([128, 512], name="dnsq")
    rs = k.sb([128, 512], name="dnrs")
    for hp in range(2):
        qn, kn, vc, sm, oall = HP[hp]
        for which, dst in ((0, qn), (1, kn), (2, vc)):
            r0 = which * 256 + hp * 128
            k.memset(raw[:, 0:1], 0.0)
            k.memset(raw[:, L + 1:L + 2], 0.0)
            k.dma(raw[:, 1:L + 1], dnpre[r0:r0 + 128, tokS:tokS + L])
            ti = which * 2 + hp
            if which == 2:
                conv_silu(k, dst, tmp, raw, pcol(l, PV_DNCONV + ti * 3, 3), L)
            else:
                conv_silu(k, tmp, tmp, raw, pcol(l, PV_DNCONV + ti * 3, 3), L)
                for c0 in range(0, L, 512):
                    n = min(512, L - c0)
                    k.tt(sq[:, 0:n], tmp[:, c0:c0 + n], tmp[:, c0:c0 + n], ALU.mult, eng="pool")
                    for q4 in range(0, n, 128):
                        p = P0()
                        k.mm(p[:, 0:128], C(CONESBD), sq[:, q4:q4 + 128])
                        k.rsqrt(rs[:, q4:q4 + 128], p[:, 0:128], 1.0, EPS)
                    if which == 0:
                        k.stt(dst[:, c0:c0 + n], tmp[:, c0:c0 + n], 0.125, rs[:, 0:n], ALU.mult, ALU.mult)
                    else:
                        k.tt(dst[:, c0:c0 + n], tmp[:, c0:c0 + n], rs[:, 0:n], ALU.mult)
        src = smalls[tokS:tokS + L, :].rearrange("(c i) k -> i c k", i=64)
        k.dma(sm[0:64, :, :], src)
        k.dma(sm[64:128, :, :], src)
    k.stage_end()
    tmq = k.sb([128, nch], name="dntmp")
    assert nch <= 128
    chains = []

    cx = k.sb([128, (NCST - NCST_BASE) * 128], name="dncx")
    k.dma(cx[:], cst2_d[:, :])

    def C2(i):
        if i >= NCST_BASE:
            return cx[:, (i - NCST_BASE) * 128:(i - NCST_BASE + 2) * 128]
        return C(i, 128, 256)

    for hp in range(2):
        qn, kn, vc, sm, oall = HP[hp]
        dd = []
        S2 = k.sb([128, 128], name="dnS%d" % hp)
        Sb2 = k.sb([128, 128], BF16, name="dnSb%d" % hp)
        for d in range(2):
            G = k.sb([128, nch], name="dnG%d%d" % (hp, d))
            Bt = k.sb([128, nch], name="dnB%d%d" % (hp, d))
            for hh in range(2):
                h = 2 * hp + hh
                rows = slice(hh * 64, hh * 64 + 64)
                cb = l * PR_L
                k.act(Bt[rows, :], sm[rows, :, d * 4 + h], AF.Sigmoid)
                k.act(tmq[rows, :], sm[rows, :, 8 + d * 4 + h], AF.Exp,
                      bias=pr[rows, cb + PR_DNB + d * 4 + h:cb + PR_DNB + d * 4 + h + 1], scale=1.0)
                k.act(tmq[rows, :], tmq[rows, :], AF.Ln, bias=1.0, scale=1.0)
                ea = k.sb([128, 1], name="dnea")
                k.act(ea[rows, :], pr[rows, cb + PR_DNA + d * 4 + h:cb + PR_DNA + d * 4 + h + 1], AF.Exp)
                k.ts(G[rows, :], tmq[rows, :], ea[rows, :], -1.0, op0=ALU.mult, op1=ALU.mult)
            nm = "%d%d" % (hp, d)
            gc = k.sb([128, nch], name="dngc" + nm)
            gl = k.sb([128, nch], name="dngl" + nm)
            p = P0()
            k.mm(p[:, 0:nch], C(CTRI0 + d), G[:, :])
            k.copy(gc[:], p[:, 0:nch], eng="act")
            p = P0()
            k.mm(p[:, 0:nch], C(CONESBD), G[:, :])
            k.copy(gl[:], p[:, 0:nch], eng="act")
            egc = k.sb([128, nch], name="dnegc" + nm)
            egl = k.sb([128, nch], name="dnegl" + nm)
            ekd = k.sb([128, nch], name="dnekd" + nm)
            bg = k.sb([128, nch], name="dnbg" + nm)
            ngc = k.sb([128, nch], name="dnngc" + nm)
            k.act(egc[:], gc[:], AF.Exp)
            k.act(egl[:], gl[:], AF.Exp)
            k.tt(ekd[:], gl[:], gc[:], ALU.subtract)
            k.act(ekd[:], ekd[:], AF.Exp)
            k.tt(bg[:], Bt[:], egc[:], ALU.mult)
            k.ts(ngc[:], gc[:], -1.0, None, op0=ALU.mult)
            if isS:
                k.dma(S2[:, d * 64:(d + 1) * 64], sdl[l, d, hp, :, :])
            else:
                k.memset(S2[:, d * 64:(d + 1) * 64], 0.0)
            dd.append(dict(G=G, Bt=Bt, gc=gc, egc=egc, egl=egl, ekd=ekd, bg=bg, ngc=ngc))
        k.copy(Sb2[:], S2[:], eng="act")

        def R(name, shape=(128, 256), dt=BF16, n=2, hp=hp):
            return Ring(k, "%s%d" % (name, hp), shape, dt, n)

        bk = banks[4 * hp:4 * hp + 4]
        ch = dict(hp=hp, dd=dd, S2=S2, Sb2=Sb2, qn=qn, kn=kn, vc=vc, oall=oall,
                  P=Slots(bk[0:3], 256), PS=Slots(bk[3:4], 128),
                  KT=R("dnKT"), QT=R("dnQT"), VT=R("dnVT"),
                  TG=R("dnTG", dt=F32, n=1), M1=R("dnM1", dt=F32, n=1), M2=R("dnM2", dt=F32, n=1),
                  Dex=R("dnDex", dt=F32, n=1), DexT=R("dnDexT", dt=F32, n=1),
                  A=R("dnA", n=1), B=R("dnBm", n=1), Tt=R("dnTt", n=1), Ls=R("dnLs", n=6), Zs=R("dnZs", n=1),
                  Ts=R("dnTs", n=1), QKm=R("dnQKm"), vb=R("dnvb", (128, 128), n=1), kbg=R("dnkbg", n=1),
                  kdec=R("dnkdec"), U=R("dnU", (128, 128), F32), WT=R("dnWT"),
                  vnew=R("dnvn", (128, 128), BF16, 1), tq=R("dntq", (128, 128), F32, 1))
        for nm_ in ("KT", "QT", "VT"):
            for t in ch[nm_].t:
                k.memset(t[:], 0.0)
        chains.append(ch)

    def chunk_of(s, d):
        return s if d == 0 else nch - 1 - s

    def H(t, d, w=128):
        return t[:, d * w:(d + 1) * w]

    def pre(ch, s):
        P = ch["P"]
        dd = ch["dd"]
        kt2, qt2, vt2 = ch["KT"][s], ch["QT"][s], ch["VT"][s]
        tg2 = ch["TG"][s]
        for d in range(2):
            c = chunk_of(s, d)
            cs = slice(c * 64, (c + 1) * 64)
            for (dst, srcT) in ((kt2, ch["kn"]), (qt2, ch["qn"]), (vt2, ch["vc"])):
                k.copy(H(dst, d)[0:64, 0:64], srcT[0:64, cs], eng="pool")
                k.copy(H(dst, d)[64:128, 64:128], srcT[64:128, cs], eng="pool")
            k.act(H(tg2, d), C(CTRI0 + d), AF.Copy, scale=dd[d]["G"][:, c:c + 1])
        yield
        p1, p2, p3, p4, p5 = P(), P(), P(), P(), P()
        for d in range(2):
            k.mm(H(p1, d), H(kt2, d), H(kt2, d))
        for d in range(2):
            k.mm(H(p2, d), H(kt2, d), H(qt2, d))
        for d in range(2):
            k.mm(H(p3, d), H(kt2, d), CB(0))
        for d in range(2):
            k.mm(H(p4, d, 64), H(vt2, d), CB(2, 128, 64))
        for d in range(2):
            k.mm(H(p5, d), C(CONESBD), H(tg2, d))
        yield
        M1, M2 = ch["M1"][s], ch["M2"][s]
        k.tt(M1[:], p5, C2(CMB0), ALU.add)
        k.tt(M2[:], p5, C2(CNM0), ALU.add)
        vb2, kbg2, kdec2 = ch["vb"][s], ch["kbg"][s], ch["kdec"][s]
        for d in range(2):
            c = chunk_of(s, d)
            col = lambda t: t[:, c:c + 1]
            k.act(H(vb2, d, 64), H(p4, d, 64), AF.Copy, scale=col(dd[d]["Bt"]))
            k.act(H(kbg2, d), H(p3, d), AF.Copy, scale=col(dd[d]["bg"]))
            k.act(H(kdec2, d), H(p3, d), AF.Copy, scale=col(dd[d]["ekd"]))
        yield
        Dex, DexT = ch["Dex"][s], ch["DexT"][s]
        for d in range(2):
            c = chunk_of(s, d)
            col = lambda t: t[:, c:c + 1]
            k.act(H(Dex, d), H(M1, d), AF.Exp, bias=col(dd[d]["gc"]), scale=-1.0)
            k.act(H(DexT, d), H(M2, d), AF.Exp, bias=col(dd[d]["ngc"]), scale=1.0)
        yield
        a2 = ch["A"][s]
        for d in range(2):
            c = chunk_of(s, d)
            k.stt(H(a2, d), H(p1, d), dd[d]["Bt"][:, c:c + 1], H(Dex, d), ALU.mult, ALU.mult)
        k.tt(ch["QKm"][s][:], p2, DexT[:], ALU.mult)
        yield
        p6 = P()
        for d in range(2):
            k.mm(H(p6, d), H(a2, d), CB(0))
        for li in range(1, 6):
            k.tt(ch["Ls"][li][:], a2[:], C2(CMS2 + 2 * li), ALU.mult, eng=("pool", "dve")[li % 2])
        yield
        b2 = ch["B"][s]
        k.copy(b2[:], p6, eng="act")
        yield
        tt2 = ch["Tt"][s]
        ls0 = ch["Ls"][0]
        k.tt(ls0[:], b2[:], C2(CMS2), ALU.mult, eng="pool")
        yield
        k.tt(tt2[:], C2(CI2), ls0[:], ALU.subtract)
        yield
        Zs2, Ts2 = ch["Zs"][s], ch["Ts"][s]
        for li in range(1, 6):
            pa_, pb_ = P(), P()
            for d in range(2):
                k.mm(H(pa_, d), H(ch["Ls"][li], d), H(tt2, d))
            for d in range(2):
                k.mm(H(pb_, d), H(tt2, d), CB(0))
            yield
            k.copy(Zs2[:], pa_, eng=("act", "dve")[li % 2])
            k.copy(Ts2[:], pb_, eng=("dve", "act")[li % 2])
            yield
            pc_ = P()
            for d in range(2):
                k.mm(H(pc_, d), H(Ts2, d), H(Zs2, d))
            yield
            k.tt(tt2[:], tt2[:], pc_, ALU.subtract)
            yield
        pu, pw = P(), P()
        for d in range(2):
            k.mm(H(pu, d, 64), H(tt2, d), H(vb2, d, 64))
        for d in range(2):
            k.mm(H(pw, d), H(kbg2, d), H(tt2, d))
        yield
        k.copy(ch["U"][s][:], pu[:, 0:128], eng="act")
        k.copy(ch["WT"][s][:], pw, eng="act")
        yield

    def scan(ch, s):
        PS = ch["PS"]
        dd = ch["dd"]
        S2, Sb2 = ch["S2"], ch["Sb2"]
        oall = ch["oall"]
        ps1, ps2 = PS(), PS()
        for d in range(2):
            k.mm(H(ps1, d, 64), H(ch["WT"][s], d), H(Sb2, d, 64))
        for d in range(2):
            k.mm(H(ps2, d, 64), H(ch["QT"][s], d), H(Sb2, d, 64))
        yield
        vn2, tq2 = ch["vnew"][s], ch["tq"][s]
        k.tt(vn2[:], ch["U"][s][:], ps1, ALU.subtract)
        for d in range(2):
            c = chunk_of(s, d)
            k.act(H(tq2, d, 64), H(ps2, d, 64), AF.Copy, scale=dd[d]["egc"][:, c:c + 1])
        yield
        ps3, ps4 = PS(), PS()
        for d in range(2):
            k.mm(H(ps3, d, 64), H(ch["QKm"][s], d), H(vn2, d, 64))
        for d in range(2):
            k.mm(H(ps4, d, 64), H(ch["kdec"][s], d), H(vn2, d, 64))
        yield
        for d in range(2):
            c = chunk_of(s, d)
            k.stt(H(S2, d, 64), H(S2, d, 64), dd[d]["egl"][:, c:c + 1], H(ps4, d, 64), ALU.mult, ALU.add)
        if s < nch // 2:
            for d in range(2):
                c = chunk_of(s, d)
                k.tt(oall[:, c, :], H(tq2, d, 64), H(ps3, d, 64), ALU.add)
        else:
            k.tt(tq2[:], tq2[:], ps3, ALU.add)
        yield
        k.copy(Sb2[:], S2[:], eng="pool")
        if s >= nch // 2:
            for d in range(2):
                c = chunk_of(s, d)
                k.tt(oall[:, c, :], oall[:, c, :], H(tq2, d, 64), ALU.add, eng="pool")
        yield

    interleave([pre(ch, 0) for ch in chains])
    for s in range(nch):
        gens = [scan(ch, s) for ch in chains]
        if s + 1 < nch:
            gens = [pre(ch, s + 1) for ch in chains] + gens
        interleave(gens)
    P = P0
    for ch in chains:
        if not isS:
            si = (tokS - cfg.ts) // cfg.lp
            for d in range(2):
                k.dma(o_sd[si, l, d, ch["hp"], :, :], ch["S2"][:, d * 64:(d + 1) * 64], q="pool", is_out=True)
    junk = k.sb([128, 64], name="dnjunk")
    on = [k.sb([128, 64], name="dnon%d" % i) for i in range(2)]
    oTp = [k.sb([128, 512], name="dnoT%d" % i) for i in range(2)]
    gtp = [k.sb([128, 512], name="dngt%d" % i) for i in range(2)]
    o16p = [k.sb([128, 512], BF16, name="dno16%d" % i) for i in range(2)]
    for hp in range(2):
        oall = HP[hp][4]
        ssq = k.sb([128, nch], name="dnssq%d" % hp)
        k.memset(ssq[:], 0.0)
        for c in range(nch):
            k.act(junk[:], oall[:, c, :], AF.Square, accum_out=ssq[:, c:c + 1])
        k.rsqrt(ssq[:], ssq[:], 1.0 / 64.0, EPS)
        for pi, c0 in enumerate(range(0, nch, 8)):
            ncc = min(8, nch - c0)
            oT, gt, o16 = oTp[pi % 2], gtp[pi % 2], o16p[pi % 2]
            w_ = ncc * 64
            k.dma(gt[:, 0:w_], dngate[hp * 128:(hp + 1) * 128, tokS + c0 * 64:tokS + c0 * 64 + w_])
            for cc in range(ncc):
                c = c0 + cc
                o_ = on[c % 2]
                k.ts(o_[:], oall[:, c, :], ssq[:, c:c + 1], None, op0=ALU.mult)
                p = P()
                k.mm(p[0:64, 0:128], o_[:], C(CI))
                k.copy(oT[0:64, cc * 64:(cc + 1) * 64], p[0:64, 0:64], eng="act")
                k.copy(oT[64:128, cc * 64:(cc + 1) * 64], p[0:64, 64:128], eng="dve")
            k.stt(o16[:, 0:w_], oT[:, 0:w_], pcol(l, PV_DNN), gt[:, 0:w_], ALU.mult, ALU.mult)
            k.dma(mixT[256 + hp * 128:256 + (hp + 1) * 128, tokS + c0 * 64:tokS + c0 * 64 + w_], o16[:, 0:w_], q="pool")
    k.stage_end()


def ssd_seq(k, cfg, l, seq, C, CB, pcol, pr, ssmpre, smalls, ztm, sss, o_ss, mixT):
    tokS, L, ks, nk, isS = seq
    nch = L // 64
    cb = l * PR_L
    k.stage_begin()
    banks = [k.ps(name="ssP%d" % i) for i in range(8)]
    P0 = Slots(banks)
    sm = k.sb([64, nch, 24], name="sssm")
    k.dma(sm[:], smalls[tokS:tokS + L, :].rearrange("(c i) k -> i c k", i=64))
    chains = []
    GR = []
    XBC = [[k.sb([128, L], BF16, name="ss%s%d" % (n_, g)) for n_ in ("x", "B", "C")] for g in range(2)]
    YALL = [k.subkey(k.sb([64, nch, 128], name="ssyall%d" % g), nch * 128, 128) for g in range(2)]
    k.stage_begin()
    raw = k.sb([128, L + 2], name="ssraw")
    tmp = k.sb([128, L], name="sstmpc")
    for g in range(2):
        for which, dst in enumerate(XBC[g]):
            r0 = which * 256 + g * 128
            k.memset(raw[:, 0:1], 0.0)
            k.memset(raw[:, L + 1:L + 2], 0.0)
            k.dma(raw[:, 1:L + 1], ssmpre[r0:r0 + 128, tokS:tokS + L])
            ti = which * 2 + g
            conv_silu(k, dst, tmp, raw, pcol(l, PV_SCW + ti * 3, 3), L, bias=pcol(l, PV_SCB + ti))
    k.stage_end()
    for g in range(2):
        xc, Bc, Cc = XBC[g]
        yall = YALL[g]
        GR.append(yall)
        for d in range(2):
            nm = "%d%d" % (g, d)
            DT = k.sb([64, nch, 2], name="ssDT" + nm)
            DTA = k.sb([64, nch, 2], name="ssDTA" + nm)
            for hh in range(2):
                h = 2 * g + hh
                k.act(DT[:, :, hh], sm[:, :, 16 + d * 4 + h], AF.Exp,
                      bias=pr[0:64, cb + PR_SB + d * 4 + h:cb + PR_SB + d * 4 + h + 1], scale=1.0)
                k.act(DT[:, :, hh], DT[:, :, hh], AF.Ln, bias=1.0, scale=1.0)
                ea = k.sb([64, 1], name="ssea")
                k.act(ea[:], pr[0:64, cb + PR_SA + d * 4 + h:cb + PR_SA + d * 4 + h + 1], AF.Exp)
                k.ts(DTA[:, :, hh], DT[:, :, hh], ea[:], -1.0, op0=ALU.mult, op1=ALU.mult)
            AC = k.sb([64, nch, 2], name="ssAC" + nm)
            LAST = k.sb([128, nch, 2], name="ssLAST" + nm)
            for c0 in range(0, nch, 64):
                n2 = min(64, nch - c0) * 2
                p = P0()
                k.mm(p[0:64, 0:n2], C(CTRI0 + d, 64, 64), DTA[:, c0:c0 + n2 // 2, :].rearrange("p c h -> p (c h)"))
                k.copy(AC[:, c0:c0 + n2 // 2, :].rearrange("p c h -> p (c h)"), p[0:64, 0:n2], eng="act")
                p = P0()
                k.mm(p[:, 0:n2], C(CONES, 64, 128), DTA[:, c0:c0 + n2 // 2, :].rearrange("p c h -> p (c h)"))
                k.copy(LAST[:, c0:c0 + n2 // 2, :].rearrange("p c h -> p (c h)"), p[:, 0:n2], eng="act")
            EAC = k.sb([64, nch, 2], name="ssEAC" + nm)
            NAC = k.sb([64, nch, 2], name="ssNAC" + nm)
            FD = k.sb([64, nch, 2], name="ssFD" + nm)
            EL = k.sb([128, nch, 2], name="ssEL" + nm)
            k.act(EAC[:], AC[:], AF.Exp)
            k.ts(NAC[:], AC[:], -1.0, None, op0=ALU.mult)
            k.tt(FD[:], LAST[0:64, :, :], AC[:], ALU.subtract)
            k.act(FD[:], FD[:], AF.Exp)
            k.act(EL[:], LAST[:], AF.Exp)
            ST = k.sb([128, 128], name="ssST" + nm)
            STb = k.sb([128, 128], BF16, name="ssSTb" + nm)
            if isS:
                k.dma(ST[:], sss[l, d, g, :, :])
            else:
                k.memset(ST[:], 0.0)
            k.copy(STb[:], ST[:], eng="act")

            def R(name, shape=(64, 128), dt=BF16, n=2, nm=nm):
                return Ring(k, name + nm, shape, dt, n)

            chains.append(dict(g=g, d=d, xc=xc, Bc=Bc, Cc=Cc, yall=yall, DT=DT, DTA=DTA, EAC=EAC, NAC=NAC, FD=FD, EL=EL,
                               ST=ST, STb=STb, P=Slots(banks[2 * len(chains):2 * len(chains) + 2]),
                               xt=R("ssxt", dt=F32, n=1), Btm=R("ssBt"), xdt=R("ssxdt", n=1),
                               TGs=R("ssTG", dt=F32, n=1), M2=R("ssM2", dt=F32, n=1), segT=R("sssegT", dt=F32, n=1),
                               MT=R("ssMT", n=1), xdd=R("ssxdd"), ty=R("ssty", dt=F32, n=1), yi=R("ssyi", dt=F32)))

    def chunk_of(s, d):
        return s if d == 0 else nch - 1 - s

    def pre(ch, s):
        d, g = ch["d"], ch["g"]
        P = ch["P"]
        c = chunk_of(s, d)
        cs = slice(c * 64, (c + 1) * 64)
        xt, Btm, xdt, TGs, M2, segT, MT, xdd, yi = (ch[n_][s] for n_ in ("xt", "Btm", "xdt", "TGs", "M2", "segT", "MT", "xdd", "yi"))
        p1 = P(); k.mm(p1[0:64, 0:128], ch["xc"][:, cs], CB(0))
        p2 = P(); k.mm(p2[0:64, 0:128], ch["Bc"][:, cs], CB(0))
        p4 = P(); k.mm(p4[0:64, 0:64], ch["Bc"][:, cs], ch["Cc"][:, cs])
        for hh in range(2):
            hs = slice(hh * 64, hh * 64 + 64)
            k.ts(TGs[:, hs], C(CTRI0 + d, 64, 64), ch["DTA"][:, c, hh:hh + 1], None, op0=ALU.mult)
        yield
        k.copy(xt[:], p1[0:64, 0:128], eng="act")
        k.copy(Btm[:], p2[0:64, 0:128], eng="dve")
        p3 = P(); k.mm(p3[0:64, 0:128], C(CONES, 64, 64), TGs[:])
        yield
        for hh in range(2):
            hs = slice(hh * 64, hh * 64 + 64)
            k.act(xdt[:, hs], xt[:, hs], AF.Copy, scale=ch["DT"][:, c, hh:hh + 1])
            k.tt(M2[:, hs], p3[0:64, hs], C(CNM0 + d, 64, 64), ALU.add)
        yield
        for hh in range(2):
            hs = slice(hh * 64, hh * 64 + 64)
            k.act(segT[:, hs], M2[:, hs], AF.Exp, bias=ch["NAC"][:, c, hh:hh + 1], scale=1.0)
            k.act(xdd[:, hs], xdt[:, hs], AF.Copy, scale=ch["FD"][:, c, hh:hh + 1])
        yield
        for hh in range(2):
            hs = slice(hh * 64, hh * 64 + 64)
            k.tt(MT[:, hs], p4[0:64, 0:64], segT[:, hs], ALU.mult)
        yield
        p5 = P()
        for hh in range(2):
            hs = slice(hh * 64, hh * 64 + 64)
            k.mm(p5[0:64, hs], MT[:, hs], xdt[:, hs])
        yield
        k.copy(yi[:], p5[0:64, 0:128], eng="act")
        yield
        if d == 0:
            for hh in range(2):
                hs = slice(hh * 64, hh * 64 + 64)
                h = 2 * g + hh
                k.stt(yi[:, hs], xt[:, hs], pr[0:64, cb + PR_SD + h:cb + PR_SD + h + 1], yi[:, hs], ALU.mult, ALU.add)
        yield

    def scan(ch, s):
        d = ch["d"]
        P = ch["P"]
        c = chunk_of(s, d)
        cs = slice(c * 64, (c + 1) * 64)
        ST, STb, yall = ch["ST"], ch["STb"], ch["yall"]
        ty, yi = ch["ty"][s], ch["yi"][s]
        p6 = P(); k.mm(p6[0:64, 0:128], ch["Cc"][:, cs], STb[:])
        p7 = P(); k.mm(p7[:, 0:128], ch["Btm"][s][:], ch["xdd"][s][:])
        yield
        for hh in range(2):
            hs = slice(hh * 64, hh * 64 + 64)
            k.stt(ST[:, hs], ST[:, hs], ch["EL"][:, c, hh:hh + 1], p7[:, hs], ALU.mult, ALU.add)
        yield
        k.copy(STb[:], ST[:], eng="pool")
        for hh in range(2):
            hs = slice(hh * 64, hh * 64 + 64)
            k.stt(ty[:, hs], p6[0:64, hs], ch["EAC"][:, c, hh:hh + 1], yi[:, hs], ALU.mult, ALU.add)
        yield
        if s < nch // 2:
            k.copy(yall[:, c, :], ty[:], eng="pool")
        else:
            k.tt(yall[:, c, :], yall[:, c, :], ty[:], ALU.add, eng="pool")
        yield

    interleave([pre(ch, 0) for ch in chains])
    for s in range(nch):
        gens = [scan(ch, s) for ch in chains]
        if s + 1 < nch:
            gens = [pre(ch, s + 1) for ch in chains] + gens
        interleave(gens)
    P = P0
    if not isS:
        si = (tokS - cfg.ts) // cfg.lp
        for ch in chains:
            k.dma(o_ss[si, l, ch["d"], ch["g"], :, :], ch["ST"][:], q="pool", is_out=True)
    junk = k.sb([64, 128], name="ssjunk")
    yn = [k.sb([64, 128], name="ssyn%d" % i) for i in range(2)]
    ztp = [k.sb([64, 8, 128], name="sszt%d" % i) for i in range(2)]
    yTp = [k.sb([128, 512], BF16, name="ssyT%d" % i) for i in range(2)]
    for g in range(2):
        yall = GR[g]
        ssq = k.sb([64, nch], name="ssssq%d" % g)
        k.memset(ssq[:], 0.0)
        for pi, c0 in enumerate(range(0, nch, 8)):
            ncc = min(8, nch - c0)
            zt = ztp[pi % 2]
            k.dma(zt[:, 0:ncc, :], ztm[tokS + c0 * 64:tokS + (c0 + ncc) * 64, g * 128:(g + 1) * 128].rearrange("(c i) k -> i c k", i=64))
            for cc in range(ncc):
                c = c0 + cc
                k.tt(yall[:, c, :], yall[:, c, :], zt[:, cc, :], ALU.mult, eng=("dve", "pool")[c % 2])
                k.act(junk[:], yall[:, c, :], AF.Square, accum_out=ssq[:, c:c + 1])
        k.rsqrt(ssq[:], ssq[:], 1.0 / 128.0, EPS)
        for pi, c0 in enumerate(range(0, nch, 8)):
            ncc = min(8, nch - c0)
            yT = yTp[pi % 2]
            for cc in range(ncc):
                c = c0 + cc
                y_ = yn[c % 2]
                k.stt(y_[:], yall[:, c, :], ssq[:, c:c + 1], pr[0:64, cb + PR_SN + g * 128:cb + PR_SN + (g + 1) * 128], ALU.mult, ALU.mult)
                p = P()
                k.mm(p[:, 0:64], y_[:], C(CI, 64, 64))
                k.copy(yT[:, cc * 64:(cc + 1) * 64], p[:, 0:64], eng=("act", "dve")[c % 2])
            k.dma(mixT[768 + g * 128:768 + (g + 1) * 128, tokS + c0 * 64:tokS + (c0 + ncc) * 64], yT[:, 0:ncc * 64], q="pool")
    k.stage_end()


def host_inputs(cfg, inp, core):
    Dp, TS, LP, NPQ, PAST = cfg.depth, cfg.ts, cfg.lp, cfg.npq, cfg.past
    f = lambda a: np.ascontiguousarray(a, dtype=np.float32)
    b = core % inp["x_sample"].shape[0]
    ps = [core * NPQ + i for i in range(NPQ)]
    m = {}
    xT = np.concatenate([inp["x_sample"][b].T] + [inp["x_prompt"][p].T for p in ps], axis=1)
    m["xT"] = f(xT)
    m["cdk"] = f(inp["cache_diff_k"][b].reshape(Dp, PAST, 256).transpose(0, 2, 1))
    m["cdv"] = f(inp["cache_diff_v"][b].reshape(Dp, PAST, 256))
    m["sdl"] = f(inp["state_delta"][b].reshape(Dp, 2, 2, 128, 64))
    m["cckv"] = f(inp["cache_mla_ckv"][b].transpose(0, 2, 1))
    m["ckr"] = f(inp["cache_mla_krope"][b].transpose(0, 2, 1))
    m["sss"] = f(inp["state_ssm"][b].reshape(Dp, 2, 2, 2, 64, 128).transpose(0, 1, 2, 5, 3, 4).reshape(Dp, 2, 2, 128, 128))
    m["cvec"] = f(np.concatenate([inp["c"][b].reshape(8, 128).T, inp["c_ctx"].reshape(8, 128).T], axis=1))
    pv = np.zeros((128, Dp, PV_L), np.float32)
    pr = np.zeros((Dp, PR_L), np.float32)
    for l in range(Dp):
        pv[:, l, PV_BMOD:PV_BMOD + 48] = inp["b_mod"][l].reshape(48, 128).T
        pv[:, l, PV_LN1G:PV_LN1G + 8] = inp["ln1_g"][l].reshape(8, 128).T
        pv[:, l, PV_LN1B:PV_LN1B + 8] = inp["ln1_b"][l].reshape(8, 128).T
        pv[:, l, PV_LN2G:PV_LN2G + 8] = inp["ln2_g"][l].reshape(8, 128).T
        pv[:, l, PV_LN2B:PV_LN2B + 8] = inp["ln2_b"][l].reshape(8, 128).T
        pv[:, l, PV_DIFFN] = np.tile(inp["diff_norm"][l], 2)
        pv[:, l, PV_DNCONV:PV_DNCONV + 18] = inp["dn_conv"][l].reshape(3, 6, 128).transpose(2, 1, 0).reshape(128, 18)
        pv[:, l, PV_DNN] = np.tile(inp["dn_norm"][l], 2)
        pv[:, l, PV_QN:PV_QN + 2] = inp["mla_q_norm"][l].reshape(2, 128).T
        pv[:, l, PV_KVN] = inp["mla_kv_norm"][l]
        pv[:, l, PV_SCW:PV_SCW + 18] = inp["ssm_conv_w"][l].reshape(3, 6, 128).transpose(2, 1, 0).reshape(128, 18)
        pv[:, l, PV_SCB:PV_SCB + 6] = inp["ssm_conv_b"][l].reshape(6, 128).T
        pr[l, PR_LAM:PR_LAM + 128] = inp["diff_lam"][l].reshape(128)
        pr[l, PR_DNA:PR_DNA + 8] = inp["dn_a_log"][l].reshape(8)
        pr[l, PR_DNB:PR_DNB + 8] = inp["dn_dt_bias"][l].reshape(8)
        pr[l, PR_SA:PR_SA + 8] = inp["ssm_a_log"][l].reshape(8)
        pr[l, PR_SB:PR_SB + 8] = inp["ssm_dt_bias"][l].reshape(8)
        pr[l, PR_SD:PR_SD + 4] = inp["ssm_d"][l]
        pr[l, PR_SN:PR_SN + 256] = inp["ssm_norm"][l]
    m["pvec"] = f(pv.reshape(128, Dp * PV_L))
    m["prow"] = f(pr.reshape(1, Dp * PR_L))
    return m


_SHARED = {}


def shared_inputs(cfg, inp):
    f = lambda a: np.ascontiguousarray(a, dtype=np.float32)
    m = {}
    cc = make_consts()
    m["cst"] = np.ascontiguousarray(cc[:, :NCST_BASE * 128])
    m["cst2"] = np.ascontiguousarray(cc[:, NCST_BASE * 128:])
    cos, sin = rope_tables(cfg.ts)
    rc = np.ones((128, cfg.ts), np.float32)
    rs = np.zeros((128, cfg.ts), np.float32)
    for r in range(128):
        rc[r] = cos[:, r % 32]
        rs[r] = sin[:, r % 32]
    m["ropec"] = rc
    m["ropes"] = rs
    m["w_mod"] = f(inp["w_mod"])
    m["w_in"] = f(np.stack([perm_w_in(np.asarray(inp["w_in"][l])) for l in range(cfg.depth)]))
    m["w_uq"] = f(inp["mla_w_uq"])
    m["w_uk"] = f(inp["mla_w_uk"])
    m["w_uv"] = f(inp["mla_w_uv"])
    m["w_out"] = f(inp["w_out"])
    m["w_ff1"] = f(inp["w_ff1"])
    m["w_ff2"] = f(inp["w_ff2"])
    return m


def run(cfg, inp, ncores=8, debug=False):
    inp = {k_: np.asarray(v) for k_, v in inp.items()}
    nc = build(cfg, debug=debug)
    sh = shared_inputs(cfg, inp)
    in_maps = []
    for c in range(ncores):
        m = dict(sh)
        m.update(host_inputs(cfg, inp, c))
        in_maps.append(m)
    res = run_bass_kernel_spmd(nc, in_maps, core_ids=list(range(ncores)))
    return res.results


def assemble(cfg, inp, results):
    Dp, TS, LP, NPQ = cfg.depth, cfg.ts, cfg.lp, cfg.npq
    nb = inp["x_sample"].shape[0]
    B = inp["x_prompt"].shape[0]
    y_s = np.stack([results[b]["yT"][:, :TS].T for b in range(nb)])
    y_p = np.zeros((B, LP, D), np.float32)
    ndk = np.zeros((B, Dp, LP, 4, 2, 32), np.float32)
    ndv = np.zeros((B, Dp, LP, 4, 64), np.float32)
    nsd = np.zeros((B, Dp, 2, 4, 64, 64), np.float32)
    nck = np.zeros((B, Dp, LP, 128), np.float32)
    nkr = np.zeros((B, Dp, LP, 32), np.float32)
    nss = np.zeros((B, Dp, 2, 4, 64, 128), np.float32)
    for c in range(len(results)):
        r = results[c]
        for i in range(NPQ):
            p = c * NPQ + i
            if p >= B:
                continue
            y_p[p] = r["yT"][:, TS + i * LP:TS + (i + 1) * LP].T
            ndk[p] = r["o_dk"][i].transpose(0, 2, 1).reshape(Dp, LP, 4, 2, 32)
            ndv[p] = r["o_dv"][i].reshape(Dp, LP, 4, 64)
            nsd[p] = r["o_sd"][i].reshape(Dp, 2, 4, 64, 64)
            nck[p] = r["o_ckv"][i].transpose(0, 2, 1)
            nkr[p] = r["o_kr"][i].transpose(0, 2, 1)
            nss[p] = r["o_ss"][i].reshape(Dp, 2, 2, 128, 2, 64).transpose(0, 1, 2, 4, 5, 3).reshape(Dp, 2, 4, 64, 128)
    return (y_p, y_s.astype(np.float32), ndk, ndv, nsd, nck, nkr, nss)


def kernel(**inputs):
    cfg = Cfg(depth=4, ts=4096, lp=256, npq=2, past=256)
    inp = {k_: np.asarray(v) for k_, v in inputs.items()}
    results = run(cfg, inp, ncores=8)
    return assemble(cfg, inp, results)
```

```python
import math
from contextlib import ExitStack
import numpy as np
import ml_dtypes
import concourse.bass as bass
import concourse.mybir as mybir
from concourse.bass_utils import run_bass_kernel_spmd

F32 = mybir.dt.float32
BF16 = mybir.dt.bfloat16
AF = mybir.ActivationFunctionType
ALU = mybir.AluOpType
AX = mybir.AxisListType

ENGS = ("pe", "act", "dve", "pool", "sp")
NDMASEM = 16
EPS = 1e-6
PE_FILL = True
D = 1024
DFF = 4096
BIG = 1.0e4


class Sched:
    def __init__(self, nc, es):
        self.nc = nc
        self.ops = []
        self.last_w = {}
        self.readers = {}
        self.cnt_sem = {e: es.enter_context(nc.semaphore("c_" + e)) for e in ENGS}
        self.dma_sems = {e: [es.enter_context(nc.semaphore("d_%s%d" % (e, i))) for i in range(NDMASEM)]
                         for e in ("sp", "pool")}
        self.dma_n = {e: 0 for e in self.dma_sems}
        self.dma_semcnt = {e: [0] * NDMASEM for e in self.dma_sems}
        self.dma_last = {e: [None] * NDMASEM for e in self.dma_sems}
        self.sig_cnt = {e: 0 for e in ENGS}
        self.last_op = {e: None for e in ENGS}
        self.out_dma_events = []

    def add(self, eng, emit, reads=(), writes=(), dma=False, is_out=False, extra_deps=()):
        idx = len(self.ops)
        deps = set(extra_deps)
        for r in reads:
            if r in self.last_w:
                deps.add(self.last_w[r])
        for w in writes:
            if w in self.last_w:
                deps.add(self.last_w[w])
            for rd in self.readers.get(w, ()):
                deps.add(rd)
        op = dict(eng=eng, emit=emit, deps=deps, dma=dma, pre=None)
        if dma:
            n = self.dma_n[eng]
            slot = n % NDMASEM
            self.dma_n[eng] = n + 1
            prev = self.dma_semcnt[eng][slot]
            self.dma_semcnt[eng][slot] = prev + 16
            op["event"] = (self.dma_sems[eng][slot], prev + 16)
            op["pre"] = (self.dma_sems[eng][slot], prev) if prev > 0 else None
            self.dma_last[eng][slot] = idx
            if is_out:
                self.out_dma_events.append(op["event"])
        else:
            self.sig_cnt[eng] += 1
            op["event"] = (self.cnt_sem[eng], self.sig_cnt[eng])
            self.last_op[eng] = idx
        self.ops.append(op)
        for w in writes:
            self.last_w[w] = idx
            self.readers[w] = []
        for r in reads:
            if r not in writes:
                self.readers.setdefault(r, []).append(idx)
        return idx

    def barrier(self):
        deps = [v for v in self.last_op.values() if v is not None]
        for e in self.dma_last:
            deps += [v for v in self.dma_last[e] if v is not None]
        for e in ENGS:
            self.add(e, lambda eng: eng.nop(), extra_deps=deps)
        self.last_w = {}
        self.readers = {}

    def emit_all(self):
        nc = self.nc
        per_eng = {e: [] for e in ENGS}
        waited = {e: {} for e in ENGS}
        for op in self.ops:
            e = op["eng"]
            need = {}
            if op["pre"] is not None:
                need[op["pre"][0]] = op["pre"][1]
            for d in op["deps"]:
                p = self.ops[d]
                if p["eng"] == "pe" and e == "pe" and not p["dma"]:
                    continue
                s, v = p["event"]
                if need.get(s, 0) < v:
                    need[s] = v
            ws = []
            for s, v in need.items():
                if waited[e].get(s, 0) >= v:
                    continue
                waited[e][s] = v
                ws.append((s, v))
            op["waits"] = ws
            per_eng[e].append(op)
        finals = {}
        for s, v in self.out_dma_events:
            finals[s] = max(finals.get(s, 0), v)

        with nc.Block() as block:
            def run(engname):
                def body(eng):
                    for op in per_eng[engname]:
                        for s, v in op["waits"]:
                            eng.wait_ge(s, v)
                        ins = op["emit"](eng)
                        s, v = op["event"]
                        ins.then_inc(s, 16 if op["dma"] else 1)
                    if engname == "sp":
                        for s, v in finals.items():
                            eng.wait_ge(s, v)
                return body
            block.tensor(run("pe"))
            block.scalar(run("act"))
            block.vector(run("dve"))
            block.gpsimd(run("pool"))
            block.sync(run("sp"))


SUBKEY = {}


def _keys(ap):
    if isinstance(ap, (str, tuple)):
        return (ap,)
    nm = ap.name
    sk = SUBKEY.get(nm)
    if sk is None:
        return (nm,)
    row, slot = sk
    first = ap.offset % row
    ext = 1
    for (st, cnt) in ap.ap[1:]:
        ext += (cnt - 1) * st
    last = min(first + ext - 1, row - 1)
    return tuple((nm, i) for i in range(first // slot, last // slot + 1))


def _key(ap):
    return _keys(ap)[0]


class K:
    def __init__(self, nc):
        self.nc = nc
        self.es = ExitStack()
        self.s = Sched(nc, self.es)
        self.ntile = 0
        self.stack = [self.es]
        self.rr = 0
        self.dram_names = set()
        self.psum_names = set()
        self.ndw = 0

    def sb(self, shape, dt=F32, name=None):
        self.ntile += 1
        return self.stack[-1].enter_context(self.nc.sbuf_tensor("%s_%d" % (name or "t", self.ntile), list(shape), dt))

    def ps(self, shape=(128, 512), dt=F32, name=None):
        self.ntile += 1
        nm = "%s_%d" % (name or "p", self.ntile)
        self.psum_names.add(nm)
        return self.stack[-1].enter_context(self.nc.psum_tensor(nm, list(shape), dt))

    def fix(self, eng, aps):
        if eng == "pool" and any((not isinstance(a, (float, int))) and a.name in self.psum_names for a in aps):
            return "dve"
        return eng

    def stage_begin(self):
        self.stack.append(ExitStack())

    def subkey(self, tile, row_stride, slot):
        SUBKEY[tile.name] = (row_stride, slot)
        return tile

    def sbk(self, shape, dt=F32, name=None):
        t = self.sb(shape, dt, name)
        row = 1
        for d_ in shape[1:]:
            row *= d_
        return self.subkey(t, row, row // shape[1])

    def stage_end(self):
        self.s.barrier()
        self.stack.pop().close()

    def op(self, eng, fn, r=(), w=()):
        w = list(w) + [x for x in r if (not isinstance(x, (str, float, int))) and x.name in self.psum_names]
        rk = tuple(kk for x in r for kk in _keys(x))
        wk = tuple(kk for x in w for kk in _keys(x))
        return self.s.add(eng, fn, reads=rk, writes=wk)

    def dma(self, out, in_, q="sp", is_out=False, rk=None, wk=None, **kw):
        r = (rk,) if rk is not None else _keys(in_)
        w = (wk,) if wk is not None else _keys(out)
        if w[0] in self.dram_names:
            self.ndw += 1
            w = ("__dw%d" % self.ndw,)
        return self.s.add(q, lambda e: e.dma_start(out=out, in_=in_, **kw), reads=r, writes=w, dma=True, is_out=is_out)

    def mm(self, out, lhsT, rhs, start=True, stop=True):
        w = (out,)
        r = (lhsT, rhs) if start else (lhsT, rhs, out)
        return self.op("pe", lambda e: e.matmul(out, lhsT=lhsT, rhs=rhs, start=start, stop=stop), r, w)

    def act(self, out, in_, func, bias=None, scale=None, accum_out=None):
        kw = {}
        r = [in_]
        w = [out]
        if bias is not None:
            kw["bias"] = bias
            if not isinstance(bias, float):
                r.append(bias)
        if scale is not None:
            kw["scale"] = scale
            if not isinstance(scale, float):
                r.append(scale)
        if accum_out is not None:
            kw["accum_out"] = accum_out
            w.append(accum_out)
        return self.op("act", lambda e: e.activation(out=out, in_=in_, func=func, **kw), r, w)

    def tt(self, out, in0, in1, op, eng="dve"):
        eng = self.fix(eng, (out, in0, in1))
        return self.op(eng, lambda e: e.tensor_tensor(out=out, in0=in0, in1=in1, op=op), (in0, in1), (out,))

    def ts(self, out, in0, s1, s2=None, op0=ALU.mult, op1=None, eng="dve"):
        eng = self.fix(eng, (out, in0))
        r = [in0] + [s for s in (s1, s2) if s is not None and not isinstance(s, (float, int))]
        if op1 is None:
            return self.op(eng, lambda e: e.tensor_scalar(out=out, in0=in0, scalar1=s1, scalar2=None, op0=op0), r, (out,))
        return self.op(eng, lambda e: e.tensor_scalar(out=out, in0=in0, scalar1=s1, scalar2=s2, op0=op0, op1=op1), r, (out,))

    def stt(self, out, in0, scalar, in1, op0, op1, eng="dve"):
        eng = "dve"
        r = [in0, in1] + ([] if isinstance(scalar, (float, int)) else [scalar])
        return self.op(eng, lambda e: e.scalar_tensor_tensor(out=out, in0=in0, scalar=scalar, in1=in1, op0=op0, op1=op1), r, (out,))

    def copy(self, out, in_, eng="dve"):
        if eng == "act":
            return self.act(out, in_, AF.Copy)
        eng = self.fix(eng, (out, in_))
        return self.op(eng, lambda e: e.tensor_copy(out=out, in_=in_), (in_,), (out,))

    def anycopy(self, out, in_):
        self.rr += 1
        return self.copy(out, in_, eng=("dve", "pool")[self.rr % 2])

    def memset(self, ap, val, eng="pool"):
        return self.op(eng, lambda e: e.memset(ap, val), (), (ap,))

    def recip(self, out, in_):
        return self.op("dve", lambda e: e.reciprocal(out=out, in_=in_), (in_,), (out,))

    def rsqrt(self, out, in_, scale, eps):
        self.act(out, in_, AF.Ln, bias=eps, scale=scale)
        self.act(out, out, AF.Exp, scale=-0.5)

    def arecip(self, out, in_):
        self.act(out, in_, AF.Ln)
        self.act(out, out, AF.Exp, scale=-1.0)

    def finish(self):
        self.s.emit_all()
        self.es.close()


O_AQ, O_AK, O_AV, O_BQKV, O_BBETA, O_BDEC, O_BGATE, O_CQ, O_CKV, O_CKR, O_DZ, O_DXBC, O_DDT = (
    0, 256, 512, 768, 1536, 1544, 1552, 1808, 2064, 2192, 2224, 2480, 3248)
FM_TILES = [("aq0", O_AQ, 128), ("aq1", O_AQ + 128, 128), ("ak0", O_AK, 128), ("ak1", O_AK + 128, 128),
            ("bq0", O_BQKV, 128), ("bq1", O_BQKV + 128, 128), ("bk0", O_BQKV + 256, 128), ("bk1", O_BQKV + 384, 128),
            ("bv0", O_BQKV + 512, 128), ("bv1", O_BQKV + 640, 128), ("bg0", O_BGATE, 128), ("bg1", O_BGATE + 128, 128),
            ("cq0", O_CQ, 128), ("cq1", O_CQ + 128, 128), ("ckv", O_CKV, 128),
            ("dx0", O_DXBC, 128), ("dx1", O_DXBC + 128, 128), ("dB0", O_DXBC + 256, 128), ("dB1", O_DXBC + 384, 128),
            ("dC0", O_DXBC + 512, 128), ("dC1", O_DXBC + 640, 128)]
NFM = len(FM_TILES)
C_KR = NFM * 128
C_TM1 = C_KR + 96
C_TM2 = C_TM1 + 280
WIN_COLS = C_TM2 + 256


def perm_w_in(w):
    out = np.zeros((w.shape[0], WIN_COLS), np.float32)
    for i, (_, o, n) in enumerate(FM_TILES):
        out[:, i * 128:i * 128 + n] = w[:, o:o + n]
    out[:, C_KR + 64:C_KR + 96] = w[:, O_CKR:O_CKR + 32]
    out[:, C_TM1:C_TM1 + 256] = w[:, O_AV:O_AV + 256]
    out[:, C_TM1 + 256:C_TM1 + 264] = w[:, O_BBETA:O_BBETA + 8]
    out[:, C_TM1 + 264:C_TM1 + 272] = w[:, O_BDEC:O_BDEC + 8]
    out[:, C_TM1 + 272:C_TM1 + 280] = w[:, O_DDT:O_DDT + 8]
    out[:, C_TM2:C_TM2 + 256] = w[:, O_DZ:O_DZ + 256]
    return out


PV_BMOD, PV_LN1G, PV_LN1B, PV_LN2G, PV_LN2B, PV_DIFFN, PV_DNCONV, PV_DNN, PV_QN, PV_KVN, PV_SCW, PV_SCB = (
    0, 48, 56, 64, 72, 80, 81, 99, 100, 102, 103, 121)
PV_L = 127
PR_LAM, PR_DNA, PR_DNB, PR_SA, PR_SB, PR_SD, PR_SN = 0, 128, 136, 144, 152, 160, 164
PR_L = 164 + 256
(CI, CONESBD, CONES, CR128, CR96, CE2, CTRI0, CTRI1, CMB0, CMB1, CNM0, CNM1, CST0, CST1) = range(14)
CMS = 14
CMS2 = 20
CI2 = 32
NCST = 34
NCST_BASE = 20


def make_consts():
    c = np.zeros((NCST, 128, 128), np.float32)
    c[CI] = np.eye(128)
    bd = np.zeros((128, 128), np.float32)
    bd[:64, :64] = 1
    bd[64:, 64:] = 1
    c[CONESBD] = bd
    c[CONES] = 1
    r32 = np.zeros((32, 32), np.float32)
    for dp in range(32):
        q = dp // 8
        if q == 0:
            r32[dp + 8, dp] = -1
        elif q == 1:
            r32[dp - 8, dp] = 1
        elif q == 2:
            r32[dp + 8, dp] = -1
        else:
            r32[dp - 8, dp] = 1
    for b in range(4):
        c[CR128, b * 32:(b + 1) * 32, b * 32:(b + 1) * 32] = r32
    c[CR96, 64:96, 64:96] = r32
    c[CE2, :64, :64] = np.eye(64)
    c[CE2, 64:, :64] = np.eye(64)
    idx = np.arange(64)
    for d in range(2):
        tri = (idx[:, None] <= idx[None, :]) if d == 0 else (idx[:, None] >= idx[None, :])
        t = np.zeros((128, 128), np.float32)
        t[:64, :64] = tri
        t[64:, 64:] = tri
        c[CTRI0 + d] = t
        valid = (idx[:, None] >= idx[None, :]) if d == 0 else (idx[:, None] <= idx[None, :])
        strict_v = (idx[:, None] > idx[None, :]) if d == 0 else (idx[:, None] < idx[None, :])
        mb = np.full((128, 128), BIG, np.float32)
        mb[:64, :64] = np.where(strict_v, 0, BIG)
        mb[64:, 64:] = np.where(strict_v, 0, BIG)
        c[CMB0 + d] = mb
        nm = np.full((128, 128), -BIG, np.float32)
        nm[:64, :64] = np.where(valid.T, 0, -BIG)
        nm[64:, 64:] = np.where(valid.T, 0, -BIG)
        c[CNM0 + d] = nm
        strict = (idx[:, None] > idx[None, :]) if d == 0 else (idx[:, None] < idx[None, :])
        st = np.zeros((128, 128), np.float32)
        st[:64, :64] = strict
        st[64:, 64:] = strict
        c[CST0 + d] = st
    for li in range(6):
        sz = 1 << li
        m64 = ((idx[:, None] // (2 * sz) == idx[None, :] // (2 * sz)) & (idx[:, None] // sz != idx[None, :] // sz))
        c[CMS + li, :64, :64] = m64
        c[CMS + li, 64:, 64:] = m64
    for li in range(6):
        c[CMS2 + 2 * li] = c[CMS + li]
        c[CMS2 + 2 * li + 1] = c[CMS + li]
    c[CI2] = c[CI]
    c[CI2 + 1] = c[CI]
    return np.ascontiguousarray(c.transpose(1, 0, 2).reshape(128, NCST * 128))


def rope_tables(length, grid_w=64, base=10000.0):
    rows = length // grid_w
    row_pos = np.repeat(np.arange(rows, dtype=np.float32), grid_w)
    col_pos = np.tile(np.arange(grid_w, dtype=np.float32), rows)
    half = 16
    inv_freq = (base ** (-np.arange(0, half, 2, dtype=np.float32) / half)).astype(np.float32)
    ang_r = row_pos[:, None] * inv_freq
    ang_c = col_pos[:, None] * inv_freq
    ang = np.concatenate([ang_r, ang_r, ang_c, ang_c], axis=-1)
    return np.cos(ang).astype(np.float32), np.sin(ang).astype(np.float32)


class Cfg:
    def __init__(self, depth=4, ts=4096, lp=256, npq=2, past=256):
        self.depth, self.ts, self.lp, self.npq, self.past = depth, ts, lp, npq, past
        self.tp = lp * npq
        self.tt = ts + self.tp
        assert ts % 512 == 0 and self.tp % 256 == 0 and lp % 64 == 0 and past % 128 == 0 and lp % 128 == 0
        self.kk = ts + past + self.tp
        self.seqs = [(0, ts, 0, ts + past, True)]
        for i in range(npq):
            self.seqs.append((ts + i * lp, lp, ts + past + i * lp, lp, False))

    def kcol(self, tok):
        return tok if tok < self.ts else tok + self.past


def build(cfg, debug=False):
    nc = bass.Bass("TRN2", target_bir_lowering=False)
    SUBKEY.clear()
    k = K(nc)
    Dp, TS, LP, NPQ, PAST, TT, KKC = cfg.depth, cfg.ts, cfg.lp, cfg.npq, cfg.past, cfg.tt, cfg.kk

    def din(name, shape, dt=F32):
        return nc.dram_tensor(name, list(shape), dt, kind="ExternalInput").ap()

    def dout(name, shape, dt=F32):
        k.dram_names.add(name)
        return nc.dram_tensor(name, list(shape), dt, kind="ExternalOutput").ap()

    def dscr(name, shape, dt=F32):
        k.dram_names.add(name)
        return nc.dram_tensor(name, list(shape), dt, kind="Internal").ap()

    xT_in = din("xT", [D, TT])
    cdk = din("cdk", [Dp, 256, PAST])
    cdv = din("cdv", [Dp, PAST, 256])
    sdl = din("sdl", [Dp, 2, 2, 128, 64])
    cckv = din("cckv", [Dp, 128, PAST])
    ckr = din("ckr", [Dp, 32, PAST])
    sss = din("sss", [Dp, 2, 2, 128, 128])
    cvec = din("cvec", [128, 16])
    pvec = din("pvec", [128, Dp * PV_L])
    prow = din("prow", [1, Dp * PR_L])
    cst_d = din("cst", [128, NCST_BASE * 128])
    cst2_d = din("cst2", [128, (NCST - NCST_BASE) * 128])
    ropec = din("ropec", [128, TS])
    ropes = din("ropes", [128, TS])
    w_mod = din("w_mod", [Dp, D, 6 * D])
    w_in = din("w_in", [Dp, D, WIN_COLS])
    w_uq = din("w_uq", [Dp, 256, 384])
    w_uk = din("w_uk", [Dp, 128, 256])
    w_uv = din("w_uv", [Dp, 128, 256])
    w_out = din("w_out", [Dp, D, D])
    w_ff1 = din("w_ff1", [Dp, D, DFF])
    w_ff2 = din("w_ff2", [Dp, DFF, D])
    yT = dout("yT", [D, TT])
    o_dk = dout("o_dk", [NPQ, Dp, 256, LP])
    o_dv = dout("o_dv", [NPQ, Dp, LP, 256])
    o_sd = dout("o_sd", [NPQ, Dp, 2, 2, 128, 64])
    o_ckv = dout("o_ckv", [NPQ, Dp, 128, LP])
    o_kr = dout("o_kr", [NPQ, Dp, 32, LP])
    o_ss = dout("o_ss", [NPQ, Dp, 2, 2, 128, 128])
    xs = dscr("xs", [D, TT])
    qd = dscr("qd", [256, TT], BF16)
    kd = dscr("kd", [256, KKC], BF16)
    vd = dscr("vd", [KKC, 4, 128], BF16)
    dnpre = dscr("dnpre", [768, TT])
    dngate = dscr("dngate", [256, TT])
    qm = dscr("qm", [4, 96, TT], BF16)
    ckvs = dscr("ckvs", [128, KKC], BF16)
    kr96 = dscr("kr96", [96, KKC], BF16)
    ssmpre = dscr("ssmpre", [768, TT])
    smalls = dscr("smalls", [TT, 24])
    ztm = dscr("ztm", [TT, 256])
    mixT = dscr("mixT", [D, TT], BF16)
    dbg = {}
    if debug:
        dbg["mixT"] = dout("dbg_mixT", [D, TT], BF16)

    cst = k.sb([128, NCST_BASE * 128], name="cst")
    k.dma(cst[:], cst_d[:, :])

    def C(i, rows=128, cols=128, r0=0):
        return cst[r0:r0 + rows, i * 128:i * 128 + cols]

    cstb = k.sb([128, 3 * 128], BF16, name="cstb")
    k.copy(cstb[:, 0:128], C(CI))
    k.copy(cstb[:, 128:256], C(CONES))
    k.copy(cstb[:, 256:384], C(CE2))

    def CB(i, rows=128, cols=128):
        return cstb[0:rows, i * 128:i * 128 + cols]
    pv = k.sb([128, Dp * PV_L], name="pv")
    k.dma(pv[:], pvec[:, :])
    pr = k.sb([128, Dp * PR_L], name="pr")
    k.dma(pr[:], prow[0:1, :].partition_broadcast(128))
    cv = k.sb([128, 16], name="cv")
    k.dma(cv[:], cvec[:, :])
    scv = k.sb([128, 16], name="scv")
    k.act(scv[:], cv[:], AF.Silu)
    modt = k.sb([128, Dp, 48, 2], name="modt")
    onep = k.sb([128, Dp, 48, 2], name="onep")
    neglam = k.sb([128, Dp], name="neglam")

    k.stage_begin()
    wmb = [k.sb([128, 8, 768], name="wmb%d" % i) for i in range(2)]
    pmod = [k.ps([128, 512], name="pmod%d" % i) for i in range(2)]
    it = 0
    for l in range(Dp):
        for g in range(8):
            wb = wmb[it % 2]
            pm = pmod[it % 2]
            it += 1
            k.dma(wb[:], w_mod[l, :, g * 768:(g + 1) * 768].rearrange("(j p) c -> p j c", p=128))
            for cc in range(6):
                for j in range(8):
                    k.mm(pm[:, cc * 2:cc * 2 + 2], wb[:, j, cc * 128:(cc + 1) * 128], scv[:, j:j + 9:8],
                         start=(j == 0), stop=(j == 7))
            for cc in range(6):
                ch = g * 6 + cc
                k.ts(modt[:, l, ch, :], pm[:, cc * 2:cc * 2 + 2], pv[:, l * PV_L + PV_BMOD + ch:l * PV_L + PV_BMOD + ch + 1],
                     None, op0=ALU.add)
        lt = k.sb([128, 64], name="lt%d" % l)
        lsum = k.sb([128, 2], name="lsum%d" % l)
        lb = l * PR_L + PR_LAM
        k.tt(lt[:, 0:32], pr[:, lb:lb + 32], pr[:, lb + 32:lb + 64], ALU.mult)
        k.tt(lt[:, 32:64], pr[:, lb + 64:lb + 96], pr[:, lb + 96:lb + 128], ALU.mult)
        k.op("dve", lambda e, lt=lt, lsum=lsum: e.reduce_sum(out=lsum[:, 0:1], in_=lt[:, 0:32], axis=AX.X), (lt,), (lsum,))
        k.op("dve", lambda e, lt=lt, lsum=lsum: e.reduce_sum(out=lsum[:, 1:2], in_=lt[:, 32:64], axis=AX.X), (lt, lsum), (lsum,))
        k.act(lsum[:], lsum[:], AF.Exp)
        lam_init = 0.8 - 0.6 * math.exp(-0.3 * l)
        k.tt(neglam[:, l:l + 1], lsum[:, 1:2], lsum[:, 0:1], ALU.subtract)
        k.ts(neglam[:, l:l + 1], neglam[:, l:l + 1], -lam_init, None, op0=ALU.add)
    k.ts(onep[:], modt[:], 1.0, None, op0=ALU.add)
    ALPHA0 = (2 * Dp) ** 0.25
    for l in range(Dp):
        for w_ in (2, 5):
            k.ts(modt[:, l, w_ * 8:(w_ + 1) * 8, :], modt[:, l, w_ * 8:(w_ + 1) * 8, :], 1.0 / ALPHA0, None, op0=ALU.mult)
    k.stage_end()

    def modcol(l, which, j, isS):
        t = onep if which in (1, 4) else modt
        return t[:, l, which * 8 + j, (0 if isS else 1):(1 if isS else 2)]

    def pcol(l, off, n=1):
        return pv[:, l * PV_L + off:l * PV_L + off + n]

    nblk = TT // 512 if TT % 512 == 0 else None
    blocks = []
    t0 = 0
    while t0 < TS:
        blocks.append((t0, 512, True))
        t0 += 512
    while t0 < TT:
        n = min(512, TT - t0)
        blocks.append((t0, n, False))
        t0 += n

    def load_w_bf16(dst, src_ap, ncols, stg=None):
        J = dst.shape[1]
        for j in range(J):
            c0 = 0
            while c0 < ncols:
                n = min(2048, ncols - c0)
                k.dma(dst[:, j, c0:c0 + n], src_ap[j * 128:(j + 1) * 128, c0:c0 + n], q="pool")
                c0 += n

    for l in range(Dp):
        xsrc = xT_in if l == 0 else xs
        xdst = yT if l == Dp - 1 else xs
        k.stage_begin()
        winb = k.sbk([128, 8, WIN_COLS], BF16, name="winb")
        load_w_bf16(winb, w_in[l], WIN_COLS)
        wuqb = k.sb([128, 2, 384], BF16, name="wuqb")
        load_w_bf16(wuqb, w_uq[l], 384)
        rcb = [k.sb([128, 512], name="rc%d" % i) for i in range(2)]
        rsb = [k.sb([128, 512], name="rs%d" % i) for i in range(2)]
        xb = [k.sbk([128, 8, 512], name="xb%d" % i) for i in range(2)]
        hb = [k.sbk([128, 8, 512], BF16, name="hb%d" % i) for i in range(2)]
        pA = [k.ps(name="pA%d" % i) for i in range(4)]
        pR = k.ps(name="pR")
        pS = k.ps(name="pS")
        pT = [k.ps(name="pT%d" % i) for i in range(2)]
        ev = [k.sb([128, 512], name="ev%d" % i) for i in range(4)]
        evb = [k.sb([128, 512], BF16, name="evb%d" % i) for i in range(4)]
        t1 = k.sb([128, 512], name="t1")
        sqt = [k.sb([128, 512], name="sq%d" % i) for i in range(2)]
        cqk = [k.sb([128, 512], name="cqk%d" % i) for i in range(2)]
        cqn = k.sb([128, 2, 512], BF16, name="cqn")
        rstd = k.sb([128, 512], name="rstd")
        vtm = [k.sb([128, 280], name="vtm%d" % i) for i in range(2)]
        vaug = [k.sb([128, 4, 128], BF16, name="vaug%d" % i) for i in range(2)]
        for v in vaug:
            k.memset(v[:], 1.0)
        ztt = [k.sb([128, 256], name="ztt%d" % i) for i in range(2)]
        nev = [0]

        def next_ev():
            nev[0] += 1
            return ev[nev[0] % 4], evb[nev[0] % 4]

        def rope(ps_in, rows, Rc, tok0, n, e32):
            lo = 0 if rows == 128 else 64
            k.copy(e32[0:rows, 0:n], ps_in, eng="act")
            k.mm(pR[0:rows, 0:n], C(Rc, rows, rows), e32[0:rows, 0:n])
            k.tt(t1[lo:rows, 0:n], pR[lo:rows, 0:n], rs_[lo:rows, 0:n], ALU.mult)
            k.tt(e32[lo:rows, 0:n], e32[lo:rows, 0:n], rc[lo:rows, 0:n], ALU.mult, eng="pool")
            k.tt(e32[lo:rows, 0:n], e32[lo:rows, 0:n], t1[lo:rows, 0:n], ALU.add)

        for bi, (tok0, n, isS) in enumerate(blocks):
            x_ = xb[bi % 2]
            h_ = hb[bi % 2]
            kc0 = cfg.kcol(tok0)
            k.dma(x_[:, :, 0:n], xsrc[:, tok0:tok0 + n].rearrange("(j p) t -> p j t", p=128))
            rc, rs_ = rcb[bi % 2], rsb[bi % 2]
            if isS:
                k.dma(rc[:, 0:n], ropec[:, tok0:tok0 + n])
                k.dma(rs_[:, 0:n], ropes[:, tok0:tok0 + n])
            for j in range(8):
                k.act(h_[:, j, 0:n], x_[:, j, 0:n], AF.Identity, bias=modcol(l, 0, j, isS), scale=modcol(l, 1, j, isS))
            pseqs = []
            if not isS:
                for si in range(NPQ):
                    s0 = TS + si * LP
                    lo, hi = max(s0, tok0), min(s0 + LP, tok0 + n)
                    if lo < hi:
                        pseqs.append((si, lo - s0, lo - tok0, hi - lo))
            for ti, (nm, _, _) in enumerate(FM_TILES):
                pa = pA[ti % 4]
                for j in range(8):
                    k.mm(pa[:, 0:n], winb[:, j, ti * 128:(ti + 1) * 128], h_[:, j, 0:n], start=(j == 0), stop=(j == 7))
                e32, e16 = next_ev()
                if nm in ("aq0", "aq1", "ak0", "ak1"):
                    hp = int(nm[2])
                    if isS:
                        rope(pa[:, 0:n], 128, CR128, tok0, n, e32)
                    else:
                        k.copy(e32[:, 0:n], pa[:, 0:n], eng="act")
                        if nm[1] == "k":
                            for (si, so, bo, nn) in pseqs:
                                k.dma(o_dk[si, l, hp * 128:(hp + 1) * 128, so:so + nn], e32[:, bo:bo + nn], q="pool", is_out=True)
                    k.anycopy(e16[:, 0:n], e32[:, 0:n])
                    if nm[1] == "q":
                        k.dma(qd[hp * 128:(hp + 1) * 128, tok0:tok0 + n], e16[:, 0:n], q="pool")
                    else:
                        k.dma(kd[hp * 128:(hp + 1) * 128, kc0:kc0 + n], e16[:, 0:n], q="pool")
                elif nm[0] == "b" and nm[1] in "qkv":
                    r0 = {"q": 0, "k": 256, "v": 512}[nm[1]] + int(nm[2]) * 128
                    k.copy(e32[:, 0:n], pa[:, 0:n], eng="act")
                    k.dma(dnpre[r0:r0 + 128, tok0:tok0 + n], e32[:, 0:n], q="pool")
                elif nm[0:2] == "bg":
                    r0 = int(nm[2]) * 128
                    k.act(e32[:, 0:n], pa[:, 0:n], AF.Silu)
                    k.dma(dngate[r0:r0 + 128, tok0:tok0 + n], e32[:, 0:n], q="pool")
                elif nm[0:2] == "cq":
                    i = int(nm[2])
                    k.copy(cqk[i][:, 0:n], pa[:, 0:n], eng="act")
                    k.tt(sqt[i][:, 0:n], cqk[i][:, 0:n], cqk[i][:, 0:n], ALU.mult, eng="pool")
                    k.mm(pS[:, 0:n], C(CONES), sqt[i][:, 0:n], start=(i == 0), stop=(i == 1))
                    if i == 1:
                        k.rsqrt(rstd[:, 0:n], pS[:, 0:n], 1.0 / 256.0, EPS)
                        for ii in range(2):
                            k.stt(cqn[:, ii, 0:n], cqk[ii][:, 0:n], pcol(l, PV_QN + ii), rstd[:, 0:n], ALU.mult, ALU.mult)
                        for h in range(4):
                            pq = pT[h % 2]
                            for ii in range(2):
                                k.mm(pq[0:96, 0:n], wuqb[:, ii, h * 96:(h + 1) * 96], cqn[:, ii, 0:n], start=(ii == 0), stop=(ii == 1))
                            q32, q16 = next_ev()
                            if isS:
                                rope(pq[0:96, 0:n], 96, CR96, tok0, n, q32)
                            else:
                                k.copy(q32[0:96, 0:n], pq[0:96, 0:n], eng="act")
                            k.anycopy(q16[0:96, 0:n], q32[0:96, 0:n])
                            k.dma(qm[h, :, tok0:tok0 + n], q16[0:96, 0:n], q="pool")
                elif nm == "ckv":
                    k.copy(e32[:, 0:n], pa[:, 0:n], eng="act")
                    k.tt(sqt[0][:, 0:n], e32[:, 0:n], e32[:, 0:n], ALU.mult, eng="pool")
                    k.mm(pS[:, 0:n], C(CONES), sqt[0][:, 0:n])
                    k.rsqrt(rstd[:, 0:n], pS[:, 0:n], 1.0 / 128.0, EPS)
                    k.stt(e32[:, 0:n], e32[:, 0:n], pcol(l, PV_KVN), rstd[:, 0:n], ALU.mult, ALU.mult)
                    for (si, so, bo, nn) in pseqs:
                        k.dma(o_ckv[si, l, :, so:so + nn], e32[:, bo:bo + nn], q="pool", is_out=True)
                    k.anycopy(e16[:, 0:n], e32[:, 0:n])
                    k.dma(ckvs[:, kc0:kc0 + n], e16[:, 0:n], q="pool")
                elif nm[0] == "d":
                    r0 = {"x": 0, "B": 256, "C": 512}[nm[1]] + int(nm[2]) * 128
                    k.copy(e32[:, 0:n], pa[:, 0:n], eng="act")
                    k.dma(ssmpre[r0:r0 + 128, tok0:tok0 + n], e32[:, 0:n], q="pool")
            pa = pA[NFM % 4]
            for j in range(8):
                k.mm(pa[0:96, 0:n], winb[:, j, C_KR:C_KR + 96], h_[:, j, 0:n], start=(j == 0), stop=(j == 7))
            e32, e16 = next_ev()
            if isS:
                rope(pa[0:96, 0:n], 96, CR96, tok0, n, e32)
            else:
                k.copy(e32[0:96, 0:n], pa[0:96, 0:n], eng="act")
                for (si, so, bo, nn) in pseqs:
                    k.dma(o_kr[si, l, :, so:so + nn], e32[64:96, bo:bo + nn], q="pool", is_out=True)
            k.anycopy(e16[0:96, 0:n], e32[0:96, 0:n])
            k.dma(kr96[:, kc0:kc0 + n], e16[0:96, 0:n], q="pool")
            for tt_ in range(n // 128):
                tk = tok0 + tt_ * 128
                p1 = pT[0]
                p2 = pT[1]
                for j in range(8):
                    k.mm(p1[:, 0:280], h_[:, j, tt_ * 128:(tt_ + 1) * 128], winb[:, j, C_TM1:C_TM1 + 280], start=(j == 0), stop=(j == 7))
                for j in range(8):
                    k.mm(p2[:, 0:256], h_[:, j, tt_ * 128:(tt_ + 1) * 128], winb[:, j, C_TM2:C_TM2 + 256], start=(j == 0), stop=(j == 7))
                vt = vtm[tt_ % 2]
                va = vaug[tt_ % 2]
                zt = ztt[tt_ % 2]
                k.copy(vt[:, 0:280], p1[:, 0:280], eng="act")
                k.act(zt[:], p2[:, 0:256], AF.Silu)
                for h in range(4):
                    k.anycopy(va[:, h, 0:64], vt[:, h * 64:(h + 1) * 64])
                if not isS:
                    for si in range(NPQ):
                        s0 = TS + si * LP
                        if s0 <= tk < s0 + LP:
                            k.dma(o_dv[si, l, tk - s0:tk - s0 + 128, :], vt[:, 0:256], q="pool", is_out=True)
                kc = cfg.kcol(tk)
                k.dma(vd[kc:kc + 128, :, :], va[:], q="pool")
                k.dma(smalls[tk:tk + 128, :], vt[:, 256:280], q="pool")
                k.dma(ztm[tk:tk + 128, :], zt[:], q="pool")
        pk = k.sb([128, 2, PAST], name="pk")
        pk16 = k.sb([128, 2, PAST], BF16, name="pk16")
        k.dma(pk[:], cdk[l].rearrange("(a p) t -> p a t", p=128))
        k.anycopy(pk16[:], pk[:])
        for a in range(2):
            k.dma(kd[a * 128:(a + 1) * 128, TS:TS + PAST], pk16[:, a, :], q="pool")
        for tt_ in range(PAST // 128):
            vt = vtm[tt_ % 2]
            va = vaug[tt_ % 2]
            k.dma(vt[:, 0:256], cdv[l, tt_ * 128:(tt_ + 1) * 128, :])
            for h in range(4):
                k.anycopy(va[:, h, 0:64], vt[:, h * 64:(h + 1) * 64])
            k.dma(vd[TS + tt_ * 128:TS + (tt_ + 1) * 128, :, :], va[:], q="pool")
        pc = k.sb([128, PAST], name="pc")
        pc16 = k.sb([128, PAST], BF16, name="pc16")
        k.dma(pc[:], cckv[l])
        k.anycopy(pc16[:], pc[:])
        k.dma(ckvs[:, TS:TS + PAST], pc16[:], q="pool")
        pr32 = k.sb([128, PAST], name="pr32")
        pr16 = k.sb([128, PAST], BF16, name="pr16")
        k.dma(pr32[64:96, :], ckr[l])
        k.anycopy(pr16[64:96, :], pr32[64:96, :])
        k.dma(kr96[64:96, TS:TS + PAST], pr16[64:96, :], q="pool")
        k.stage_end()

        def attention(seq, streams, scale, emit_out):
            tokS, L, ks, nk, isS = seq
            nkt = nk // 128
            QC = 512 if L % 512 == 0 else (256 if L % 256 == 0 else 128)
            for qc in range(L // QC):
                qts = []
                for si, (q_ap, kap, vfn) in enumerate(streams):
                    qt = qtl[si][qc % 2]
                    rows = kap.shape[0]
                    r0 = kap.base_partition() if hasattr(kap, "base_partition") else 0
                    k.dma(qt[r0:r0 + rows, 0:QC], q_ap[:, qc * QC:(qc + 1) * QC])
                    qts.append(qt[r0:r0 + rows, 0:QC])
                for si, (q_ap, kap, vfn) in enumerate(streams):
                    k.mm(pSc[si][0][:, 0:QC], kap[:, 0:128], qts[si])
                for kt in range(nkt):
                    if kt + 1 < nkt:
                        for si, (q_ap, kap, vfn) in enumerate(streams):
                            k.mm(pSc[si][(kt + 1) % 2][:, 0:QC], kap[:, (kt + 1) * 128:(kt + 2) * 128], qts[si])
                    if PE_FILL:
                        k.mm(pDm[:, 0:QC], CB(0), dmy[:, 0:QC])
                    for si in range(len(streams)):
                        k.act(ptl[si][kt % 2][:, 0:QC], pSc[si][kt % 2][:, 0:QC], AF.Exp, scale=scale)
                    for si, (q_ap, kap, vfn) in enumerate(streams):
                        k.mm(pO[si][:, 0:QC], vfn(kt), ptl[si][kt % 2][:, 0:QC], start=(kt == 0), stop=(kt == nkt - 1))
                emit_out(qc, QC)

        k.stage_begin()
        qtl = [[k.sb([128, 512], BF16, name="qt%d_%d" % (si, i)) for i in range(2)] for si in range(2)]
        ptl = [[k.sb([128, 512], BF16, name="pt%d_%d" % (si, i)) for i in range(2)] for si in range(2)]
        pSc = [[k.ps(name="pSc%d_%d" % (si, i)) for i in range(2)] for si in range(2)]
        pO = [k.ps(name="pO%d" % i) for i in range(2)]
        pN = k.ps(name="pN")
        pDm = k.ps(name="pDm")
        dmy = k.sb([128, 512], BF16, name="dmy")
        k.memset(dmy[:], 1.0)
        pKV = [pSc[0][0], pSc[0][1]]
        maxk = TS + PAST
        ktl = [k.sb([128, maxk], BF16, name="ktile%d" % i) for i in range(2)]
        vtile = k.sb([128, maxk // 128, 128], BF16, name="vtile")
        o0 = k.sb([64, 512], name="o0")
        o1 = k.sb([64, 512], name="o1")
        rr0 = k.sb([64, 512], name="rr0")
        rr1 = k.sb([64, 512], name="rr1")
        osq = k.sb([64, 512], name="osq")
        ob = [k.sb([64, 512], BF16, name="ob%d" % i) for i in range(2)]
        ob16 = [[k.sb([64, 512], BF16, name="obm%d_%d" % (hh, i)) for i in range(2)] for hh in range(2)]
        osl = [[k.sb([128, 512], name="os%d_%d" % (i, si)) for si in range(2)] for i in range(2)]
        gcol = k.sb([128, 1], name="gcol")
        lam_init = 0.8 - 0.6 * math.exp(-0.3 * l)
        k.ts(gcol[:], pcol(l, PV_DIFFN), 1.0 - lam_init, None, op0=ALU.mult)
        nob = [0]
        for seq in cfg.seqs:
            tokS, L, ks, nk, isS = seq
            for h in range(4):
                ktile = ktl[h % 2]
                k.dma(ktile[0:64, 0:nk], kd[h * 64:(h + 1) * 64, ks:ks + nk])
                k.dma(vtile[:, 0:nk // 128, :], vd[ks:ks + nk, h, :].rearrange("(t p) c -> p t c", p=128))

                def out_diff(qc, QC, h=h, tokS=tokS):
                    nob[0] += 1
                    osb = osl[nob[0] % 2]
                    for si in range(2):
                        k.copy(osb[si][:, 0:QC], pO[si][:, 0:QC], eng="act")
                        k.arecip(osb[si][64:128, 0:QC], osb[si][64:128, 0:QC])
                    k.copy(rr0[:, 0:QC], osb[0][64:128, 0:QC], eng="pool")
                    k.copy(rr1[:, 0:QC], osb[1][64:128, 0:QC], eng="pool")
                    k.tt(o0[:, 0:QC], osb[0][0:64, 0:QC], rr0[:, 0:QC], ALU.mult)
                    k.tt(o1[:, 0:QC], osb[1][0:64, 0:QC], rr1[:, 0:QC], ALU.mult)
                    k.stt(o0[:, 0:QC], o1[:, 0:QC], neglam[0:64, l:l + 1], o0[:, 0:QC], ALU.mult, ALU.add)
                    k.tt(osq[:, 0:QC], o0[:, 0:QC], o0[:, 0:QC], ALU.mult, eng="pool")
                    k.mm(pN[0:64, 0:QC], C(CONESBD, 64, 64), osq[:, 0:QC])
                    k.rsqrt(rr0[:, 0:QC], pN[0:64, 0:QC], 1.0 / 64.0, EPS)
                    o16 = ob[nob[0] % 2]
                    k.stt(o16[:, 0:QC], o0[:, 0:QC], gcol[0:64, :], rr0[:, 0:QC], ALU.mult, ALU.mult)
                    k.dma(mixT[h * 64:(h + 1) * 64, tokS + qc * QC:tokS + (qc + 1) * QC], o16[:, 0:QC], q="pool")

                vfn = lambda kt: vtile[:, kt, :]
                streams = [(qd[h * 64 + s_ * 32:h * 64 + (s_ + 1) * 32, tokS:tokS + L], ktile[s_ * 32:(s_ + 1) * 32, 0:nk], vfn)
                           for s_ in range(2)]
                attention(seq, streams, 32 ** -0.5, out_diff)
        wukb = k.sb([128, 1, 256], BF16, name="wukb")
        wuvb = k.sb([128, 1, 256], BF16, name="wuvb")
        load_w_bf16(wukb, w_uk[l], 256)
        load_w_bf16(wuvb, w_uv[l], 256)
        ckt = k.sb([128, maxk], BF16, name="ckt")
        vall = k.sb([128, maxk // 128, 4, 128], BF16, name="vall")
        k.memset(vall[:], 1.0)
        for seq in cfg.seqs:
            tokS, L, ks, nk, isS = seq
            k.dma(ckt[:, 0:nk], ckvs[:, ks:ks + nk])
            for kt in range(nk // 128):
                pkv = pKV[kt % 2]
                k.mm(pkv[:, 0:256], ckt[:, kt * 128:(kt + 1) * 128], wuvb[:, 0, :])
                for h in range(4):
                    k.copy(vall[:, kt, h, 0:64], pkv[:, h * 64:(h + 1) * 64], eng="dve")
            for hpair in range(2):
                for hh in range(2):
                    h = hpair * 2 + hh
                    ktile = ktl[hh]
                    c0 = 0
                    while c0 < nk:
                        n = min(512, nk - c0)
                        pkv = pKV[(c0 // 512) % 2]
                        k.mm(pkv[0:64, 0:n], wukb[:, 0, h * 64:(h + 1) * 64], ckt[:, c0:c0 + n])
                        k.copy(ktile[0:64, c0:c0 + n], pkv[0:64, 0:n], eng="act")
                        c0 += n
                    k.dma(ktile[64:96, 0:nk], kr96[64:96, ks:ks + nk])

                def out_mla(qc, QC, hpair=hpair, tokS=tokS):
                    nob[0] += 1
                    osb = osl[nob[0] % 2]
                    for hh, rr in ((0, rr0), (1, rr1)):
                        h = hpair * 2 + hh
                        k.copy(osb[hh][:, 0:QC], pO[hh][:, 0:QC], eng="act")
                        k.arecip(osb[hh][64:128, 0:QC], osb[hh][64:128, 0:QC])
                        k.copy(rr[:, 0:QC], osb[hh][64:128, 0:QC], eng="pool")
                        o16 = ob16[hh][nob[0] % 2]
                        k.tt(o16[:, 0:QC], osb[hh][0:64, 0:QC], rr[:, 0:QC], ALU.mult)
                        k.dma(mixT[512 + h * 64:512 + (h + 1) * 64, tokS + qc * QC:tokS + (qc + 1) * QC], o16[:, 0:QC], q="pool")

                streams = [(qm[hpair * 2 + hh, :, tokS:tokS + L], ktl[hh][0:96, 0:nk],
                            (lambda kt, h=hpair * 2 + hh: vall[:, kt, h, :])) for hh in range(2)]
                attention(seq, streams, 96 ** -0.5, out_mla)
        k.stage_end()

        for seq in cfg.seqs:
            dn_seq(k, cfg, l, seq, C, CB, pcol, pr, dnpre, dngate, smalls, sdl, o_sd, mixT, cst2_d)
        for seq in cfg.seqs:
            ssd_seq(k, cfg, l, seq, C, CB, pcol, pr, ssmpre, smalls, ztm, sss, o_ss, mixT)

        if debug and l == 0:
            k.stage_begin()
            dt_ = k.sb([128, 8, 512], BF16, name="dbgt")
            for (tok0, n, isS) in blocks:
                k.dma(dt_[:, :, 0:n], mixT[:, tok0:tok0 + n].rearrange("(j p) t -> p j t", p=128))
                k.dma(dbg["mixT"][:, tok0:tok0 + n].rearrange("(j p) t -> p j t", p=128), dt_[:, :, 0:n], q="pool", is_out=True)
            k.stage_end()

        def layer_norm(xin, n, gq, bq, xout):
            for j in range(8):
                k.mm(pL[0][:, 0:n], C(CONES), xin[:, j, 0:n], start=(j == 0), stop=(j == 7))
            k.ts(mean[:, 0:n], pL[0][:, 0:n], -1.0 / D, None, op0=ALU.mult)
            for j in range(8):
                k.tt(xin[:, j, 0:n], xin[:, j, 0:n], mean[:, 0:n], ALU.add)
                k.act(lsq[:, j, 0:n], xin[:, j, 0:n], AF.Square)
            for j in range(8):
                k.mm(pL[1][:, 0:n], C(CONES), lsq[:, j, 0:n], start=(j == 0), stop=(j == 7))
            k.rsqrt(mean[:, 0:n], pL[1][:, 0:n], 1.0 / D, EPS / (ALPHA * ALPHA))
            for j in range(8):
                k.tt(xin[:, j, 0:n], xin[:, j, 0:n], mean[:, 0:n], ALU.mult)
                k.act(xout[:, j, 0:n], xin[:, j, 0:n], AF.Identity, bias=pcol(l, bq + j), scale=pcol(l, gq + j))

        ALPHA = (2 * Dp) ** 0.25
        k.stage_begin()
        woutb = k.sbk([128, 8, D], BF16, name="woutb")
        load_w_bf16(woutb, w_out[l], D)
        xb = [k.sbk([128, 8, 512], name="xc%d" % i) for i in range(2)]
        mb = [k.sbk([128, 8, 512], BF16, name="mb%d" % i) for i in range(2)]
        xo = [k.sbk([128, 8, 512], name="xo%d" % i) for i in range(2)]
        lsq = k.sbk([128, 8, 512], name="lsq")
        mean = k.sb([128, 512], name="mean")
        pL = [k.ps(name="pL%d" % i) for i in range(2)]
        pC = [k.ps(name="pC%d" % i) for i in range(4)]
        for bi, (tok0, n, isS) in enumerate(blocks):
            x_ = xb[bi % 2]
            m_ = mb[bi % 2]
            xo_ = xo[bi % 2]
            k.dma(x_[:, :, 0:n], xsrc[:, tok0:tok0 + n].rearrange("(j p) t -> p j t", p=128))
            k.dma(m_[:, :, 0:n], mixT[:, tok0:tok0 + n].rearrange("(j p) t -> p j t", p=128))
            for oc in range(8):
                pc_ = pC[oc % 4]
                for j in range(8):
                    k.mm(pc_[:, 0:n], woutb[:, j, oc * 128:(oc + 1) * 128], m_[:, j, 0:n], start=(j == 0), stop=(j == 7))
                k.stt(x_[:, oc, 0:n], pc_[:, 0:n], modcol(l, 2, oc, isS), x_[:, oc, 0:n], ALU.mult, ALU.add)
            layer_norm(x_, n, PV_LN1G, PV_LN1B, xo_)
            k.dma(xs[:, tok0:tok0 + n].rearrange("(j p) t -> p j t", p=128), xo_[:, :, 0:n], q="pool")
        k.stage_end()

        k.stage_begin()
        w1b = k.sbk([128, 8, DFF], BF16, name="w1b")
        w2b = k.sbk([128, 32, D], BF16, name="w2b")
        load_w_bf16(w1b, w_ff1[l], DFF)
        load_w_bf16(w2b, w_ff2[l], D)
        NB = 256
        xb = [k.sbk([128, 8, NB], name="xd%d" % i) for i in range(2)]
        hb = [k.sbk([128, 8, NB], BF16, name="hd%d" % i) for i in range(2)]
        fh = k.sbk([128, 32, NB], BF16, name="fh")
        rl = [k.sb([128, NB], name="rl%d" % i) for i in range(2)]
        lsq = k.sbk([128, 8, NB], name="lsq2")
        mean = k.sb([128, NB], name="mean2")
        pL = [k.ps(name="pL2%d" % i) for i in range(2)]
        pC = [k.ps(name="pC2%d" % i) for i in range(4)]
        bi = 0
        for t0 in range(0, TT, NB):
            n = NB
            isS = t0 < TS
            x_ = xb[bi % 2]
            h_ = hb[bi % 2]
            xo_ = x_
            bi += 1
            k.dma(x_[:, :, 0:n], xs[:, t0:t0 + n].rearrange("(j p) t -> p j t", p=128))
            for j in range(8):
                k.act(h_[:, j, 0:n], x_[:, j, 0:n], AF.Identity, bias=modcol(l, 3, j, isS), scale=modcol(l, 4, j, isS))
            for fc in range(32):
                pc_ = pC[fc % 4]
                for j in range(8):
                    k.mm(pc_[:, 0:n], w1b[:, j, fc * 128:(fc + 1) * 128], h_[:, j, 0:n], start=(j == 0), stop=(j == 7))
                r_ = rl[fc % 2]
                k.act(r_[:, 0:n], pc_[:, 0:n], AF.Relu)
                k.tt(fh[:, fc, 0:n], r_[:, 0:n], r_[:, 0:n], ALU.mult, eng=("dve", "pool")[fc % 2])
            for oc in range(8):
                pc_ = pC[oc % 4]
                for fc in range(32):
                    k.mm(pc_[:, 0:n], w2b[:, fc, oc * 128:(oc + 1) * 128], fh[:, fc, 0:n], start=(fc == 0), stop=(fc == 31))
                k.stt(x_[:, oc, 0:n], pc_[:, 0:n], modcol(l, 5, oc, isS), x_[:, oc, 0:n], ALU.mult, ALU.add)
            layer_norm(x_, n, PV_LN2G, PV_LN2B, xo_)
            k.dma(xdst[:, t0:t0 + n].rearrange("(j p) t -> p j t", p=128), xo_[:, :, 0:n], q="pool", is_out=(l == Dp - 1))
        k.stage_end()

    k.finish()
    return nc


def conv_silu(k, out, tmp, src, w3, n, bias=None):
    k.ts(tmp[:, 0:n], src[:, 1:n + 1], w3[:, 1:2], None, op0=ALU.mult)
    k.stt(tmp[:, 0:n], src[:, 0:n], w3[:, 0:1], tmp[:, 0:n], ALU.mult, ALU.add)
    k.stt(tmp[:, 0:n], src[:, 2:n + 2], w3[:, 2:3], tmp[:, 0:n], ALU.mult, ALU.add)
    if bias is None:
        k.act(out[:, 0:n], tmp[:, 0:n], AF.Silu)
    else:
        k.act(out[:, 0:n], tmp[:, 0:n], AF.Silu, bias=bias, scale=1.0)


class Ring:
    def __init__(self, k, name, shape, dt, n):
        self.t = [k.sb(list(shape), dt, name="%s_%d" % (name, i)) for i in range(n)]

    def __getitem__(self, i):
        return self.t[i % len(self.t)]


def psum_slots(k, name, nbanks, slot=128):
    slots = []
    for i in range(nbanks):
        p = k.ps(name="%s%d" % (name, i))
        slots.append(p[:, 0:slot])
    return slots


def interleave(gens):
    gens = list(gens)
    while gens:
        for g_ in list(gens):
            try:
                next(g_)
            except StopIteration:
                gens.remove(g_)


class Slots:
    def __init__(self, banks, width=128):
        self.s = []
        for j in range(512 // width):
            for b in banks:
                self.s.append(b[:, j * width:(j + 1) * width])
        self.n = 0

    def __call__(self):
        self.n += 1
        return self.s[self.n % len(self.s)]


def dn_seq(k, cfg, l, seq, C, CB, pcol, pr, dnpre, dngate, smalls, sdl, o_sd, mixT, cst2_d):
    tokS, L, ks, nk, isS = seq
    nch = L // 64
    k.stage_begin()
    banks = [k.ps(name="dnP%d" % i) for i in range(8)]
    P0 = Slots(banks)
    HP = []
    for hp in range(2):
        qn = k.sb([128, L], BF16, name="dnq%d" % hp)
        kn = k.sb([128, L], BF16, name="dnk%d" % hp)
        vc = k.sb([128, L], BF16, name="dnv%d" % hp)
        sm = k.sb([128, nch, 24], name="dnsm%d" % hp)
        oall = k.subkey(k.sb([128, nch, 64], name="dnoall%d" % hp), nch * 64, 64)
        HP.append((qn, kn, vc, sm, oall))
    k.stage_begin()
    raw = k.sb([128, L + 2], name="dnraw")
    tmp = k.sb([128, L], name="dntmpc")
    sq = k.sb([128, 512], name="dnsq")
    rs = k.sb([128, 512], name="dnrs")
    for hp in range(2):
        qn, kn, vc, sm, oall = HP[hp]
        for which, dst in ((0, qn), (1, kn), (2, vc)):
            r0 = which * 256 + hp * 128
            k.memset(raw[:, 0:1], 0.0)
            k.memset(raw[:, L + 1:L + 2], 0.0)
            k.dma(raw[:, 1:L + 1], dnpre[r0:r0 + 128, tokS:tokS + L])
            ti = which * 2 + hp
            if which == 2:
                conv_silu(k, dst, tmp, raw, pcol(l, PV_DNCONV + ti * 3, 3), L)
            else:
                conv_silu(k, tmp, tmp, raw, pcol(l, PV_DNCONV + ti * 3, 3), L)
                for c0 in range(0, L, 512):
                    n = min(512, L - c0)
                    k.tt(sq[:, 0:n], tmp[:, c0:c0 + n], tmp[:, c0:c0 + n], ALU.mult, eng="pool")
                    for q4 in range(0, n, 128):
                        p = P0()
                        k.mm(p[:, 0:128], C(CONESBD), sq[:, q4:q4 + 128])
                        k.rsqrt(rs[:, q4:q4 + 128], p[:, 0:128], 1.0, EPS)
                    if which == 0:
                        k.stt(dst[:, c0:c0 + n], tmp[:, c0:c0 + n], 0.125, rs[:, 0:n], ALU.mult, ALU.mult)
                    else:
                        k.tt(dst[:, c0:c0 + n], tmp[:, c0:c0 + n], rs[:, 0:n], ALU.mult)
        src = smalls[tokS:tokS + L, :].rearrange("(c i) k -> i c k", i=64)
        k.dma(sm[0:64, :, :], src)
        k.dma(sm[64:128, :, :], src)
    k.stage_end()
    tmq = k.sb([128, nch], name="dntmp")
    assert nch <= 128
    chains = []

    cx = k.sb([128, (NCST - NCST_BASE) * 128], name="dncx")
    k.dma(cx[:], cst2_d[:, :])

    def C2(i):
        if i >= NCST_BASE:
            return cx[:, (i - NCST_BASE) * 128:(i - NCST_BASE + 2) * 128]
        return C(i, 128, 256)

    for hp in range(2):
        qn, kn, vc, sm, oall = HP[hp]
        dd = []
        S2 = k.sb([128, 128], name="dnS%d" % hp)
        Sb2 = k.sb([128, 128], BF16, name="dnSb%d" % hp)
        for d in range(2):
            G = k.sb([128, nch], name="dnG%d%d" % (hp, d))
            Bt = k.sb([128, nch], name="dnB%d%d" % (hp, d))
            for hh in range(2):
                h = 2 * hp + hh
                rows = slice(hh * 64, hh * 64 + 64)
                cb = l * PR_L
                k.act(Bt[rows, :], sm[rows, :, d * 4 + h], AF.Sigmoid)
                k.act(tmq[rows, :], sm[rows, :, 8 + d * 4 + h], AF.Exp,
                      bias=pr[rows, cb + PR_DNB + d * 4 + h:cb + PR_DNB + d * 4 + h + 1], scale=1.0)
                k.act(tmq[rows, :], tmq[rows, :], AF.Ln, bias=1.0, scale=1.0)
                ea = k.sb([128, 1], name="dnea")
                k.act(ea[rows, :], pr[rows, cb + PR_DNA + d * 4 + h:cb + PR_DNA + d * 4 + h + 1], AF.Exp)
                k.ts(G[rows, :], tmq[rows, :], ea[rows, :], -1.0, op0=ALU.mult, op1=ALU.mult)
            nm = "%d%d" % (hp, d)
            gc = k.sb([128, nch], name="dngc" + nm)
            gl = k.sb([128, nch], name="dngl" + nm)
            p = P0()
            k.mm(p[:, 0:nch], C(CTRI0 + d), G[:, :])
            k.copy(gc[:], p[:, 0:nch], eng="act")
            p = P0()
            k.mm(p[:, 0:nch], C(CONESBD), G[:, :])
            k.copy(gl[:], p[:, 0:nch], eng="act")
            egc = k.sb([128, nch], name="dnegc" + nm)
            egl = k.sb([128, nch], name="dnegl" + nm)
            ekd = k.sb([128, nch], name="dnekd" + nm)
            bg = k.sb([128, nch], name="dnbg" + nm)
            ngc = k.sb([128, nch], name="dnngc" + nm)
            k.act(egc[:], gc[:], AF.Exp)
            k.act(egl[:], gl[:], AF.Exp)
            k.tt(ekd[:], gl[:], gc[:], ALU.subtract)
            k.act(ekd[:], ekd[:], AF.Exp)
            k.tt(bg[:], Bt[:], egc[:], ALU.mult)
            k.ts(ngc[:], gc[:], -1.0, None, op0=ALU.mult)
            if isS:
                k.dma(S2[:, d * 64:(d + 1) * 64], sdl[l, d, hp, :, :])
            else:
                k.memset(S2[:, d * 64:(d + 1) * 64], 0.0)
            dd.append(dict(G=G, Bt=Bt, gc=gc, egc=egc, egl=egl, ekd=ekd, bg=bg, ngc=ngc))
        k.copy(Sb2[:], S2[:], eng="act")

        def R(name, shape=(128, 256), dt=BF16, n=2, hp=hp):
            return Ring(k, "%s%d" % (name, hp), shape, dt, n)

        bk = banks[4 * hp:4 * hp + 4]
        ch = dict(hp=hp, dd=dd, S2=S2, Sb2=Sb2, qn=qn, kn=kn, vc=vc, oall=oall,
                  P=Slots(bk[0:3], 256), PS=Slots(bk[3:4], 128),
                  KT=R("dnKT"), QT=R("dnQT"), VT=R("dnVT"),
                  TG=R("dnTG", dt=F32, n=1), M1=R("dnM1", dt=F32, n=1), M2=R("dnM2", dt=F32, n=1),
                  Dex=R("dnDex", dt=F32, n=1), DexT=R("dnDexT", dt=F32, n=1),
                  A=R("dnA", n=1), B=R("dnBm", n=1), Tt=R("dnTt", n=1), Ls=R("dnLs", n=6), Zs=R("dnZs", n=1),
                  Ts=R("dnTs", n=1), QKm=R("dnQKm"), vb=R("dnvb", (128, 128), n=1), kbg=R("dnkbg", n=1),
                  kdec=R("dnkdec"), U=R("dnU", (128, 128), F32), WT=R("dnWT"),
                  vnew=R("dnvn", (128, 128), BF16, 1), tq=R("dntq", (128, 128), F32, 1))
        for nm_ in ("KT", "QT", "VT"):
            for t in ch[nm_].t:
                k.memset(t[:], 0.0)
        chains.append(ch)

    def chunk_of(s, d):
        return s if d == 0 else nch - 1 - s

    def H(t, d, w=128):
        return t[:, d * w:(d + 1) * w]

    def pre(ch, s):
        P = ch["P"]
        dd = ch["dd"]
        kt2, qt2, vt2 = ch["KT"][s], ch["QT"][s], ch["VT"][s]
        tg2 = ch["TG"][s]
        for d in range(2):
            c = chunk_of(s, d)
            cs = slice(c * 64, (c + 1) * 64)
            for (dst, srcT) in ((kt2, ch["kn"]), (qt2, ch["qn"]), (vt2, ch["vc"])):
                k.copy(H(dst, d)[0:64, 0:64], srcT[0:64, cs], eng="pool")
                k.copy(H(dst, d)[64:128, 64:128], srcT[64:128, cs], eng="pool")
            k.act(H(tg2, d), C(CTRI0 + d), AF.Copy, scale=dd[d]["G"][:, c:c + 1])
        yield
        p1, p2, p3, p4, p5 = P(), P(), P(), P(), P()
        for d in range(2):
            k.mm(H(p1, d), H(kt2, d), H(kt2, d))
        for d in range(2):
            k.mm(H(p2, d), H(kt2, d), H(qt2, d))
        for d in range(2):
            k.mm(H(p3, d), H(kt2, d), CB(0))
        for d in range(2):
            k.mm(H(p4, d, 64), H(vt2, d), CB(2, 128, 64))
        for d in range(2):
            k.mm(H(p5, d), C(CONESBD), H(tg2, d))
        yield
        M1, M2 = ch["M1"][s], ch["M2"][s]
        k.tt(M1[:], p5, C2(CMB0), ALU.add)
        k.tt(M2[:], p5, C2(CNM0), ALU.add)
        vb2, kbg2, kdec2 = ch["vb"][s], ch["kbg"][s], ch["kdec"][s]
        for d in range(2):
            c = chunk_of(s, d)
            col = lambda t: t[:, c:c + 1]
            k.act(H(vb2, d, 64), H(p4, d, 64), AF.Copy, scale=col(dd[d]["Bt"]))
            k.act(H(kbg2, d), H(p3, d), AF.Copy, scale=col(dd[d]["bg"]))
            k.act(H(kdec2, d), H(p3, d), AF.Copy, scale=col(dd[d]["ekd"]))
        yield
        Dex, DexT = ch["Dex"][s], ch["DexT"][s]
        for d in range(2):
            c = chunk_of(s, d)
            col = lambda t: t[:, c:c + 1]
            k.act(H(Dex, d), H(M1, d), AF.Exp, bias=col(dd[d]["gc"]), scale=-1.0)
            k.act(H(DexT, d), H(M2, d), AF.Exp, bias=col(dd[d]["ngc"]), scale=1.0)
        yield
        a2 = ch["A"][s]
        for d in range(2):
            c = chunk_of(s, d)
            k.stt(H(a2, d), H(p1, d), dd[d]["Bt"][:, c:c + 1], H(Dex, d), ALU.mult, ALU.mult)
        k.tt(ch["QKm"][s][:], p2, DexT[:], ALU.mult)
        yield
        p6 = P()
        for d in range(2):
            k.mm(H(p6, d), H(a2, d), CB(0))
        for li in range(1, 6):
            k.tt(ch["Ls"][li][:], a2[:], C2(CMS2 + 2 * li), ALU.mult, eng=("pool", "dve")[li % 2])
        yield
        b2 = ch["B"][s]
        k.copy(b2[:], p6, eng="act")
        yield
        tt2 = ch["Tt"][s]
        ls0 = ch["Ls"][0]
        k.tt(ls0[:], b2[:], C2(CMS2), ALU.mult, eng="pool")
        yield
        k.tt(tt2[:], C2(CI2), ls0[:], ALU.subtract)
        yield
        Zs2, Ts2 = ch["Zs"][s], ch["Ts"][s]
        for li in range(1, 6):
            pa_, pb_ = P(), P()
            for d in range(2):
                k.mm(H(pa_, d), H(ch["Ls"][li], d), H(tt2, d))
            for d in range(2):
                k.mm(H(pb_, d), H(tt2, d), CB(0))
            yield
            k.copy(Zs2[:], pa_, eng="act")
            k.copy(Ts2[:], pb_, eng="dve")
            yield
            pc_ = P()
            for d in range(2):
                k.mm(H(pc_, d), H(Ts2, d), H(Zs2, d))
            yield
            k.tt(tt2[:], tt2[:], pc_, ALU.subtract)
            yield
        pu, pw = P(), P()
        for d in range(2):
            k.mm(H(pu, d, 64), H(tt2, d), H(vb2, d, 64))
        for d in range(2):
            k.mm(H(pw, d), H(kbg2, d), H(tt2, d))
        yield
        k.copy(ch["U"][s][:], pu[:, 0:128], eng="act")
        k.copy(ch["WT"][s][:], pw, eng="act")
        yield

    def scan(ch, s):
        PS = ch["PS"]
        dd = ch["dd"]
        S2, Sb2 = ch["S2"], ch["Sb2"]
        oall = ch["oall"]
        ps1, ps2 = PS(), PS()
        for d in range(2):
            k.mm(H(ps1, d, 64), H(ch["WT"][s], d), H(Sb2, d, 64))
        for d in range(2):
            k.mm(H(ps2, d, 64), H(ch["QT"][s], d), H(Sb2, d, 64))
        yield
        vn2, tq2 = ch["vnew"][s], ch["tq"][s]
        k.tt(vn2[:], ch["U"][s][:], ps1, ALU.subtract)
        for d in range(2):
            c = chunk_of(s, d)
            k.act(H(tq2, d, 64), H(ps2, d, 64), AF.Copy, scale=dd[d]["egc"][:, c:c + 1])
        yield
        ps3, ps4 = PS(), PS()
        for d in range(2):
            k.mm(H(ps3, d, 64), H(ch["QKm"][s], d), H(vn2, d, 64))
        for d in range(2):
            k.mm(H(ps4, d, 64), H(ch["kdec"][s], d), H(vn2, d, 64))
        yield
        for d in range(2):
            c = chunk_of(s, d)
            k.stt(H(S2, d, 64), H(S2, d, 64), dd[d]["egl"][:, c:c + 1], H(ps4, d, 64), ALU.mult, ALU.add)
        if s < nch // 2:
            for d in range(2):
                c = chunk_of(s, d)
                k.tt(oall[:, c, :], H(tq2, d, 64), H(ps3, d, 64), ALU.add)
        else:
            k.tt(tq2[:], tq2[:], ps3, ALU.add)
        yield
        k.copy(Sb2[:], S2[:], eng="pool")
        if s >= nch // 2:
            for d in range(2):
                c = chunk_of(s, d)
                k.tt(oall[:, c, :], oall[:, c, :], H(tq2, d, 64), ALU.add, eng="pool")
        yield

    interleave([pre(ch, 0) for ch in chains])
    for s in range(nch):
        gens = [scan(ch, s) for ch in chains]
        if s + 1 < nch:
            gens = [pre(ch, s + 1) for ch in chains] + gens
        interleave(gens)
    P = P0
    for ch in chains:
        if not isS:
            si = (tokS - cfg.ts) // cfg.lp
            for d in range(2):
                k.dma(o_sd[si, l, d, ch["hp"], :, :], ch["S2"][:, d * 64:(d + 1) * 64], q="pool", is_out=True)
    junk = k.sb([128, 64], name="dnjunk")
    on = [k.sb([128, 64], name="dnon%d" % i) for i in range(2)]
    oTp = [k.sb([128, 512], name="dnoT%d" % i) for i in range(2)]
    gtp = [k.sb([128, 512], name="dngt%d" % i) for i in range(2)]
    o16p = [k.sb([128, 512], BF16, name="dno16%d" % i) for i in range(2)]
    for hp in range(2):
        oall = HP[hp][4]
        ssq = k.sb([128, nch], name="dnssq%d" % hp)
        k.memset(ssq[:], 0.0)
        for c in range(nch):
            k.act(junk[:], oall[:, c, :], AF.Square, accum_out=ssq[:, c:c + 1])
        k.rsqrt(ssq[:], ssq[:], 1.0 / 64.0, EPS)
        for pi, c0 in enumerate(range(0, nch, 8)):
            ncc = min(8, nch - c0)
            oT, gt, o16 = oTp[pi % 2], gtp[pi % 2], o16p[pi % 2]
            w_ = ncc * 64
            k.dma(gt[:, 0:w_], dngate[hp * 128:(hp + 1) * 128, tokS + c0 * 64:tokS + c0 * 64 + w_])
            for cc in range(ncc):
                c = c0 + cc
                o_ = on[c % 2]
                k.ts(o_[:], oall[:, c, :], ssq[:, c:c + 1], None, op0=ALU.mult)
                p = P()
                k.mm(p[0:64, 0:128], o_[:], C(CI))
                k.copy(oT[0:64, cc * 64:(cc + 1) * 64], p[0:64, 0:64], eng="act")
                k.copy(oT[64:128, cc * 64:(cc + 1) * 64], p[0:64, 64:128], eng="dve")
            k.stt(o16[:, 0:w_], oT[:, 0:w_], pcol(l, PV_DNN), gt[:, 0:w_], ALU.mult, ALU.mult)
            k.dma(mixT[256 + hp * 128:256 + (hp + 1) * 128, tokS + c0 * 64:tokS + c0 * 64 + w_], o16[:, 0:w_], q="pool")
    k.stage_end()


def ssd_seq(k, cfg, l, seq, C, CB, pcol, pr, ssmpre, smalls, ztm, sss, o_ss, mixT):
    tokS, L, ks, nk, isS = seq
    nch = L // 64
    cb = l * PR_L
    k.stage_begin()
    banks = [k.ps(name="ssP%d" % i) for i in range(8)]
    P0 = Slots(banks)
    sm = k.sb([64, nch, 24], name="sssm")
    k.dma(sm[:], smalls[tokS:tokS + L, :].rearrange("(c i) k -> i c k", i=64))
    chains = []
    GR = []
    XBC = [[k.sb([128, L], BF16, name="ss%s%d" % (n_, g)) for n_ in ("x", "B", "C")] for g in range(2)]
    YALL = [k.subkey(k.sb([64, nch, 128], name="ssyall%d" % g), nch * 128, 128) for g in range(2)]
    k.stage_begin()
    raw = k.sb([128, L + 2], name="ssraw")
    tmp = k.sb([128, L], name="sstmpc")
    for g in range(2):
        for which, dst in enumerate(XBC[g]):
            r0 = which * 256 + g * 128
            k.memset(raw[:, 0:1], 0.0)
            k.memset(raw[:, L + 1:L + 2], 0.0)
            k.dma(raw[:, 1:L + 1], ssmpre[r0:r0 + 128, tokS:tokS + L])
            ti = which * 2 + g
            conv_silu(k, dst, tmp, raw, pcol(l, PV_SCW + ti * 3, 3), L, bias=pcol(l, PV_SCB + ti))
    k.stage_end()
    for g in range(2):
        xc, Bc, Cc = XBC[g]
        yall = YALL[g]
        GR.append(yall)
        for d in range(2):
            nm = "%d%d" % (g, d)
            DT = k.sb([64, nch, 2], name="ssDT" + nm)
            DTA = k.sb([64, nch, 2], name="ssDTA" + nm)
            for hh in range(2):
                h = 2 * g + hh
                k.act(DT[:, :, hh], sm[:, :, 16 + d * 4 + h], AF.Exp,
                      bias=pr[0:64, cb + PR_SB + d * 4 + h:cb + PR_SB + d * 4 + h + 1], scale=1.0)
                k.act(DT[:, :, hh], DT[:, :, hh], AF.Ln, bias=1.0, scale=1.0)
                ea = k.sb([64, 1], name="ssea")
                k.act(ea[:], pr[0:64, cb + PR_SA + d * 4 + h:cb + PR_SA + d * 4 + h + 1], AF.Exp)
                k.ts(DTA[:, :, hh], DT[:, :, hh], ea[:], -1.0, op0=ALU.mult, op1=ALU.mult)
            AC = k.sb([64, nch, 2], name="ssAC" + nm)
            LAST = k.sb([128, nch, 2], name="ssLAST" + nm)
            for c0 in range(0, nch, 64):
                n2 = min(64, nch - c0) * 2
                p = P0()
                k.mm(p[0:64, 0:n2], C(CTRI0 + d, 64, 64), DTA[:, c0:c0 + n2 // 2, :].rearrange("p c h -> p (c h)"))
                k.copy(AC[:, c0:c0 + n2 // 2, :].rearrange("p c h -> p (c h)"), p[0:64, 0:n2], eng="act")
                p = P0()
                k.mm(p[:, 0:n2], C(CONES, 64, 128), DTA[:, c0:c0 + n2 // 2, :].rearrange("p c h -> p (c h)"))
                k.copy(LAST[:, c0:c0 + n2 // 2, :].rearrange("p c h -> p (c h)"), p[:, 0:n2], eng="act")
            EAC = k.sb([64, nch, 2], name="ssEAC" + nm)
            NAC = k.sb([64, nch, 2], name="ssNAC" + nm)
            FD = k.sb([64, nch, 2], name="ssFD" + nm)
            EL = k.sb([128, nch, 2], name="ssEL" + nm)
            k.act(EAC[:], AC[:], AF.Exp)
            k.ts(NAC[:], AC[:], -1.0, None, op0=ALU.mult)
            k.tt(FD[:], LAST[0:64, :, :], AC[:], ALU.subtract)
            k.act(FD[:], FD[:], AF.Exp)
            k.act(EL[:], LAST[:], AF.Exp)
            ST = k.sb([128, 128], name="ssST" + nm)
            STb = k.sb([128, 128], BF16, name="ssSTb" + nm)
            if isS:
                k.dma(ST[:], sss[l, d, g, :, :])
            else:
                k.memset(ST[:], 0.0)
            k.copy(STb[:], ST[:], eng="act")

            def R(name, shape=(64, 128), dt=BF16, n=2, nm=nm):
                return Ring(k, name + nm, shape, dt, n)

            chains.append(dict(g=g, d=d, xc=xc, Bc=Bc, Cc=Cc, yall=yall, DT=DT, DTA=DTA, EAC=EAC, NAC=NAC, FD=FD, EL=EL,
                               ST=ST, STb=STb, P=Slots(banks[2 * len(chains):2 * len(chains) + 2]),
                               xt=R("ssxt", dt=F32, n=1), Btm=R("ssBt"), xdt=R("ssxdt", n=1),
                               TGs=R("ssTG", dt=F32, n=1), M2=R("ssM2", dt=F32, n=1), segT=R("sssegT", dt=F32, n=1),
                               MT=R("ssMT", n=1), xdd=R("ssxdd"), ty=R("ssty", dt=F32, n=1), yi=R("ssyi", dt=F32)))

    def chunk_of(s, d):
        return s if d == 0 else nch - 1 - s

    def pre(ch, s):
        d, g = ch["d"], ch["g"]
        P = ch["P"]
        c = chunk_of(s, d)
        cs = slice(c * 64, (c + 1) * 64)
        xt, Btm, xdt, TGs, M2, segT, MT, xdd, yi = (ch[n_][s] for n_ in ("xt", "Btm", "xdt", "TGs", "M2", "segT", "MT", "xdd", "yi"))
        p1 = P(); k.mm(p1[0:64, 0:128], ch["xc"][:, cs], CB(0))
        p2 = P(); k.mm(p2[0:64, 0:128], ch["Bc"][:, cs], CB(0))
        p4 = P(); k.mm(p4[0:64, 0:64], ch["Bc"][:, cs], ch["Cc"][:, cs])
        for hh in range(2):
            hs = slice(hh * 64, hh * 64 + 64)
            k.ts(TGs[:, hs], C(CTRI0 + d, 64, 64), ch["DTA"][:, c, hh:hh + 1], None, op0=ALU.mult)
        yield
        k.copy(xt[:], p1[0:64, 0:128], eng="act")
        k.copy(Btm[:], p2[0:64, 0:128], eng="dve")
        p3 = P(); k.mm(p3[0:64, 0:128], C(CONES, 64, 64), TGs[:])
        yield
        for hh in range(2):
            hs = slice(hh * 64, hh * 64 + 64)
            k.act(xdt[:, hs], xt[:, hs], AF.Copy, scale=ch["DT"][:, c, hh:hh + 1])
            k.tt(M2[:, hs], p3[0:64, hs], C(CNM0 + d, 64, 64), ALU.add)
        yield
        for hh in range(2):
            hs = slice(hh * 64, hh * 64 + 64)
            k.act(segT[:, hs], M2[:, hs], AF.Exp, bias=ch["NAC"][:, c, hh:hh + 1], scale=1.0)
            k.act(xdd[:, hs], xdt[:, hs], AF.Copy, scale=ch["FD"][:, c, hh:hh + 1])
        yield
        for hh in range(2):
            hs = slice(hh * 64, hh * 64 + 64)
            k.tt(MT[:, hs], p4[0:64, 0:64], segT[:, hs], ALU.mult)
        yield
        p5 = P()
        for hh in range(2):
            hs = slice(hh * 64, hh * 64 + 64)
            k.mm(p5[0:64, hs], MT[:, hs], xdt[:, hs])
        yield
        k.copy(yi[:], p5[0:64, 0:128], eng="act")
        yield
        if d == 0:
            for hh in range(2):
                hs = slice(hh * 64, hh * 64 + 64)
                h = 2 * g + hh
                k.stt(yi[:, hs], xt[:, hs], pr[0:64, cb + PR_SD + h:cb + PR_SD + h + 1], yi[:, hs], ALU.mult, ALU.add)
        yield

    def scan(ch, s):
        d = ch["d"]
        P = ch["P"]
        c = chunk_of(s, d)
        cs = slice(c * 64, (c + 1) * 64)
        ST, STb, yall = ch["ST"], ch["STb"], ch["yall"]
        ty, yi = ch["ty"][s], ch["yi"][s]
        p6 = P(); k.mm(p6[0:64, 0:128], ch["Cc"][:, cs], STb[:])
        p7 = P(); k.mm(p7[:, 0:128], ch["Btm"][s][:], ch["xdd"][s][:])
        yield
        for hh in range(2):
            hs = slice(hh * 64, hh * 64 + 64)
            k.stt(ST[:, hs], ST[:, hs], ch["EL"][:, c, hh:hh + 1], p7[:, hs], ALU.mult, ALU.add)
        yield
        k.copy(STb[:], ST[:], eng="pool")
        for hh in range(2):
            hs = slice(hh * 64, hh * 64 + 64)
            k.stt(ty[:, hs], p6[0:64, hs], ch["EAC"][:, c, hh:hh + 1], yi[:, hs], ALU.mult, ALU.add)
        yield
        if s < nch // 2:
            k.copy(yall[:, c, :], ty[:], eng="pool")
        else:
            k.tt(yall[:, c, :], yall[:, c, :], ty[:], ALU.add, eng="pool")
        yield

    interleave([pre(ch, 0) for ch in chains])
    for s in range(nch):
        gens = [scan(ch, s) for ch in chains]
        if s + 1 < nch:
            gens = [pre(ch, s + 1) for ch in chains] + gens
        interleave(gens)
    P = P0
    if not isS:
        si = (tokS - cfg.ts) // cfg.lp
        for ch in chains:
            k.dma(o_ss[si, l, ch["d"], ch["g"], :, :], ch["ST"][:], q="pool", is_out=True)
    junk = k.sb([64, 128], name="ssjunk")
    yn = [k.sb([64, 128], name="ssyn%d" % i) for i in range(2)]
    ztp = [k.sb([64, 8, 128], name="sszt%d" % i) for i in range(2)]
    yTp = [k.sb([128, 512], BF16, name="ssyT%d" % i) for i in range(2)]
    for g in range(2):
        yall = GR[g]
        ssq = k.sb([64, nch], name="ssssq%d" % g)
        k.memset(ssq[:], 0.0)
        for pi, c0 in enumerate(range(0, nch, 8)):
            ncc = min(8, nch - c0)
            zt = ztp[pi % 2]
            k.dma(zt[:, 0:ncc, :], ztm[tokS + c0 * 64:tokS + (c0 + ncc) * 64, g * 128:(g + 1) * 128].rearrange("(c i) k -> i c k", i=64))
            for cc in range(ncc):
                c = c0 + cc
                k.tt(yall[:, c, :], yall[:, c, :], zt[:, cc, :], ALU.mult, eng=("dve", "pool")[c % 2])
                k.act(junk[:], yall[:, c, :], AF.Square, accum_out=ssq[:, c:c + 1])
        k.rsqrt(ssq[:], ssq[:], 1.0 / 128.0, EPS)
        for pi, c0 in enumerate(range(0, nch, 8)):
            ncc = min(8, nch - c0)
            yT = yTp[pi % 2]
            for cc in range(ncc):
                c = c0 + cc
                y_ = yn[c % 2]
                k.stt(y_[:], yall[:, c, :], ssq[:, c:c + 1], pr[0:64, cb + PR_SN + g * 128:cb + PR_SN + (g + 1) * 128], ALU.mult, ALU.mult)
                p = P()
                k.mm(p[:, 0:64], y_[:], C(CI, 64, 64))
                k.copy(yT[:, cc * 64:(cc + 1) * 64], p[:, 0:64], eng=("act", "dve")[c % 2])
            k.dma(mixT[768 + g * 128:768 + (g + 1) * 128, tokS + c0 * 64:tokS + (c0 + ncc) * 64], yT[:, 0:ncc * 64], q="pool")
    k.stage_end()


def host_inputs(cfg, inp, core):
    Dp, TS, LP, NPQ, PAST = cfg.depth, cfg.ts, cfg.lp, cfg.npq, cfg.past
    f = lambda a: np.ascontiguousarray(a, dtype=np.float32)
    b = core % inp["x_sample"].shape[0]
    ps = [core * NPQ + i for i in range(NPQ)]
    m = {}
    xT = np.concatenate([inp["x_sample"][b].T] + [inp["x_prompt"][p].T for p in ps], axis=1)
    m["xT"] = f(xT)
    m["cdk"] = f(inp["cache_diff_k"][b].reshape(Dp, PAST, 256).transpose(0, 2, 1))
    m["cdv"] = f(inp["cache_diff_v"][b].reshape(Dp, PAST, 256))
    m["sdl"] = f(inp["state_delta"][b].reshape(Dp, 2, 2, 128, 64))
    m["cckv"] = f(inp["cache_mla_ckv"][b].transpose(0, 2, 1))
    m["ckr"] = f(inp["cache_mla_krope"][b].transpose(0, 2, 1))
    m["sss"] = f(inp["state_ssm"][b].reshape(Dp, 2, 2, 2, 64, 128).transpose(0, 1, 2, 5, 3, 4).reshape(Dp, 2, 2, 128, 128))
    m["cvec"] = f(np.concatenate([inp["c"][b].reshape(8, 128).T, inp["c_ctx"].reshape(8, 128).T], axis=1))
    pv = np.zeros((128, Dp, PV_L), np.float32)
    pr = np.zeros((Dp, PR_L), np.float32)
    for l in range(Dp):
        pv[:, l, PV_BMOD:PV_BMOD + 48] = inp["b_mod"][l].reshape(48, 128).T
        pv[:, l, PV_LN1G:PV_LN1G + 8] = inp["ln1_g"][l].reshape(8, 128).T
        pv[:, l, PV_LN1B:PV_LN1B + 8] = inp["ln1_b"][l].reshape(8, 128).T
        pv[:, l, PV_LN2G:PV_LN2G + 8] = inp["ln2_g"][l].reshape(8, 128).T
        pv[:, l, PV_LN2B:PV_LN2B + 8] = inp["ln2_b"][l].reshape(8, 128).T
        pv[:, l, PV_DIFFN] = np.tile(inp["diff_norm"][l], 2)
        pv[:, l, PV_DNCONV:PV_DNCONV + 18] = inp["dn_conv"][l].reshape(3, 6, 128).transpose(2, 1, 0).reshape(128, 18)
        pv[:, l, PV_DNN] = np.tile(inp["dn_norm"][l], 2)
        pv[:, l, PV_QN:PV_QN + 2] = inp["mla_q_norm"][l].reshape(2, 128).T
        pv[:, l, PV_KVN] = inp["mla_kv_norm"][l]
        pv[:, l, PV_SCW:PV_SCW + 18] = inp["ssm_conv_w"][l].reshape(3, 6, 128).transpose(2, 1, 0).reshape(128, 18)
        pv[:, l, PV_SCB:PV_SCB + 6] = inp["ssm_conv_b"][l].reshape(6, 128).T
        pr[l, PR_LAM:PR_LAM + 128] = inp["diff_lam"][l].reshape(128)
        pr[l, PR_DNA:PR_DNA + 8] = inp["dn_a_log"][l].reshape(8)
        pr[l, PR_DNB:PR_DNB + 8] = inp["dn_dt_bias"][l].reshape(8)
        pr[l, PR_SA:PR_SA + 8] = inp["ssm_a_log"][l].reshape(8)
        pr[l, PR_SB:PR_SB + 8] = inp["ssm_dt_bias"][l].reshape(8)
        pr[l, PR_SD:PR_SD + 4] = inp["ssm_d"][l]
        pr[l, PR_SN:PR_SN + 256] = inp["ssm_norm"][l]
    m["pvec"] = f(pv.reshape(128, Dp * PV_L))
    m["prow"] = f(pr.reshape(1, Dp * PR_L))
    return m


_SHARED = {}


def shared_inputs(cfg, inp):
    f = lambda a: np.ascontiguousarray(a, dtype=np.float32)
    m = {}
    cc = make_consts()
    m["cst"] = np.ascontiguousarray(cc[:, :NCST_BASE * 128])
    m["cst2"] = np.ascontiguousarray(cc[:, NCST_BASE * 128:])
    cos, sin = rope_tables(cfg.ts)
    rc = np.ones((128, cfg.ts), np.float32)
    rs = np.zeros((128, cfg.ts), np.float32)
    for r in range(128):
        rc[r] = cos[:, r % 32]
        rs[r] = sin[:, r % 32]
    m["ropec"] = rc
    m["ropes"] = rs
    m["w_mod"] = f(inp["w_mod"])
    m["w_in"] = f(np.stack([perm_w_in(np.asarray(inp["w_in"][l])) for l in range(cfg.depth)]))
    m["w_uq"] = f(inp["mla_w_uq"])
    m["w_uk"] = f(inp["mla_w_uk"])
    m["w_uv"] = f(inp["mla_w_uv"])
    m["w_out"] = f(inp["w_out"])
    m["w_ff1"] = f(inp["w_ff1"])
    m["w_ff2"] = f(inp["w_ff2"])
    return m


def run(cfg, inp, ncores=8, debug=False):
    inp = {k_: np.asarray(v) for k_, v in inp.items()}
    nc = build(cfg, debug=debug)
    sh = shared_inputs(cfg, inp)
    in_maps = []
    for c in range(ncores):
        m = dict(sh)
        m.update(host_inputs(cfg, inp, c))
        in_maps.append(m)
    res = run_bass_kernel_spmd(nc, in_maps, core_ids=list(range(ncores)))
    return res.results


def assemble(cfg, inp, results):
    Dp, TS, LP, NPQ = cfg.depth, cfg.ts, cfg.lp, cfg.npq
    nb = inp["x_sample"].shape[0]
    B = inp["x_prompt"].shape[0]
    y_s = np.stack([results[b]["yT"][:, :TS].T for b in range(nb)])
    y_p = np.zeros((B, LP, D), np.float32)
    ndk = np.zeros((B, Dp, LP, 4, 2, 32), np.float32)
    ndv = np.zeros((B, Dp, LP, 4, 64), np.float32)
    nsd = np.zeros((B, Dp, 2, 4, 64, 64), np.float32)
    nck = np.zeros((B, Dp, LP, 128), np.float32)
    nkr = np.zeros((B, Dp, LP, 32), np.float32)
    nss = np.zeros((B, Dp, 2, 4, 64, 128), np.float32)
    for c in range(len(results)):
        r = results[c]
        for i in range(NPQ):
            p = c * NPQ + i
            if p >= B:
                continue
            y_p[p] = r["yT"][:, TS + i * LP:TS + (i + 1) * LP].T
            ndk[p] = r["o_dk"][i].transpose(0, 2, 1).reshape(Dp, LP, 4, 2, 32)
            ndv[p] = r["o_dv"][i].reshape(Dp, LP, 4, 64)
            nsd[p] = r["o_sd"][i].reshape(Dp, 2, 4, 64, 64)
            nck[p] = r["o_ckv"][i].transpose(0, 2, 1)
            nkr[p] = r["o_kr"][i].transpose(0, 2, 1)
            nss[p] = r["o_ss"][i].reshape(Dp, 2, 2, 128, 2, 64).transpose(0, 1, 2, 4, 5, 3).reshape(Dp, 2, 4, 64, 128)
    return (y_p, y_s.astype(np.float32), ndk, ndv, nsd, nck, nkr, nss)


def kernel(**inputs):
    cfg = Cfg(depth=4, ts=4096, lp=256, npq=2, past=256)
    inp = {k_: np.asarray(v) for k_, v in inputs.items()}
    results = run(cfg, inp, ncores=8)
    return assemble(cfg, inp, results)
```
